# Optimizing a Trainium2 kernel written in Bass

```python
import math
import jax, jax.numpy as jnp
from jax import lax
import numpy as np

D_MODEL = 2048
BATCH = 16
SEQ = 256
DEPTH = 2
DEC_BATCH = 4
DEC_SEQ = 1024
PAST_LEN = 512

GRID_W = 64
POOL_WIDTH = 512
POOL_GROUPS = 4
POOL_CH = POOL_WIDTH // POOL_GROUPS
POOL_WINDOWS = (2, 4, 8, 16)
S5_WIDTH = 512
S5_CH = 16
S5_GROUPS = S5_WIDTH // S5_CH
S5_STATE = 64
N_HEADS = 8
N_KV_HEADS = 2
HEAD_DIM = 128
REP = N_HEADS // N_KV_HEADS
ATTN_WIDTH = N_HEADS * HEAD_DIM
KV_WIDTH = N_KV_HEADS * HEAD_DIM
WINDOW = 128
BLOCK = 128
ROPE_BASE = 10000.0
NEG_INF = -1e30
CONV_WIDTH = 512
CONV_K = 31
N_BRANCH = 4
D_FF = 5632
FFN_CONV_K = 3
ALPHA = (2.0 * DEPTH) ** 0.25
BETA = (8.0 * DEPTH) ** -0.25
LN_EPS = 1e-5
MIX_COLS = POOL_WIDTH + S5_WIDTH + ATTN_WIDTH + 2 * KV_WIDTH + 2 * CONV_WIDTH
N_IN = MIX_COLS + N_BRANCH * D_MODEL
IN_SPLITS = (
    POOL_WIDTH,
    POOL_WIDTH + S5_WIDTH,
    POOL_WIDTH + S5_WIDTH + ATTN_WIDTH,
    POOL_WIDTH + S5_WIDTH + ATTN_WIDTH + KV_WIDTH,
    POOL_WIDTH + S5_WIDTH + ATTN_WIDTH + 2 * KV_WIDTH,
    MIX_COLS,
)

kernel_name = "hybrid_flow_prefix_trunk_step"


def layer_norm(x, g=None, b=None):
    xf = x.astype(jnp.float32)
    mu = jnp.mean(xf, axis=-1, keepdims=True)
    var = jnp.mean(jnp.square(xf - mu), axis=-1, keepdims=True)
    y = (xf - mu) * lax.rsqrt(var + LN_EPS)
    if g is not None:
        y = y * g.astype(jnp.float32) + b.astype(jnp.float32)
    return y.astype(x.dtype)


def depthwise_conv(x, w, bias):
    k = w.shape[0]
    y = lax.conv_general_dilated(
        x, w[:, None, :].astype(x.dtype), window_strides=(1,),
        padding=[(k // 2, k // 2)], dimension_numbers=("NWC", "WIO", "NWC"),
        feature_group_count=x.shape[-1])
    return y + bias.astype(x.dtype)


def rope_2d(x):
    n = x.shape[1]
    t = jnp.arange(n)
    row, col = t // GRID_W, t % GRID_W
    half = HEAD_DIM // 2
    quarter = half // 2
    inv = ROPE_BASE ** (-jnp.arange(quarter, dtype=jnp.float32) / quarter)

    def rot(xp, pos):
        ang = pos.astype(jnp.float32)[:, None] * inv[None, :]
        cos = jnp.cos(ang)[None, :, None, :]
        sin = jnp.sin(ang)[None, :, None, :]
        x1, x2 = xp[..., :quarter], xp[..., quarter:]
        return jnp.concatenate([x1 * cos - x2 * sin, x2 * cos + x1 * sin], axis=-1)

    xf = x.astype(jnp.float32)
    return jnp.concatenate([rot(xf[..., :half], row), rot(xf[..., half:], col)], axis=-1).astype(x.dtype)


def pool_mixer(a, w_grp, scale):
    bsz, n, _ = a.shape
    ag = a.reshape(bsz, n, POOL_GROUPS, POOL_CH).astype(jnp.float32)
    cs = jnp.concatenate([jnp.zeros((bsz, 1, POOL_GROUPS, POOL_CH), jnp.float32),
                          jnp.cumsum(ag, axis=1)], axis=1)
    t = jnp.arange(n)
    outs = []
    for g, w in enumerate(POOL_WINDOWS):
        left = w // 2
        right = w - 1 - left
        lo = jnp.clip(t - left, 0, n)
        hi = jnp.clip(t + right + 1, 0, n)
        mean = (cs[:, hi, g] - cs[:, lo, g]) / (hi - lo).astype(jnp.float32)[None, :, None]
        outs.append(mean - ag[:, :, g])
    pooled = jnp.stack(outs, axis=2)
    mixed = jnp.einsum("bngc,gcd->bngd", pooled, w_grp.astype(jnp.float32))
    return (mixed.reshape(bsz, n, POOL_WIDTH) * scale.astype(jnp.float32)).astype(a.dtype)


def _lin_comb(left, right):
    a_l, b_l = left
    a_r, b_r = right
    return a_r * a_l, a_r * b_l + b_r


def s5_mixer(u, lam_re, lam_im, log_dt, b_re, b_im, c_re, c_im, d, w_glu, h0):
    bsz, n, _ = u.shape
    f32 = jnp.float32
    ug = u.reshape(bsz, n, S5_GROUPS, S5_CH).astype(f32)
    uc = ug.astype(jnp.complex64)
    lam = lax.complex(lam_re.astype(f32), lam_im.astype(f32))
    lam_dt = lam * jnp.exp(log_dt.astype(f32))[..., None]
    lam_bar = jnp.exp(lam_dt)
    b_mat = lax.complex(b_re.astype(f32), b_im.astype(f32))
    b_bar = ((lam_bar - 1.0) / lam)[..., None] * b_mat
    c_mat = lax.complex(c_re.astype(f32), c_im.astype(f32))
    pos = (jnp.arange(n) + 1).astype(f32)
    powers = jnp.exp(lam_dt[:, None] * pos[None, :, None, None])
    ys, finals = [], []
    for dirn in range(2):
        bu = jnp.einsum("gpc,bngc->bngp", b_bar[dirn], uc)
        if dirn == 1:
            bu = jnp.flip(bu, axis=1)
        a_el = jnp.broadcast_to(lam_bar[dirn], bu.shape)
        _, hs = lax.associative_scan(_lin_comb, (a_el, bu), axis=1)
        hs = hs + powers[dirn][None] * h0[:, dirn][:, None]
        finals.append(hs[:, -1])
        if dirn == 1:
            hs = jnp.flip(hs, axis=1)
        ys.append(jnp.real(jnp.einsum("gcp,bngp->bngc", c_mat[dirn], hs)))
    y = ys[0] + ys[1] + d.astype(f32).reshape(S5_GROUPS, S5_CH) * ug
    y = jax.nn.gelu(y.reshape(bsz, n, S5_WIDTH))
    y = y * jax.nn.sigmoid(y @ w_glu.astype(f32))
    return y.astype(u.dtype), jnp.stack(finals, axis=1)


def _attend_block(qi, keys, vals, mask, sink):
    s = jnp.einsum("bqkrd,bskd->bkrqs", qi, keys, preferred_element_type=jnp.float32) * (HEAD_DIM ** -0.5)
    s = jnp.where(mask, s, NEG_INF)
    sink_col = jnp.broadcast_to(sink.astype(jnp.float32)[None, :, :, None, None], s.shape[:-1] + (1,))
    p = jax.nn.softmax(jnp.concatenate([s, sink_col], axis=-1), axis=-1)[..., :-1]
    o = jnp.einsum("bkrqs,bskd->bqkrd", p.astype(vals.dtype), vals)
    return o.reshape(o.shape[0], BLOCK, ATTN_WIDTH)


def context_attention(q, k, v, sink):
    b, n = q.shape[:2]
    nblk = n // BLOCK
    qb = q.reshape(b, nblk, BLOCK, N_KV_HEADS, REP, HEAD_DIM).swapaxes(0, 1)
    mask = jnp.ones((BLOCK, k.shape[1]), bool)
    sk = sink.reshape(N_KV_HEADS, REP)
    out = lax.map(lambda qi: _attend_block(qi, k, v, mask, sk), qb)
    return out.swapaxes(0, 1).reshape(b, n, ATTN_WIDTH)


def latent_attention(q, k, v, k_ctx, v_ctx, sink):
    b, n = q.shape[:2]
    nblk = n // BLOCK
    n_ctx = k_ctx.shape[1]
    qb = q.reshape(b, nblk, BLOCK, N_KV_HEADS, REP, HEAD_DIM).swapaxes(0, 1)
    pad = ((0, 0), (BLOCK, BLOCK), (0, 0), (0, 0))
    kp = jnp.pad(k, pad)
    vp = jnp.pad(v, pad)
    sk = sink.reshape(N_KV_HEADS, REP)
    q_off = jnp.arange(BLOCK)
    k_off = jnp.arange(3 * BLOCK)
    ctx_mask = jnp.ones((BLOCK, n_ctx), bool)

    def one_block(args):
        i, qi = args
        start = i * BLOCK
        kw = lax.dynamic_slice_in_dim(kp, start, 3 * BLOCK, axis=1)
        vw = lax.dynamic_slice_in_dim(vp, start, 3 * BLOCK, axis=1)
        qpos = start + q_off
        kpos = start - BLOCK + k_off
        lat_mask = (kpos[None, :] >= 0) & (kpos[None, :] < n) & (jnp.abs(qpos[:, None] - kpos[None, :]) <= WINDOW)
        mask = jnp.concatenate([lat_mask, ctx_mask], axis=1)
        keys = jnp.concatenate([kw, k_ctx.astype(kw.dtype)], axis=1)
        vals = jnp.concatenate([vw, v_ctx.astype(vw.dtype)], axis=1)
        return _attend_block(qi, keys, vals, mask, sk)

    out = lax.map(one_block, (jnp.arange(nblk), qb))
    return out.swapaxes(0, 1).reshape(b, n, ATTN_WIDTH)


def conformer_conv(cv, dw, db, ln_g, ln_b):
    a, gt = jnp.split(cv, 2, axis=-1)
    x = a * jax.nn.sigmoid(gt)
    x = depthwise_conv(x, dw, db)
    return jax.nn.silu(layer_norm(x, ln_g, ln_b))


def conv_ffn(h, w_up, dw, db, w_down):
    u = depthwise_conv(h @ w_up, dw, db)
    gt, val = jnp.split(u, 2, axis=-1)
    return (jax.nn.silu(gt) * val) @ w_down


def trunk_layer(x, mod, p, ctx):
    shift1, scale1, gate1, shift2, scale2, gate2 = jnp.split(mod, 6, axis=-1)
    b, n = x.shape[:2]
    h = layer_norm(x) * (1.0 + scale1) + shift1
    proj = h @ p["w_in"]
    a, s, q, k, v, cv, gates = jnp.split(proj, IN_SPLITS, axis=-1)
    q = q.reshape(b, n, N_HEADS, HEAD_DIM)
    k = k.reshape(b, n, N_KV_HEADS, HEAD_DIM)
    v = v.reshape(b, n, N_KV_HEADS, HEAD_DIM)
    if ctx is None:
        h0 = jnp.zeros((b, 2, S5_GROUPS, S5_STATE), jnp.complex64)
        attn = context_attention(q, k, v, p["attn_sink"])
    else:
        k_ctx, v_ctx, h0 = ctx
        attn = latent_attention(rope_2d(q), rope_2d(k), v, k_ctx, v_ctx, p["attn_sink"])
    y_pool = pool_mixer(a, p["pool_w"], p["pool_scale"])
    y_s5, s5_final = s5_mixer(s, p["s5_lambda_re"], p["s5_lambda_im"], p["s5_log_dt"],
                              p["s5_b_re"], p["s5_b_im"], p["s5_c_re"], p["s5_c_im"],
                              p["s5_d"], p["s5_w_glu"], h0)
    y_conv = conformer_conv(cv, p["conv_dw"], p["conv_db"], p["conv_ln_g"], p["conv_ln_b"])
    g = jax.nn.sigmoid(gates.reshape(b, n, N_BRANCH, D_MODEL))
    merged = (g[:, :, 0] * (y_pool @ p["w_br_pool"]) + g[:, :, 1] * (y_s5 @ p["w_br_s5"])
              + g[:, :, 2] * (attn @ p["w_br_attn"]) + g[:, :, 3] * (y_conv @ p["w_br_conv"]))
    x = layer_norm(ALPHA * x + gate1 * (merged @ p["w_out"]), p["ln1_g"], p["ln1_b"])
    h = layer_norm(x) * (1.0 + scale2) + shift2
    f = conv_ffn(h, p["ffn_w_up"], p["ffn_dw"], p["ffn_db"], p["ffn_w_down"])
    x = layer_norm(ALPHA * x + gate2 * f, p["ln2_g"], p["ln2_b"])
    ctx_out = (k, v, s5_final) if ctx is None else None
    return x, ctx_out


def setup_inputs(seed: int = 0) -> dict:
    key = jax.random.key(seed)
    ks = jax.random.split(key, 48)
    f32 = jnp.float32

    def nrm(k, shape, scale):
        return scale * jax.random.normal(k, shape, f32)

    L = DEPTH
    s5_n = jnp.arange(S5_STATE, dtype=f32)
    return {
        "x_prompt": nrm(ks[0], (BATCH, SEQ, D_MODEL), 1.0),
        "x_sample": nrm(ks[1], (DEC_BATCH, DEC_SEQ, D_MODEL), 1.0),
        "cache_k": nrm(ks[2], (DEC_BATCH, DEPTH, PAST_LEN, N_KV_HEADS, HEAD_DIM), 1.0),
        "cache_v": nrm(ks[3], (DEC_BATCH, DEPTH, PAST_LEN, N_KV_HEADS, HEAD_DIM), 1.0),
        "state_s5": nrm(ks[4], (DEC_BATCH, DEPTH, 2, 2, S5_GROUPS, S5_STATE), 0.5),
        "c": nrm(ks[5], (DEC_BATCH, D_MODEL), 1.0),
        "c_ctx": nrm(ks[6], (D_MODEL,), 1.0),
        "w_ada": nrm(ks[7], (L, D_MODEL, 6 * D_MODEL), 0.5 * D_MODEL ** -0.5),
        "b_ada": nrm(ks[8], (L, 6 * D_MODEL), 0.02),
        "w_in": nrm(ks[9], (L, D_MODEL, N_IN), D_MODEL ** -0.5),
        "pool_w": nrm(ks[10], (L, POOL_GROUPS, POOL_CH, POOL_CH), POOL_CH ** -0.5),
        "pool_scale": 1.0 + nrm(ks[11], (L, POOL_WIDTH), 0.02),
        "s5_lambda_re": -0.5 + nrm(ks[12], (L, 2, S5_GROUPS, S5_STATE), 0.01),
        "s5_lambda_im": jnp.pi * s5_n + nrm(ks[13], (L, 2, S5_GROUPS, S5_STATE), 0.01),
        "s5_log_dt": jax.random.uniform(ks[14], (L, 2, S5_GROUPS), f32, math.log(1e-3), math.log(1e-1)),
        "s5_b_re": nrm(ks[15], (L, 2, S5_GROUPS, S5_STATE, S5_CH), (2 * S5_CH) ** -0.5),
        "s5_b_im": nrm(ks[16], (L, 2, S5_GROUPS, S5_STATE, S5_CH), (2 * S5_CH) ** -0.5),
        "s5_c_re": nrm(ks[17], (L, 2, S5_GROUPS, S5_CH, S5_STATE), (2 * S5_STATE) ** -0.5),
        "s5_c_im": nrm(ks[18], (L, 2, S5_GROUPS, S5_CH, S5_STATE), (2 * S5_STATE) ** -0.5),
        "s5_d": nrm(ks[19], (L, S5_WIDTH), 0.5),
        "s5_w_glu": nrm(ks[20], (L, S5_WIDTH, S5_WIDTH), S5_WIDTH ** -0.5),
        "attn_sink": nrm(ks[21], (L, N_HEADS), 0.5),
        "conv_dw": nrm(ks[22], (L, CONV_K, CONV_WIDTH), CONV_K ** -0.5),
        "conv_db": nrm(ks[23], (L, CONV_WIDTH), 0.02),
        "conv_ln_g": 1.0 + nrm(ks[24], (L, CONV_WIDTH), 0.02),
        "conv_ln_b": nrm(ks[25], (L, CONV_WIDTH), 0.02),
        "w_br_pool": nrm(ks[26], (L, POOL_WIDTH, D_MODEL), BETA * POOL_WIDTH ** -0.5),
        "w_br_s5": nrm(ks[27], (L, S5_WIDTH, D_MODEL), BETA * S5_WIDTH ** -0.5),
        "w_br_attn": nrm(ks[28], (L, ATTN_WIDTH, D_MODEL), BETA * ATTN_WIDTH ** -0.5),
        "w_br_conv": nrm(ks[29], (L, CONV_WIDTH, D_MODEL), BETA * CONV_WIDTH ** -0.5),
        "w_out": nrm(ks[30], (L, D_MODEL, D_MODEL), BETA * D_MODEL ** -0.5),
        "ln1_g": 1.0 + nrm(ks[31], (L, D_MODEL), 0.02),
        "ln1_b": nrm(ks[32], (L, D_MODEL), 0.02),
        "ffn_w_up": nrm(ks[33], (L, D_MODEL, 2 * D_FF), D_MODEL ** -0.5),
        "ffn_dw": nrm(ks[34], (L, FFN_CONV_K, 2 * D_FF), FFN_CONV_K ** -0.5),
        "ffn_db": nrm(ks[35], (L, 2 * D_FF), 0.02),
        "ffn_w_down": nrm(ks[36], (L, D_FF, D_MODEL), BETA * D_FF ** -0.5),
        "ln2_g": 1.0 + nrm(ks[37], (L, D_MODEL), 0.02),
        "ln2_b": nrm(ks[38], (L, D_MODEL), 0.02),
    }


def reference(x_prompt, x_sample, cache_k, cache_v, state_s5, c, c_ctx, w_ada, b_ada, w_in,
              pool_w, pool_scale, s5_lambda_re, s5_lambda_im, s5_log_dt, s5_b_re, s5_b_im,
              s5_c_re, s5_c_im, s5_d, s5_w_glu, attn_sink, conv_dw, conv_db, conv_ln_g, conv_ln_b,
              w_br_pool, w_br_s5, w_br_attn, w_br_conv, w_out, ln1_g, ln1_b, ffn_w_up, ffn_dw,
              ffn_db, ffn_w_down, ln2_g, ln2_b):
    y_prompt = x_prompt
    y_sample = x_sample
    ks_out, vs_out, ss_out = [], [], []
    for l in range(DEPTH):
        p = {
            "w_in": w_in[l], "pool_w": pool_w[l], "pool_scale": pool_scale[l],
            "s5_lambda_re": s5_lambda_re[l], "s5_lambda_im": s5_lambda_im[l], "s5_log_dt": s5_log_dt[l],
            "s5_b_re": s5_b_re[l], "s5_b_im": s5_b_im[l], "s5_c_re": s5_c_re[l], "s5_c_im": s5_c_im[l],
            "s5_d": s5_d[l], "s5_w_glu": s5_w_glu[l], "attn_sink": attn_sink[l],
            "conv_dw": conv_dw[l], "conv_db": conv_db[l], "conv_ln_g": conv_ln_g[l], "conv_ln_b": conv_ln_b[l],
            "w_br_pool": w_br_pool[l], "w_br_s5": w_br_s5[l], "w_br_attn": w_br_attn[l], "w_br_conv": w_br_conv[l],
            "w_out": w_out[l], "ln1_g": ln1_g[l], "ln1_b": ln1_b[l],
            "ffn_w_up": ffn_w_up[l], "ffn_dw": ffn_dw[l], "ffn_db": ffn_db[l], "ffn_w_down": ffn_w_down[l],
            "ln2_g": ln2_g[l], "ln2_b": ln2_b[l],
        }
        mod_ctx = jax.nn.silu(c_ctx) @ w_ada[l] + b_ada[l]
        mod_lat = (jax.nn.silu(c) @ w_ada[l] + b_ada[l])[:, None, :]
        y_prompt, (k_new, v_new, s5_new) = trunk_layer(y_prompt, mod_ctx, p, None)
        ks_out.append(k_new)
        vs_out.append(v_new)
        ss_out.append(jnp.stack([jnp.real(s5_new), jnp.imag(s5_new)], axis=2))
        st = state_s5[:, l]
        h0 = lax.complex(st[:, :, 0].astype(jnp.float32), st[:, :, 1].astype(jnp.float32))
        y_sample, _ = trunk_layer(y_sample, mod_lat, p, (cache_k[:, l], cache_v[:, l], h0))
    new_cache_k = jnp.stack(ks_out, axis=1)
    new_cache_v = jnp.stack(vs_out, axis=1)
    new_state_s5 = jnp.stack(ss_out, axis=1)
    return (y_prompt, y_sample, new_cache_k, new_cache_v, new_state_s5)
```

```python
import math
from contextlib import ExitStack
import numpy as np
import concourse.bass as bass
import concourse.mybir as mybir
from concourse.bass_utils import run_bass_kernel_spmd

F32 = mybir.dt.float32
BF16 = mybir.dt.bfloat16
I32 = mybir.dt.int32
AF = mybir.ActivationFunctionType
ALU = mybir.AluOpType

D = 2048
KC = 16
NT = 1024
NL = 2
DFF = 5632
NFF = 88
ALPHA = (2.0 * NL) ** 0.25
EPS = 1e-5
MIXC = 3584
N_IN = 11776
ARENA_F32 = 40960


class Buf:
    __slots__ = ("name", "w", "r", "excl")

    def __init__(self, name):
        self.name = name
        self.w = {}
        self.r = {}
        self.excl = False


class Prog:
    COMPUTE = ("pe", "act", "dve", "pool")
    NSLOT = 12

    def __init__(self, nc):
        self.nc = nc
        self.q = {e: [] for e in ("pe", "act", "dve", "pool", "sp")}
        self.cnt = {e: 0 for e in self.COMPUTE}
        self.known = {e: {} for e in self.q}
        self.slot_cnt = {}
        self.slot_rr = {"sp": 0, "pool": 0}
        self.nbuf = 0

    def buf(self, name=None):
        self.nbuf += 1
        return Buf(name or f"b{self.nbuf}")

    def bufs(self, n, name="b"):
        return [self.buf(f"{name}{i}") for i in range(n)]

    def _deps(self, reads, writes):
        deps = {}

        def add(d):
            for k, v in d.items():
                if deps.get(k, 0) < v:
                    deps[k] = v
        for b in reads:
            add(b.w)
            if b.excl:
                add(b.r)
        for b in writes:
            add(b.w)
            add(b.r)
        return deps

    def _filter(self, eng, deps):
        waits = []
        kn = self.known[eng]
        for k, v in deps.items():
            if k == eng and eng == "pe":
                continue
            if kn.get(k, 0) >= v:
                continue
            kn[k] = v
            waits.append((k, v))
        return waits

    def _commit(self, reads, writes, key, val):
        for b in reads:
            if b.r.get(key, 0) < val:
                b.r[key] = val
        for b in writes:
            b.w = {key: val}
            b.r = {}

    def op(self, eng, emit, reads=(), writes=()):
        deps = self._deps(reads, writes)
        waits = self._filter(eng, deps)
        self.cnt[eng] += 1
        val = self.cnt[eng]
        self.q[eng].append((waits, emit, eng, 1))
        self._commit(reads, writes, eng, val)

    def dma(self, queue, out, in_, reads=(), writes=(), **kw):
        deps = self._deps(reads, writes)
        slot = self.slot_rr[queue]
        self.slot_rr[queue] = (slot + 1) % self.NSLOT
        key = f"d_{queue}_{slot}"
        n = self.slot_cnt.get(key, 0)
        if n > 0 and deps.get(key, 0) < 16 * n:
            deps[key] = 16 * n
        self.slot_cnt[key] = n + 1
        val = 16 * (n + 1)
        waits = self._filter(queue, deps)

        def emit(e, out=out, in_=in_, kw=kw):
            return e.dma_start(out=out, in_=in_, **kw)
        self.q[queue].append((waits, emit, key, 16))
        self._commit(reads, writes, key, val)

    def mark(self, queue):
        m = [None, None, None, 0]
        self.q[queue].append(m)
        return m

    def dma_hoisted(self, queue, key, out, in_, after_marker, reads=(), writes=(), **kw):
        deps = self._deps(reads, writes)
        n = self.slot_cnt.get(key, 0)
        if n > 0 and deps.get(key, 0) < 16 * n:
            deps[key] = 16 * n
        self.slot_cnt[key] = n + 1
        val = 16 * (n + 1)
        kn = self.known[queue]
        waits = []
        for k, v in deps.items():
            waits.append((k, v))
            if kn.get(k, 0) < v:
                kn[k] = v

        def emit(e, out=out, in_=in_, kw=kw):
            return e.dma_start(out=out, in_=in_, **kw)
        entry = (waits, emit, key, 16)
        if after_marker is None:
            self.q[queue].append(entry)
        else:
            qq = self.q[queue]
            idx = next(i for i, x in enumerate(qq) if x is after_marker)
            qq.insert(idx + 1, entry)
        self._commit(reads, writes, key, val)

    def finish(self):
        deps = {}
        for key, n in self.slot_cnt.items():
            deps[key] = 16 * n
        for e in self.COMPUTE:
            if self.cnt[e]:
                deps[e] = self.cnt[e]
        waits = self._filter("sp", deps)
        self.q["sp"].append((waits, None, None, 0))

    def emit(self):
        nc = self.nc
        keys = list(self.COMPUTE) + sorted(self.slot_cnt.keys())
        with ExitStack() as st:
            st.enter_context(nc.allow_non_contiguous_dma(reason="small per-partition parameter columns"))
            sems = {k: st.enter_context(nc.semaphore(k)) for k in keys}
            block = st.enter_context(nc.Block())

            def body_for(name):
                items = self.q[name]

                def body(e):
                    for waits, emit, key, inc in items:
                        if waits is None:
                            continue
                        for k, v in waits:
                            e.wait_ge(sems[k], v)
                        if emit is None:
                            continue
                        inst = emit(e)
                        inst.then_inc(sems[key], inc)
                return body

            block.tensor(body_for("pe"))
            block.scalar(body_for("act"))
            block.vector(body_for("dve"))
            block.gpsimd(body_for("pool"))
            block.sync(body_for("sp"))


class Arena:
    def __init__(self, P, tensor, nbytes):
        self.P = P
        self.t = tensor
        self.nbytes = nbytes
        self.live = []

    def alloc(self, name, off, shape, dt=F32):
        esz = 2 if dt == BF16 else 4
        n = 1
        for s in shape[1:]:
            n *= s
        nbytes = n * esz
        assert off % 4 == 0 and nbytes % 4 == 0, (name, off, nbytes)
        assert off + nbytes <= self.nbytes, (name, off, nbytes)
        lo, hi = off, off + nbytes
        b = self.P.buf(name)
        keep = []
        for (l2, h2, b2) in self.live:
            if l2 < hi and lo < h2:
                for src in (b2.w, b2.r):
                    for k, v in src.items():
                        if b.r.get(k, 0) < v:
                            b.r[k] = v
                if l2 < lo:
                    keep.append((l2, lo, b2))
                if hi < h2:
                    keep.append((hi, h2, b2))
            else:
                keep.append((l2, h2, b2))
        keep.append((lo, hi, b))
        self.live = keep
        ap = self.t[0:shape[0], off // 4:(off + nbytes) // 4]
        if dt != F32:
            ap = ap.bitcast(dt)
        if len(shape) == 3:
            ap = ap.rearrange("p (a b) -> p a b", a=shape[1])
        elif len(shape) == 4:
            ap = ap.rearrange("p (a b c) -> p a b c", a=shape[1], b=shape[2])
        return ap, b

    def alias(self, name, off, nbytes):
        b = self.P.buf(name)
        self.live.append((off, off + nbytes, b))
        return b


KB_ = 1024
import os


class StopBuild(Exception):
    pass


def ck(n):
    if int(os.environ.get('KSTOP', '0')) == n:
        raise StopBuild()


class KBld:
    def __init__(self, nc, P, st, dbg=False):
        self.nc, self.P, self.st, self.dbg = nc, P, st, dbg
        self.din = {}
        self.ar_t = st.enter_context(nc.sbuf_tensor("arena", [128, ARENA_F32], F32))
        self.hb_t = st.enter_context(nc.sbuf_tensor("hbuf", [128, 8192], F32))
        self.cs_t = st.enter_context(nc.sbuf_tensor("cstsb", [128, 2048], F32))
        self.ps_t = st.enter_context(nc.psum_tensor("ps", [128, 4096], F32))
        self.A = Arena(P, self.ar_t, ARENA_F32 * 4)
        self.H = Arena(P, self.hb_t, 32768)
        self.C = Arena(P, self.cs_t, 8192)
        self.pb = P.bufs(8, "psb")
        for b_ in self.pb:
            b_.excl = True
        self.wi = 0
        self.wmarks = []
        self.ev = 0

    def inp(self, name, shape):
        ap = self.nc.dram_tensor(name, list(shape), F32, kind="ExternalInput").ap()
        self.din[name] = ap
        return ap

    def outp(self, name, shape):
        return self.nc.dram_tensor(name, list(shape), F32, kind="ExternalOutput").ap()

    def bank(self, i, n=512, off=0):
        return self.ps_t[:, i * 512 + off:i * 512 + off + n]

    def bank_bf(self, i):
        return self.ps_t[:, i * 512:(i + 1) * 512].bitcast(BF16)

    def act(self, out, in_, func, reads, writes, bias=None, scale=None):
        kw = {}
        if bias is not None:
            kw["bias"] = bias
        if scale is not None:
            kw["scale"] = scale
        self.P.op("act", lambda e: e.activation(out=out, in_=in_, func=func, **kw), reads, writes)

    def tt(self, eng, out, a, b, op, reads, writes):
        self.P.op(eng, lambda e: e.tensor_tensor(out=out, in0=a, in1=b, op=op), reads, writes)

    def ts(self, eng, out, a, s1, s2, op0, op1, reads, writes):
        if op1 is None:
            self.P.op(eng, lambda e: e.tensor_scalar(out=out, in0=a, scalar1=s1, scalar2=None, op0=op0), reads, writes)
        else:
            self.P.op(eng, lambda e: e.tensor_scalar(out=out, in0=a, scalar1=s1, scalar2=s2, op0=op0, op1=op1), reads, writes)

    def stt(self, out, in0, scalar, in1, op0, op1, reads, writes):
        self.P.op("dve", lambda e: e.scalar_tensor_tensor(out=out, in0=in0, scalar=scalar, in1=in1, op0=op0, op1=op1), reads, writes)

    def cp(self, eng, out, in_, reads, writes):
        if eng == "act":
            self.P.op("act", lambda e: e.activation(out=out, in_=in_, func=AF.Identity), reads, writes)
        else:
            self.P.op(eng, lambda e: e.tensor_copy(out=out, in_=in_), reads, writes)

    def evac(self, out, in_, reads, writes):
        self.ev += 1
        self.cp("act" if self.ev % 2 else "dve", out, in_, reads, writes)

    def mm(self, out, lhsT, rhs, start, stop, reads, writes):
        self.P.op("pe", lambda e: e.matmul(out, lhsT=lhsT, rhs=rhs, start=start, stop=stop), reads, writes)

    def tr(self, out, in_, ident, reads, writes):
        self.P.op("pe", lambda e: e.transpose(out, in_, ident), reads, writes)

    def memset(self, eng, ap, val, writes):
        self.P.op(eng, lambda e: e.memset(ap, val), (), writes)

    def wload(self, src, nk, ncols):
        assert nk * ncols <= 4096
        i = self.wi % 3
        self.wi += 1
        ap, b = self.A.alloc(f"w{self.wi}", (136 + 8 * i) * KB_, [128, nk, ncols], BF16)
        mk = self.P.mark("pool")
        self.wmarks.append(mk)
        tgt = self.wmarks[-3] if len(self.wmarks) >= 3 else None
        self.P.dma_hoisted("pool", f"d_w_{i}", ap, src.rearrange("(kc p) n -> p kc n", p=128), tgt, writes=[b])
        return ap, b

    def ln_stats_chunk(self, c, nch, xa, xb_, so, arena=None):
        K = self
        AR = arena if arena is not None else K.A
        ones = self.ones_b
        s = c % 2
        cb, cbb = AR.alloc(f"lnc{c}", so + s * 2048, [128, 1024], BF16)
        sq, sqb = AR.alloc(f"lnq{c}", so + 4096 + s * 2048, [128, 1024], BF16)
        K.cp("dve", cb, xa, [xb_], [cbb])
        K.act(sq, xa, AF.Square, [xb_], [sqb])
        for tt in range(2):
            K.mm(K.bank(4 + tt), ones, cb[:, tt * 512:(tt + 1) * 512], c == 0, c == nch - 1, [cbb, K.cb_ones], [K.pb[4 + tt]])
            K.mm(K.bank(6 + tt), ones, sq[:, tt * 512:(tt + 1) * 512], c == 0, c == nch - 1, [sqb, K.cb_ones], [K.pb[6 + tt]])

    def ln_stats_finish(self, Dn, so):
        K = self
        R, Rb = K.A.alloc("lnR", so + 8192, [128, 1024], F32)
        MR, MRb = K.A.alloc("lnMR", so + 12288, [128, 1024], F32)
        T, Tb = K.A.alloc("lnT", so + 16384, [128, 1024], F32)
        inv = 1.0 / Dn
        K.ts("dve", MR, K.ps_t[:, 4 * 512:6 * 512], inv, None, ALU.mult, None, [K.pb[4], K.pb[5]], [MRb])
        K.act(R, K.ps_t[:, 6 * 512:8 * 512], AF.Identity, [K.pb[6], K.pb[7]], [Rb], scale=inv)
        K.tt("dve", T, MR, MR, ALU.mult, [MRb], [Tb])
        K.tt("dve", R, R, T, ALU.subtract, [Rb, Tb], [Rb])
        K.act(R, R, AF.Ln, [Rb, K.cb_eps], [Rb], bias=K.epsc[:, 0:1])
        K.act(R, R, AF.Exp, [Rb], [Rb], scale=-0.5)
        K.stt(MR, MR, -1.0, R, ALU.mult, ALU.mult, [MRb, Rb], [MRb])
        return R, Rb, MR, MRb

    def ln_stats(self, xs, Dn, so):
        for c, (xa, xb_) in enumerate(xs):
            self.ln_stats_chunk(c, len(xs), xa, xb_, so)
        return self.ln_stats_finish(Dn, so)

    def ln_apply(self, xs, outs, R, Rb, MR, MRb, scol, bcol, pb_, func=AF.Identity, after=None):
        K = self
        for c, ((xa, xb_), (oa, ob_)) in enumerate(zip(xs, outs)):
            K.tt("dve", xa, xa, R, ALU.mult, [xb_, Rb], [xb_])
            K.tt("pool" if c % 2 else "dve", xa, xa, MR, ALU.add, [xb_, MRb], [xb_])
            rd = [xb_] + list(pb_)
            K.act(oa, xa, func, rd, [ob_] if ob_ is not xb_ else [xb_], bias=bcol[:, c:c + 1], scale=scol[:, c:c + 1])
            if after is not None:
                after(c, oa, ob_)

    def proj_fm(self, wt, wb, nk, mo, rhs, rhs_b, pair, kofs=0, ktot=None, mw=128):
        K = self
        ktot = ktot or nk
        b0 = pair * 2
        for tt in range(2):
            for kc in range(nk):
                K.mm(K.ps_t[0:mw, (b0 + tt) * 512:(b0 + tt + 1) * 512], wt[:, kc, mo:mo + mw], rhs[:, kofs + kc, tt * 512:(tt + 1) * 512],
                     kofs + kc == 0, kofs + kc == ktot - 1, [wb, rhs_b], [K.pb[b0 + tt]])
        return K.ps_t[0:mw, b0 * 512:(b0 + 2) * 512], [K.pb[b0], K.pb[b0 + 1]]


def build(dbg=False):
    nc = bass.Bass("TRN2", target_bir_lowering=False)
    P = Prog(nc)
    with ExitStack() as st:
        K = KBld(nc, P, st, dbg)
        A, H, C = K.A, K.H, K.C
        x_d = K.inp("x", [NT, D])
        cst_d = K.inp("cst", [128, 528])
        cvec_d = K.inp("cvec", [128, 16])
        flag_d = K.inp("flag", [128, 1])
        w_ada = K.inp("w_ada", [NL, D, 6 * D])
        bada_d = K.inp("b_ada_fm", [NL, 128, 96])
        w_in = K.inp("w_in", [NL, D, N_IN])
        pool_w = K.inp("pool_w", [NL, 4, 128, 128])
        pool_sc_d = K.inp("pool_scale_fm", [NL, 128, 4])
        pool_rc_d = K.inp("pool_rc", [128, 4, NT])
        lam_re_d = K.inp("lam_re_dp", [NL, 128, 32])
        lam_im_d = K.inp("lam_im_dp", [NL, 128, 32])
        logdt_d = K.inp("logdt_dp", [NL, 128, 32])
        bre_d = K.inp("b_re_dp", [NL, 128, 32, 16])
        bim_d = K.inp("b_im_dp", [NL, 128, 32, 16])
        cre_d = K.inp("c_re_dp", [NL, 128, 32, 16])
        cim_d = K.inp("c_im_dp", [NL, 128, 32, 16])
        dcol_d = K.inp("s5_dcol", [NL, 128, 32])
        w_glu = K.inp("s5_w_glu", [NL, 512, 512])
        h0_d = K.inp("h0_dp", [NL, 128, 2, 32])
        sink_d = K.inp("attn_sink_bc", [NL, 128, 8])
        cdw_d = K.inp("conv_dw_fm", [NL, 128, 4, 31])
        cdb_d = K.inp("conv_db_fm", [NL, 128, 4])
        clg_d = K.inp("conv_ln_g_fm", [NL, 128, 4])
        clb_d = K.inp("conv_ln_b_fm", [NL, 128, 4])
        w_br_pool = K.inp("w_br_pool", [NL, 512, D])
        w_br_s5 = K.inp("w_br_s5", [NL, 512, D])
        w_br_attn = K.inp("w_br_attn", [NL, 1024, D])
        w_br_conv = K.inp("w_br_conv", [NL, 512, D])
        w_out = K.inp("w_out", [NL, D, D])
        ln1g_d = K.inp("ln1_g_fm", [NL, 128, 16])
        ln1b_d = K.inp("ln1_b_fm", [NL, 128, 16])
        ln2g_d = K.inp("ln2_g_fm", [NL, 128, 16])
        ln2b_d = K.inp("ln2_b_fm", [NL, 128, 16])
        w_up = K.inp("ffn_w_up", [NL, D, 2 * DFF])
        fdw_d = K.inp("ffn_dw_fm", [NL, 128, 3, NFF])
        fdb_d = K.inp("ffn_db_fm", [NL, 128, NFF])
        w_down = K.inp("ffn_w_down", [NL, DFF, D])
        ck_d = K.inp("cache_k_c", [NL, 512, 256])
        cv_d = K.inp("cache_v_c", [NL, 512, 256])
        masks_d = K.inp("masks", [128, 8 * 7 * 128])
        y_d = K.outp("y", [NT, D])
        kout_d = K.outp("kout", [NL, NT, 256])
        vout_d = K.outp("vout", [NL, NT, 256])
        sout_d = K.outp("sout", [4, NL, 2, 2, 32, 64])
        xs_d = nc.dram_tensor("xstash", [128, KC * NT], F32, kind="Internal").ap()
        xsb = P.buf("xstash")

        ident_f, b_idf = C.alloc("ident_f", 0, [128, 128], F32)
        ident_b, b_idb = C.alloc("ident_b", 512, [128, 128], BF16)
        ones_b, b_ones = C.alloc("ones_b", 768, [128, 128], BF16)
        rotP_b, b_rot = C.alloc("rotP_b", 1024, [128, 128], BF16)
        tm0, b_tm0 = C.alloc("tm0", 1280, [128, 128], F32)
        tm1, b_tm1 = C.alloc("tm1", 1792, [128, 128], F32)
        flag, b_flag = C.alloc("flag", 2304, [128, 1], F32)
        sc_b, b_sc = C.alloc("sc_b", 2320, [128, 16], BF16)
        cvec, b_cvec = C.alloc("cvec", 2352, [128, 16], F32)
        mod, b_mod = C.alloc("mod", 2432, [128, 96], F32)
        bada, b_bada = C.alloc("bada", 2816, [128, 96], F32)
        s1p, b_s1p = C.alloc("s1p", 3200, [128, 16], F32)
        s2p, b_s2p = C.alloc("s2p", 3264, [128, 16], F32)
        ln1g, b_ln1g = C.alloc("ln1g", 3328, [128, 16], F32)
        ln1b, b_ln1b = C.alloc("ln1b", 3392, [128, 16], F32)
        ln2g, b_ln2g = C.alloc("ln2g", 3456, [128, 16], F32)
        ln2b, b_ln2b = C.alloc("ln2b", 3520, [128, 16], F32)
        fdw, b_fdw = C.alloc("fdw", 3584, [128, 3, NFF], F32)
        fdb, b_fdb = C.alloc("fdb", 4640, [128, NFF], F32)
        cdw, b_cdw = C.alloc("cdw", 4992, [128, 4, 31], F32)
        cdb, b_cdb = C.alloc("cdb", 5488, [128, 4], F32)
        clg, b_clg = C.alloc("clg", 5504, [128, 4], F32)
        clb, b_clb = C.alloc("clb", 5520, [128, 4], F32)
        psc, b_psc = C.alloc("psc", 5536, [128, 4], F32)
        sinkexp, b_sink = C.alloc("sinkexp", 5552, [128, 8], F32)
        kio, b_kio = C.alloc("kio", 5600, [128, 8], F32)
        sgn, b_sgn = C.alloc("sgn", 5632, [128, 1], F32)
        gate1a, b_g1a = C.alloc("gate1a", 5648, [128, 16], F32)
        zero1, b_zero1 = C.alloc("zero1", 5712, [128, 1], F32)
        modraw, b_modraw = C.alloc("modraw", 5760, [128, 96], F32)
        K.ones_b, K.cb_ones = ones_b, b_ones
        epsc, b_epsc = C.alloc("epsc", 5728, [128, 1], F32)
        K.memset("dve", epsc, EPS, [b_epsc])
        K.epsc, K.cb_eps = epsc, b_epsc

        P.dma("sp", ident_f, cst_d[:, 0:128], writes=[b_idf])
        P.dma("pool", ident_b, cst_d[:, 0:128], writes=[b_idb])
        P.dma("pool", rotP_b, cst_d[:, 128:256], writes=[b_rot])
        P.dma("sp", tm0, cst_d[:, 256:384], writes=[b_tm0])
        P.dma("sp", tm1, cst_d[:, 384:512], writes=[b_tm1])
        P.dma("sp", kio, cst_d[:, 512:520], writes=[b_kio])
        P.dma("sp", sgn, cst_d[:, 520:521], writes=[b_sgn])
        P.dma("sp", flag, flag_d, writes=[b_flag])
        P.dma("sp", cvec, cvec_d, writes=[b_cvec])
        K.memset("dve", ones_b, 1.0, [b_ones])
        K.memset("dve", zero1, 0.0, [b_zero1])
        K.act(sc_b, cvec, AF.Silu, [b_cvec], [b_sc])

        xT = K.ar_t[:, 0:KC * NT].rearrange("p (a b) -> p a b", a=KC)
        xch_b = [A.alloc(f"xch{c}", c * 4096, [128, NT], F32)[1] for c in range(KC)]
        for tt in range(8):
            stg, b_stg = A.alloc(f"stg{tt}", (64 + 8 * (tt % 2)) * KB_, [128, D], F32)
            P.dma("sp", stg, x_d[tt * 128:(tt + 1) * 128, :], writes=[b_stg])
            for cg in range(4):
                bk = 4 + (tt * 4 + cg) % 4
                for ci in range(4):
                    c = cg * 4 + ci
                    K.tr(K.bank(bk, 128, ci * 128), stg[:, c * 128:(c + 1) * 128], ident_f, [b_stg, b_idf], [K.pb[bk]])
                K.evac(xT[:, cg * 4:(cg + 1) * 4, tt * 128:(tt + 1) * 128],
                       K.bank(bk).rearrange("p (a b) -> p a b", a=4), [K.pb[bk]], [xch_b[cg * 4 + i] for i in range(4)])

        def xchunks():
            return [(xT[:, c, :], xch_b[c]) for c in range(KC)]

        def stash_x():
            for c in range(KC):
                P.dma("sp", xs_d[:, c * NT:(c + 1) * NT], xT[:, c, :], reads=[xch_b[c]], writes=[xsb])

        try:
            ck(1)
            for l in range(NL):
                build_layer(K, P, l, locals())
        except StopBuild:
            P.finish()
            P.emit()
            return nc
        for tt in range(8):
            stg, b_stg = A.alloc(f"ostg{tt}", (64 + 8 * (tt % 2)) * KB_, [128, D], F32)
            for cg in range(4):
                bk = 4 + (tt * 4 + cg) % 4
                for ci in range(4):
                    c = cg * 4 + ci
                    K.tr(K.bank(bk, 128, ci * 128), xT[:, c, tt * 128:(tt + 1) * 128], ident_f, [xch_b[c], b_idf], [K.pb[bk]])
                K.evac(stg[:, cg * 512:(cg + 1) * 512], K.bank(bk), [K.pb[bk]], [b_stg])
            P.dma("sp", y_d[tt * 128:(tt + 1) * 128, :], stg, reads=[b_stg])
        P.finish()
        P.emit()
    return nc


import types


def mod_gen(K, E, l, bk=3):
    for t in range(48):
        wt, wb = K.wload(E.w_ada[l][:, t * 256:(t + 1) * 256], KC, 256)
        for jj in range(2):
            for kc in range(KC):
                K.mm(K.bank(bk, 1, 510 + jj), wt[:, kc, jj * 128:(jj + 1) * 128], E.sc_b[:, kc:kc + 1], kc == 0, kc == KC - 1,
                     [wb, E.b_sc], [K.pb[bk]])
        K.cp("dve", E.modraw[:, 2 * t:2 * t + 2], K.bank(bk, 2, 510), [K.pb[bk]], [E.b_modraw])
        yield


def build_layer(K, P, l, env):
    E = types.SimpleNamespace(**env)
    A, H, C = K.A, K.H, K.C
    PI = math.pi
    for (t, b, src) in [(E.bada, E.b_bada, E.bada_d[l]), (E.ln1g, E.b_ln1g, E.ln1g_d[l]), (E.ln1b, E.b_ln1b, E.ln1b_d[l]),
                        (E.ln2g, E.b_ln2g, E.ln2g_d[l]), (E.ln2b, E.b_ln2b, E.ln2b_d[l]), (E.fdw, E.b_fdw, E.fdw_d[l]),
                        (E.fdb, E.b_fdb, E.fdb_d[l]), (E.cdw, E.b_cdw, E.cdw_d[l]), (E.cdb, E.b_cdb, E.cdb_d[l]),
                        (E.clg, E.b_clg, E.clg_d[l]), (E.clb, E.b_clb, E.clb_d[l]), (E.psc, E.b_psc, E.pool_sc_d[l]),
                        (E.sinkexp, E.b_sink, E.sink_d[l])]:
        P.dma("sp", t, src, writes=[b])
    K.act(E.sinkexp, E.sinkexp, AF.Exp, [E.b_sink], [E.b_sink])

    if l == 0:
        mg0 = mod_gen(K, E, 0)
        for _ in range(16):
            next(mg0)
    else:
        mg0 = iter(())
    env["_mg0"] = mg0
    K.tt("dve", E.mod[:, 0:32], E.modraw[:, 0:32], E.bada[:, 0:32], ALU.add, [E.b_modraw, E.b_bada], [E.b_mod])
    K.ts("dve", E.s1p, E.mod[:, 16:32], 1.0, None, ALU.add, None, [E.b_mod], [E.b_s1p])
    shift1, gate1, shift2, gate2 = E.mod[:, 0:16], E.mod[:, 32:48], E.mod[:, 48:64], E.mod[:, 80:96]
    ck(2)

    xs_b = P.bufs(KC, "xsb")

    def stash_x():
        for c in range(KC):
            P.dma("sp", E.xs_d[:, c * NT:(c + 1) * NT], E.xT[:, c, :], reads=[E.xch_b[c]], writes=[xs_b[c]])

    stash_x()
    if getattr(K, "ln_pre", False):
        R, Rb, MR, MRb = K.ln_stats_finish(D, 64 * KB_)
    else:
        R, Rb, MR, MRb = K.ln_stats(E.xchunks(), D, 64 * KB_)
    hT, b_h = H.alloc(f"h{l}", 0, [128, KC, NT], BF16)
    K.ln_apply(E.xchunks(), [(hT[:, c, :], b_h) for c in range(KC)], R, Rb, MR, MRb, E.s1p, shift1, [E.b_s1p, E.b_mod])

    ck(3)
    ys5, b_ys5 = A.alloc("ys5", 80 * KB_, [128, 4, NT], BF16)
    smp = [40 * KB_]
    smL = [56 * KB_]
    LONG = ("LL", "LI", "LLk", "LIk", "EF", "EFi", "sp_re", "sp_im", "h0t")

    def sm(name, shape):
        n = 4
        for s_ in shape[1:]:
            n *= s_
        if name in LONG:
            ap, b = A.alloc(name, smL[0], shape, F32)
            smL[0] += n
            assert smL[0] <= 60 * KB_
            return ap, b
        if smp[0] < 56 * KB_ and smp[0] + n > 56 * KB_:
            smp[0] = 60 * KB_
        ap, b = A.alloc(name, smp[0], shape, F32)
        smp[0] += n
        return ap, b

    lre, b_lre = sm("lre", [128, 32]); lim, b_lim = sm("lim", [128, 32]); dtt, b_dt = sm("dtt", [128, 32])
    lar, b_lar = sm("lar", [128, 32]); lai, b_lai = sm("lai", [128, 32])
    lrs, b_lrs = sm("lrs", [128, 32]); lis, b_lis = sm("lis", [128, 32])
    Bre, b_Bre = sm("Bre", [128, 32, 16]); Bim, b_Bim = sm("Bim", [128, 32, 16])
    Cre, b_Cre = sm("Cre", [128, 32, 16]); Cim, b_Cim = sm("Cim", [128, 32, 16])
    BBre, b_BBre = sm("BBre", [128, 32, 16]); BBim, b_BBim = sm("BBim", [128, 32, 16])
    tb1, b_tb1 = sm("tb1", [128, 32, 16]); tb2, b_tb2 = sm("tb2", [128, 32, 16])
    dcol, b_dcol = sm("dcol", [128, 32]); h0t, b_h0 = sm("h0t", [128, 2, 32])
    ang, b_ang = sm("ang", [128, 32, 8]); mga, b_mga = sm("mga", [128, 32, 8])
    sn, b_sn = sm("sn", [128, 32, 8]); cs, b_cs = sm("cs", [128, 32, 8])
    mg, b_mg = sm("mg", [128, 32, 8]); mgi, b_mgi = sm("mgi", [128, 32, 8])
    PYre, b_PYre = sm("PYre", [128, 32, 8]); PYim, b_PYim = sm("PYim", [128, 32, 8])
    PXre, b_PXre = sm("PXre", [128, 32, 8]); PXim, b_PXim = sm("PXim", [128, 32, 8])
    kq, b_kq = sm("kq", [128, 4]); sp_re, b_spre = sm("sp_re", [128, 32, 4]); sp_im, b_spim = sm("sp_im", [128, 32, 4])
    cfr, b_cfr = sm("cfr", [128, 32]); cfi, b_cfi = sm("cfi", [128, 32]); den, b_den = sm("den", [128, 32])
    tq1, b_tq1 = sm("tq1", [128, 32]); tq2, b_tq2 = sm("tq2", [128, 32])
    LL, b_LL = sm("LL", [128, 32, 2]); LI, b_LI = sm("LI", [128, 32, 2])
    LLk, b_LLk = sm("LLk", [128, 32, 2]); LIk, b_LIk = sm("LIk", [128, 32, 2])
    EF, b_EF = sm("EF", [128, 32, 2]); EFi, b_EFi = sm("EFi", [128, 32, 2])
    pipi, b_pipi = sm("pipi", [128, 1])
    assert smp[0] <= 76 * KB_, smp[0]
    for (t, b, src) in [(lar, b_lar, E.lam_re_d[l]), (lai, b_lai, E.lam_im_d[l]), (dtt, b_dt, E.logdt_d[l]),
                        (Bre, b_Bre, E.bre_d[l]), (Bim, b_Bim, E.bim_d[l]), (Cre, b_Cre, E.cre_d[l]), (Cim, b_Cim, E.cim_d[l]),
                        (dcol, b_dcol, E.dcol_d[l]), (h0t, b_h0, E.h0_d[l])]:
        P.dma("sp", t, src, writes=[b])
    K.memset("pool", pipi, -PI, [b_pipi])
    K.act(dtt, dtt, AF.Exp, [b_dt], [b_dt])
    K.tt("dve", lre, lar, dtt, ALU.mult, [b_lar, b_dt], [b_lre])
    K.tt("dve", lim, lai, dtt, ALU.mult, [b_lai, b_dt], [b_lim])
    K.ts("dve", lrs, lre, E.sgn[:, 0:1], None, ALU.mult, None, [b_lre, E.b_sgn], [b_lrs])
    K.ts("dve", lis, lim, E.sgn[:, 0:1], None, ALU.mult, None, [b_lim, E.b_sgn], [b_lis])

    def cexp(ang_t, b_a, mga_t, b_m, sn_t, b_s, cs_t, b_c, mg_t, b_g, shape, toff):
        n = 4
        for s_ in shape[1:]:
            n *= s_
        ni, b_ni = A.alloc("cx_ni", toff, shape, I32)
        nf, b_nf = A.alloc("cx_nf", toff + n, shape, F32)
        mm_, b_mm = A.alloc("cx_m", toff + 2 * n, shape, F32)
        for (dst, bd, off) in ((sn_t, b_s, 32.0), (cs_t, b_c, 32.25)):
            K.ts("dve", dst, ang_t, 1.0 / (2.0 * PI), off, ALU.mult, ALU.add, [b_a], [bd])
            K.cp("dve", ni, dst, [bd], [b_ni])
            K.cp("dve", nf, ni, [b_ni], [b_nf])
            K.tt("dve", dst, dst, nf, ALU.subtract, [bd, b_nf], [bd])
            K.ts("dve", mm_, dst, 0.5, None, ALU.is_gt, None, [bd], [b_mm])
            K.tt("dve", dst, dst, mm_, ALU.subtract, [bd, b_mm], [bd])
            K.ts("dve", mm_, dst, -0.5, None, ALU.is_lt, None, [bd], [b_mm])
            K.tt("dve", dst, dst, mm_, ALU.add, [bd, b_mm], [bd])
            K.act(dst, dst, AF.Sin, [bd], [bd], scale=2.0 * PI)
        K.act(mg_t, mga_t, AF.Exp, [b_m], [b_g])

    kio_b = E.kio[:, :].unsqueeze(1).broadcast_to([128, 32, 8])
    K.tt("dve", ang, lis[:, :].unsqueeze(2).broadcast_to([128, 32, 8]), kio_b, ALU.mult, [b_lis, E.b_kio], [b_ang])
    K.tt("dve", mga, lrs[:, :].unsqueeze(2).broadcast_to([128, 32, 8]), kio_b, ALU.mult, [b_lrs, E.b_kio], [b_mga])
    cexp(ang, b_ang, mga, b_mga, sn, b_sn, cs, b_cs, mg, b_mg, [128, 32, 8], 132 * KB_)
    ck(41)
    K.P.op("dve", lambda e: e.reciprocal(out=mgi, in_=mg), [b_mg], [b_mgi])
    K.tt("dve", PYre, mg, cs, ALU.mult, [b_mg, b_cs], [b_PYre])
    K.tt("dve", PYim, mg, sn, ALU.mult, [b_mg, b_sn], [b_PYim])
    K.tt("dve", PXre, mgi, cs, ALU.mult, [b_mgi, b_cs], [b_PXre])
    K.stt(PXim, mgi, -1.0, sn, ALU.mult, ALU.mult, [b_mgi, b_sn], [b_PXim])
    ck(42)
    K.memset("pool", kq[:, 0:1], 8.0, [b_kq])
    K.memset("pool", kq[:, 1:2], 1.0, [b_kq])
    K.ts("dve", kq[:, 2:3], E.sgn[:, 0:1], -3.5, 4.5, ALU.mult, ALU.add, [E.b_sgn, b_kq], [b_kq])
    K.ts("dve", kq[:, 3:4], E.sgn[:, 0:1], 3.5, 3.5, ALU.mult, ALU.add, [E.b_sgn, b_kq], [b_kq])
    ang4, b_ang4 = A.alloc("ang4", 128 * KB_ + 0, [128, 32, 4], F32)
    mga4, b_mga4 = A.alloc("mga4", 128 * KB_ + 512, [128, 32, 4], F32)
    sn4, b_sn4 = A.alloc("sn4", 128 * KB_ + 1024, [128, 32, 4], F32)
    cs4, b_cs4 = A.alloc("cs4", 128 * KB_ + 1536, [128, 32, 4], F32)
    mg4, b_mg4 = A.alloc("mg4", 128 * KB_ + 2048, [128, 32, 4], F32)
    kq_b = kq[:, :].unsqueeze(1).broadcast_to([128, 32, 4])
    K.tt("dve", ang4, lim[:, :].unsqueeze(2).broadcast_to([128, 32, 4]), kq_b, ALU.mult, [b_lim, b_kq], [b_ang4])
    K.tt("dve", mga4, lre[:, :].unsqueeze(2).broadcast_to([128, 32, 4]), kq_b, ALU.mult, [b_lre, b_kq], [b_mga4])
    cexp(ang4, b_ang4, mga4, b_mga4, sn4, b_sn4, cs4, b_cs4, mg4, b_mg4, [128, 32, 4], 132 * KB_)
    K.tt("dve", sp_re, mg4, cs4, ALU.mult, [b_mg4, b_cs4], [b_spre])
    K.tt("dve", sp_im, mg4, sn4, ALU.mult, [b_mg4, b_sn4], [b_spim])
    ck(43)
    K.ts("dve", tq1, sp_re[:, :, 1], -1.0, None, ALU.add, None, [b_spre], [b_tq1])
    K.tt("dve", den, lar, lar, ALU.mult, [b_lar], [b_den])
    K.tt("dve", tq2, lai, lai, ALU.mult, [b_lai], [b_tq2])
    K.tt("dve", den, den, tq2, ALU.add, [b_den, b_tq2], [b_den])
    K.P.op("dve", lambda e: e.reciprocal(out=den, in_=den), [b_den], [b_den])
    K.tt("dve", cfr, tq1, lar, ALU.mult, [b_tq1, b_lar], [b_cfr])
    K.tt("dve", tq2, sp_im[:, :, 1], lai, ALU.mult, [b_spim, b_lai], [b_tq2])
    K.tt("dve", cfr, cfr, tq2, ALU.add, [b_cfr, b_tq2], [b_cfr])
    K.tt("dve", cfr, cfr, den, ALU.mult, [b_cfr, b_den], [b_cfr])
    K.tt("dve", cfi, sp_im[:, :, 1], lar, ALU.mult, [b_spim, b_lar], [b_cfi])
    K.tt("dve", tq2, tq1, lai, ALU.mult, [b_tq1, b_lai], [b_tq2])
    K.tt("dve", cfi, cfi, tq2, ALU.subtract, [b_cfi, b_tq2], [b_cfi])
    K.tt("dve", cfi, cfi, den, ALU.mult, [b_cfi, b_den], [b_cfi])
    cfr_b = cfr[:, :].unsqueeze(2).broadcast_to([128, 32, 16])
    cfi_b = cfi[:, :].unsqueeze(2).broadcast_to([128, 32, 16])
    K.tt("dve", tb1, Bre, cfr_b, ALU.mult, [b_Bre, b_cfr], [b_tb1])
    K.tt("dve", tb2, Bim, cfi_b, ALU.mult, [b_Bim, b_cfi], [b_tb2])
    K.tt("dve", BBre, tb1, tb2, ALU.subtract, [b_tb1, b_tb2], [b_BBre])
    K.tt("dve", tb1, Bim, cfr_b, ALU.mult, [b_Bim, b_cfr], [b_tb1])
    K.tt("dve", tb2, Bre, cfi_b, ALU.mult, [b_Bre, b_cfi], [b_tb2])
    K.tt("dve", BBim, tb1, tb2, ALU.add, [b_tb1, b_tb2], [b_BBim])
    ck(44)
    K.cp("dve", LL[:, :, 0], sp_re[:, :, 0], [b_spre], [b_LL])
    K.cp("dve", LL[:, :, 1], sp_re[:, :, 0], [b_spre], [b_LL])
    K.ts("dve", LI[:, :, 0], sp_im[:, :, 0], -1.0, None, ALU.mult, None, [b_spim], [b_LI])
    K.cp("dve", LI[:, :, 1], sp_im[:, :, 0], [b_spim], [b_LI])
    K.ts("dve", LLk, LL, E.flag[:, 0:1], None, ALU.mult, None, [b_LL, E.b_flag], [b_LLk])
    K.ts("dve", LIk, LI, E.flag[:, 0:1], None, ALU.mult, None, [b_LI, E.b_flag], [b_LIk])
    K.cp("dve", EF[:, :, 0], sp_re[:, :, 3], [b_spre], [b_EF])
    K.cp("dve", EF[:, :, 1], sp_re[:, :, 3], [b_spre], [b_EF])
    K.ts("dve", EFi[:, :, 0], sp_im[:, :, 3], -1.0, None, ALU.mult, None, [b_spim], [b_EFi])
    K.cp("dve", EFi[:, :, 1], sp_im[:, :, 3], [b_spim], [b_EFi])

    ck(45)
    Yb, b_Yb = A.alloc("Yb", 0, [128, 32, 2, 128], BF16)
    XTb, b_XTb = A.alloc("XTb", 16 * KB_, [128, 32, 2, 128], BF16)
    Tb, b_Tb = A.alloc("Tb", 32 * KB_, [128, 32, 128], BF16)
    Xb, b_Xb = A.alloc("Xb", 88 * KB_, [128, 32, 2, 128], BF16)
    t1, b_t1 = A.alloc("t1x", 72 * KB_ + 4096, [128, 8, 8, 16], F32)
    t2, b_t2 = A.alloc("t2x", 124 * KB_, [128, 8, 8, 16], F32)

    def outer(dst, Pr, bPr, Pi, bPi, Vr, bVr, Vi, bVi, g0, sign_im):
        gs = slice(g0, g0 + 8)
        pr = Pr[:, gs, :].unsqueeze(3).broadcast_to([128, 8, 8, 16])
        pi = Pi[:, gs, :].unsqueeze(3).broadcast_to([128, 8, 8, 16])
        vr = Vr[:, gs, :].unsqueeze(2).broadcast_to([128, 8, 8, 16])
        vi = Vi[:, gs, :].unsqueeze(2).broadcast_to([128, 8, 8, 16])
        d0 = dst[:, gs, 0, :].rearrange("p g (s c) -> p g s c", s=8)
        d1 = dst[:, gs, 1, :].rearrange("p g (s c) -> p g s c", s=8)
        K.tt("dve", t1, pr, vr, ALU.mult, [bPr, bVr], [b_t1])
        K.tt("dve", t2, pi, vi, ALU.mult, [bPi, bVi], [b_t2])
        K.tt("dve", d0, t1, t2, ALU.subtract, [b_t1, b_t2], [dst_b[0]])
        K.tt("dve", t1, pr, vi, ALU.mult, [bPr, bVi], [b_t1])
        K.tt("dve", t2, pi, vr, ALU.mult, [bPi, bVr], [b_t2])
        if sign_im > 0:
            K.tt("dve", d1, t1, t2, ALU.add, [b_t1, b_t2], [dst_b[0]])
        else:
            K.stt(d1, t1, -1.0, t2, ALU.mult, ALU.subtract, [b_t1, b_t2], [dst_b[0]])

    dst_b = [b_Xb]
    for g0 in range(0, 32, 8):
        outer(Xb, PXre, b_PXre, PXim, b_PXim, BBre, b_BBre, BBim, b_BBim, g0, +1)
    dst_b = [b_Yb]
    for g0 in range(0, 32, 8):
        outer(Yb, PYre, b_PYre, PYim, b_PYim, Cre, b_Cre, Cim, b_Cim, g0, -1)
    ck(46)
    ta, b_ta = A.alloc("ta", 72 * KB_ + 4096, [128, 128], F32)
    tb_, b_tb = A.alloc("tbb", 72 * KB_ + 4096 + 512, [128, 128], F32)
    tmd, b_tmd = A.alloc("tmd", 72 * KB_ + 4096 + 1024, [128, 128], F32)
    K.tt("dve", tmd, E.tm0, E.tm1, ALU.subtract, [E.b_tm0, E.b_tm1], [b_tmd])
    Xz, b_Xz = A.alloc("Xz0", 104 * KB_, [128, 32, 2, 128], BF16)
    K.memset("dve", Xz, 0.0, [b_Xz])
    K.cp("dve", Xz[0:64], Xb[0:64], [b_Xb], [b_Xz])
    for g in range(32):
        bk = 4 + g % 4
        for r in range(2):
            K.mm(K.bank(bk, 128, 0), Xz[:, g, r, :], Yb[:, g, r, :], r == 0, r == 1, [b_Xz, b_Yb], [K.pb[bk]])
        for r in range(2):
            K.mm(K.bank(bk, 128, 128), Xb[:, g, r, :], Yb[:, g, r, :], r == 0, r == 1, [b_Xb, b_Yb], [K.pb[bk]])
        K.tt("dve", ta, K.bank(bk, 128, 0), tmd, ALU.mult, [K.pb[bk], b_tmd], [b_ta])
        K.tt("dve", tb_, K.bank(bk, 128, 128), E.tm1, ALU.mult, [K.pb[bk], E.b_tm1], [b_tb])
        K.tt("dve", ta, ta, tb_, ALU.add, [b_ta, b_tb], [b_ta])
        K.stt(Tb[:, g, :], E.ident_f, dcol[:, g:g + 1], ta, ALU.mult, ALU.add, [E.b_idf, b_dcol, b_ta], [b_Tb])
    ck(47)
    for g0 in range(0, 32, 4):
        bk = 4 + (g0 // 4) % 4
        for gi in range(4):
            for r in range(2):
                K.tr(K.bank_bf(bk)[:, (gi * 2 + r) * 128:(gi * 2 + r + 1) * 128], Xb[:, g0 + gi, r, :], E.ident_b, [b_Xb, E.b_idb], [K.pb[bk]])
        K.evac(XTb[:, g0:g0 + 4, :, :], K.bank_bf(bk).rearrange("p (g r m) -> p g r m", g=4, r=2), [K.pb[bk]], [b_XTb])
    env["_s5"] = dict(Yb=Yb, b_Yb=b_Yb, XTb=XTb, b_XTb=b_XTb, Tb=Tb, b_Tb=b_Tb, LL=LL, b_LL=b_LL, LI=LI, b_LI=b_LI, LLk=LLk, b_LLk=b_LLk,
                      LIk=LIk, b_LIk=b_LIk, EF=EF, b_EF=b_EF, EFi=EFi, b_EFi=b_EFi, sp_re=sp_re, b_spre=b_spre, sp_im=sp_im, b_spim=b_spim,
                      h0t=h0t, b_h0=b_h0, ys5=ys5, b_ys5=b_ys5)
    env["_h"] = (hT, b_h)
    env["_mods"] = (shift1, gate1, shift2, gate2)
    env["_xs_b"] = xs_b
    for _ in range(6):
        next(mg0, None)
    ck(4)
    build_s5_run(K, P, l, env)
    ck(5)
    build_mixers(K, P, l, env)
    ck(9)
    build_merge_ffn(K, P, l, env)
    ck(12)


def build_s5_run(K, P, l, env):
    E = types.SimpleNamespace(**env)
    S5 = types.SimpleNamespace(**env["_s5"])
    A = K.A
    hT, b_h = env["_h"]
    u_blk, b_ub = A.alloc("u_blk", 48 * KB_, [128, 32, 8, 16], BF16)
    U_T, b_UT = A.alloc("U_T", 40 * KB_, [128, 32, 128], BF16)
    for blk in range(2):
        wt, wb = K.wload(E.w_in[l][:, 512 + blk * 256:512 + (blk + 1) * 256], KC, 256)
        for i in range(8):
            bk = 4 + i % 4
            for kc in range(KC):
                K.mm(K.bank(bk, 256), hT[:, kc, i::8], wt[:, kc, :], kc == 0, kc == KC - 1, [wb, b_h], [K.pb[bk]])
            K.evac(u_blk[:, blk * 16:(blk + 1) * 16, i, :], K.bank(bk, 256).rearrange("p (g c) -> p g c", g=16), [K.pb[bk]], [b_ub])
    for g0 in range(0, 32, 8):
        bk = 4 + (g0 // 8) % 4
        for gi in range(8):
            K.tr(K.bank_bf(bk)[:, gi * 128:(gi + 1) * 128], u_blk[:, g0 + gi, :, :].rearrange("p i c -> p (i c)"), E.ident_b,
                 [b_ub, E.b_idb], [K.pb[bk]])
        K.evac(U_T[:, g0:g0 + 8, :], K.bank_bf(bk).rearrange("p (g j) -> p g j", g=8), [K.pb[bk]], [b_UT])
    S, b_S = A.alloc("Sst", 88 * KB_, [128, 32, 2, 129], F32)
    b_S0, b_S1 = b_S, A.alias("S1", 88 * KB_, 33024)
    b_S1.r = dict(b_S.r)
    for g0 in range(0, 32, 4):
        for r in range(2):
            bk = 4 + ((g0 // 4) * 2 + r) % 4
            for gi in range(4):
                K.mm(K.bank(bk, 128, gi * 128), S5.XTb[:, g0 + gi, r, :], U_T[:, g0 + gi, :], True, True, [S5.b_XTb, b_UT], [K.pb[bk]])
            pv = K.bank(bk).rearrange("p (g j) -> p g j", g=4)
            K.cp("act", S[0:64, g0:g0 + 4, r, 1:129], pv[0:64], [K.pb[bk]], [b_S0])
            K.cp("dve", S[64:128, g0:g0 + 4, r, 1:129], pv[64:128][:, :, ::-1], [K.pb[bk]], [b_S1])
    i1, b_i1 = A.alloc("s5i1", 72 * KB_, [128, 32, 2], F32)
    i2, b_i2 = A.alloc("s5i2", 72 * KB_ + 256, [128, 32, 2], F32)
    h0v = S5.h0t[:, :, :].rearrange("p r g -> p g r")
    h0s = S5.h0t[:, ::-1, :].rearrange("p r g -> p g r")
    mr = S5.sp_re[:, :, 2:3].broadcast_to([128, 32, 2])
    K.tt("dve", i1, h0v, mr, ALU.mult, [S5.b_h0, S5.b_spre], [b_i1])
    K.tt("dve", i2, h0s, S5.sp_im[:, :, 2:3].broadcast_to([128, 32, 2]), ALU.mult, [S5.b_h0, S5.b_spim], [b_i2])
    K.tt("dve", S[:, :, 0, 0], i1[:, :, 0], i2[:, :, 0], ALU.subtract, [b_i1, b_i2], [b_S0, b_S1])
    K.tt("dve", S[:, :, 1, 0], i1[:, :, 1], i2[:, :, 1], ALU.add, [b_i1, b_i2], [b_S0, b_S1])
    finA, b_fA = A.alloc("finA", 72 * KB_ + 512, [128, 32, 2, 4], F32)
    b_fA1 = A.alias("finA1", 72 * KB_ + 512, 1024)
    b_fA1.r = dict(b_fA.r)
    tmp = {}
    for par in (0, 1):
        o = 72 * KB_ + 2048 + par * 768
        tmp[par] = [A.alloc(f"sA{par}", o, [128, 32, 2], F32), A.alloc(f"sT1{par}", o + 256, [128, 32, 2], F32),
                    A.alloc(f"sT2{par}", o + 512, [128, 32, 2], F32)]
    mg0 = env["_mg0"]
    bSS = [b_S0, b_S1]
    for j in range(128):
        if j % 8 == 0:
            next(mg0, None)
        (Aa, bA), (T1, bT1), (T2, bT2) = tmp[j % 2]
        cur, zc, dst = S[:, :, :, j], S[:, :, :, j + 1], S[:, :, :, j + 1]
        bnd = (j % 32 == 31)
        if bnd:
            q = j // 32
            K.tt("dve", finA[0:64, :, :, q], cur[0:64], zc[0:64], ALU.add, bSS, [b_fA])
            K.tt("dve", finA[64:128, :, :, 3 - q], cur[64:128], zc[64:128], ALU.add, bSS, [b_fA1])
            ll, bll, li, bli = S5.LLk, S5.b_LLk, S5.LIk, S5.b_LIk
            for rs, q_, bF in ((slice(0, 64), q, b_fA), (slice(64, 128), 3 - q, b_fA1)):
                Av = finA[rs, :, :, q_]
                K.tt("dve", T1[rs], Av, ll[rs], ALU.mult, [bF, bll], [bT1])
                K.tt("dve", T2[rs], Av[:, :, ::-1], li[rs], ALU.mult, [bF, bli], [bT2])
            K.tt("dve", dst, T1, T2, ALU.add, [bT1, bT2], bSS)
        else:
            K.tt("dve", Aa, cur, zc, ALU.add, bSS, [bA])
            K.tt("dve", T1, Aa, S5.LL, ALU.mult, [bA, S5.b_LL], [bT1])
            K.tt("dve", T2, Aa[:, :, ::-1], S5.LI, ALU.mult, [bA, S5.b_LI], [bT2])
            K.tt("dve", dst, T1, T2, ALU.add, [bT1, bT2], bSS)
    f1, b_f1 = A.alloc("s5f1", 76 * KB_, [128, 32, 2, 4], F32)
    f2, b_f2 = A.alloc("s5f2", 77 * KB_, [128, 32, 2, 4], F32)
    K.tt("dve", f1, finA, S5.EF[:, :, :].unsqueeze(3).broadcast_to([128, 32, 2, 4]), ALU.mult, [b_fA, b_fA1, S5.b_EF], [b_f1])
    K.tt("dve", f2, finA[:, :, ::-1, :], S5.EFi[:, :, :].unsqueeze(3).broadcast_to([128, 32, 2, 4]), ALU.mult, [b_fA, b_fA1, S5.b_EFi], [b_f2])
    K.tt("dve", f1, f1, f2, ALU.add, [b_f1, b_f2], [b_f1])
    fst, b_fst = A.alloc("s5fst", 48 * KB_, [32, 4, 2, 128], F32)
    for s_ in range(4):
        for r in range(2):
            K.tr(K.ps_t[0:32, 4 * 512 + (s_ * 2 + r) * 128:4 * 512 + (s_ * 2 + r + 1) * 128], f1[:, :, r, s_], E.ident_f, [b_f1, E.b_idf], [K.pb[4], K.pb[5]])
    K.cp("dve", fst, K.ps_t[0:32, 4 * 512:4 * 512 + 1024].rearrange("p (s r m) -> p s r m", s=4, r=2), [K.pb[4], K.pb[5]], [b_fst])
    for s_ in range(4):
        for r in range(2):
            for d_ in range(2):
                P.dma("sp", E.sout_d[s_, l, d_, r], fst[:, s_, r, d_ * 64:(d_ + 1) * 64], reads=[b_fst])
    Sb, b_Sb = A.alloc("Sb", 16 * KB_, [128, 32, 2, 128], BF16)
    K.cp("act", Sb[0:64], S[0:64, :, :, 0:128], [b_S0], [b_Sb])
    K.cp("dve", Sb[64:128], S[64:128, :, :, 127::-1], [b_S1], [b_Sb])
    yblk, b_yb = A.alloc("yblk", 72 * KB_, [128, 8, 32, 16], BF16)
    yT, b_yT = A.alloc("yT", 124 * KB_, [128, 4, NT], BF16)
    for g0 in range(0, 32, 4):
        bk = 4 + (g0 // 4) % 4
        for gi in range(4):
            g = g0 + gi
            o = K.bank(bk, 128, gi * 128)
            K.mm(o, U_T[:, g, :], S5.Tb[:, g, :], True, False, [b_UT, S5.b_Tb], [K.pb[bk]])
            K.mm(o, Sb[:, g, 0, :], S5.Yb[:, g, 0, :], False, False, [b_Sb, S5.b_Yb], [K.pb[bk]])
            K.mm(o, Sb[:, g, 1, :], S5.Yb[:, g, 1, :], False, True, [b_Sb, S5.b_Yb], [K.pb[bk]])
        s = (g0 // 4) % 2
        ga, b_ga = A.alloc(f"gla{g0}", (132 + 0) * KB_ + s * 2048, [128, 512], F32)
        pv = K.bank(bk)
        K.act(ga, pv, AF.Square, [K.pb[bk]], [b_ga])
        K.ts("dve", ga, ga, 0.044715, 1.0, ALU.mult, ALU.add, [b_ga], [b_ga])
        K.tt("dve", ga, ga, pv, ALU.mult, [b_ga, K.pb[bk]], [b_ga])
        K.act(ga, ga, AF.Sigmoid, [b_ga], [b_ga], scale=1.5957691216057308)
        K.tt("dve", yblk[:, :, g0:g0 + 4, :].rearrange("p i g c -> p g i c"), ga[:, :].rearrange("p (g i c) -> p g i c", g=4, i=8),
             pv.rearrange("p (g i c) -> p g i c", g=4, i=8), ALU.mult, [b_ga, K.pb[bk]], [b_yb])
    for i in range(8):
        bk = 4 + i % 4
        for q in range(4):
            K.tr(K.bank_bf(bk)[:, q * 128:(q + 1) * 128], yblk[:, i, q * 8:(q + 1) * 8, :].rearrange("p g c -> p (g c)"), E.ident_b,
                 [b_yb, E.b_idb], [K.pb[bk]])
        K.evac(yT[:, :, i::8], K.bank_bf(bk)[:, 0:512].rearrange("p (q j) -> p q j", q=4), [K.pb[bk]], [b_yT])
    wg, wgb = K.wload(E.w_glu[l], 4, 512)
    for m in range(4):
        pv, pbs = K.proj_fm(wg, wgb, 4, m * 128, yT, b_yT, m % 2)
        sg, b_sg = A.alloc(f"glus{m}", 132 * KB_ + (m % 2) * 2048, [128, NT], BF16)
        K.act(sg, pv, AF.Sigmoid, pbs, [b_sg])
        K.tt("dve", S5.ys5[:, m, :], yT[:, m, :], sg, ALU.mult, [b_yT, b_sg], [S5.b_ys5])


def build_mixers(K, P, l, env):
    E = types.SimpleNamespace(**env)
    A = K.A
    hT, b_h = env["_h"]
    flagc = E.flag[:, 0:1]
    yconv, b_yc = A.alloc("yconv", 72 * KB_, [128, 4, NT], BF16)
    xpad, b_xp = A.alloc("xpad", 0, [128, 4, 4, 286], BF16)
    acc = K.ar_t[:, 20 * 256:20 * 256 + 4 * NT].rearrange("p (a b) -> p a b", a=4)
    accb = [A.alloc(f"cacc{c}", 20 * KB_ + c * 4096, [128, NT], F32)[1] for c in range(4)]
    dg, b_dg = A.alloc("cdiag", 88 * KB_, [128, 4, 31, 128], BF16)
    for c in range(4):
        K.tt("dve", dg[:, c, :, :], E.ident_f[:, :].unsqueeze(1).broadcast_to([128, 31, 128]),
             E.cdw[:, c, :].unsqueeze(2).broadcast_to([128, 31, 128]), ALU.mult, [E.b_idf, E.b_cdw], [b_dg])
    K.memset("dve", xpad[:, :, 0, 0:15], 0.0, [b_xp])
    K.memset("dve", xpad[:, :, 3, 271:286], 0.0, [b_xp])
    for c in range(4):
        wa, wab = K.wload(E.w_in[l][:, 2560 + c * 128:2560 + (c + 1) * 128], KC, 128)
        wg, wgb = K.wload(E.w_in[l][:, 3072 + c * 128:3072 + (c + 1) * 128], KC, 128)
        pa, pab = K.proj_fm(wa, wab, KC, 0, hT, b_h, 0)
        pg, pgb = K.proj_fm(wg, wgb, KC, 0, hT, b_h, 1)
        sg, b_sg = A.alloc(f"csig{c}", 36 * KB_, [128, NT], F32)
        K.act(sg, pg, AF.Sigmoid, pgb, [b_sg])
        K.tt("dve", xpad[:, c, :, 15:271], pa.rearrange("p (s t) -> p s t", s=4), sg[:, :].rearrange("p (s t) -> p s t", s=4), ALU.mult,
             pab + [b_sg], [b_xp])
    K.ts("dve", xpad[:, :, 1:4, 0:15], xpad[:, :, 0:3, 256:271], flagc, None, ALU.mult, None, [b_xp, E.b_flag], [b_xp])
    K.ts("dve", xpad[:, :, 0:3, 271:286], xpad[:, :, 1:4, 15:30], flagc, None, ALU.mult, None, [b_xp, E.b_flag], [b_xp])
    for c in range(4):
        for half in range(2):
            bk = 4 + (c * 2 + half) % 4
            for k in range(31):
                K.mm(K.bank(bk), dg[:, c, k, :], xpad[:, c, half * 2:half * 2 + 2, k:k + 256], k == 0, k == 30, [b_dg, b_xp], [K.pb[bk]])
            K.act(acc[:, c, half * 512:(half + 1) * 512], K.bank(bk), AF.Identity, [K.pb[bk], E.b_cdb], [accb[c]], bias=E.cdb[:, c:c + 1])
    xs = [(acc[:, c, :], accb[c]) for c in range(4)]
    R, Rb, MR, MRb = K.ln_stats(xs, 512, 40 * KB_)
    K.ln_apply(xs, [(yconv[:, c, :], b_yc) for c in range(4)], R, Rb, MR, MRb, E.clg, E.clb, [E.b_clg, E.b_clb], func=AF.Silu)

    ck(6)
    ypool, b_yp = A.alloc("ypool", 64 * KB_, [128, 4, NT], BF16)
    apad, b_ap = A.alloc("apad", 0, [128, 4, 4, 272], F32)
    pooled, b_pl = A.alloc("pooled", 31 * KB_, [128, 4, NT], BF16)
    K.memset("dve", apad[:, :, 0, 0:8], 0.0, [b_ap])
    K.memset("dve", apad[:, :, 3, 264:272], 0.0, [b_ap])
    for half in range(2):
        wt, wb = K.wload(E.w_in[l][:, half * 256:(half + 1) * 256], KC, 256)
        for m in range(2):
            g = half * 2 + m
            pv, pbs = K.proj_fm(wt, wb, KC, m * 128, hT, b_h, m)
            K.evac(apad[:, g, :, 8:264], pv.rearrange("p (s t) -> p s t", s=4), pbs, [b_ap])
    K.ts("dve", apad[:, :, 1:4, 0:8], apad[:, :, 0:3, 256:264], flagc, None, ALU.mult, None, [b_ap, E.b_flag], [b_ap])
    K.ts("dve", apad[:, :, 0:3, 264:272], apad[:, :, 1:4, 8:16], flagc, None, ALU.mult, None, [b_ap, E.b_flag], [b_ap])
    ck(61)
    for g, w in enumerate((2, 4, 8, 16)):
        left = w // 2
        t1, b1 = A.alloc(f"pt1{g}", 17 * KB_, [128, 4, 272], F32)
        t2, b2 = A.alloc(f"pt2{g}", 22 * KB_, [128, 4, 272], F32)
        rc, brc = A.alloc(f"prc{g}", 27 * KB_, [128, NT], F32)
        P.dma("sp", rc, E.pool_rc_d[:, g, :], writes=[brc])
        src, bsrc = apad[:, g, :, :], b_ap
        dst, bdst = t1, b1
        n = 272
        step = 1
        while step < w:
            n2 = n - step
            K.tt("dve", dst[:, :, 0:n2], src[:, :, 0:n2], src[:, :, step:step + n2], ALU.add, [bsrc], [bdst])
            src, bsrc = dst, bdst
            dst, bdst = (t2, b2) if dst is t1 else (t1, b1)
            n = n2
            step *= 2
        o = 8 - left
        K.tt("dve", dst[:, :, 0:256], src[:, :, o:o + 256], rc[:, :].rearrange("p (s t) -> p s t", s=4), ALU.mult, [bsrc, brc], [bdst])
        K.tt("dve", pooled[:, g, :].rearrange("p (s t) -> p s t", s=4), dst[:, :, 0:256], apad[:, g, :, 8:264], ALU.subtract, [bdst, b_ap], [b_pl])
    ck(62)
    pwt, pwb = A.alloc("poolw", 40 * KB_, [128, 4, 128], BF16)
    for g in range(4):
        P.dma("pool", pwt[:, g, :], E.pool_w[l, g], writes=[pwb])
    for g in range(4):
        for tt in range(2):
            K.mm(K.bank(g % 2 * 2 + tt), pwt[:, g, :], pooled[:, g, tt * 512:(tt + 1) * 512], True, True, [pwb, b_pl], [K.pb[g % 2 * 2 + tt]])
        b0 = g % 2 * 2
        K.act(ypool[:, g, :], K.ps_t[:, b0 * 512:(b0 + 2) * 512], AF.Identity, [K.pb[b0], K.pb[b0 + 1], E.b_psc], [b_yp], scale=E.psc[:, g:g + 1])

    ck(7)
    attnT, b_at = A.alloc("attnT", 88 * KB_, [128, 8, NT], BF16)
    qT, b_q = A.alloc("qT", 0, [128, 8, NT], BF16)
    kT, b_k = A.alloc("kT", 16 * KB_, [128, 2, NT], BF16)
    ckT, b_ck = A.alloc("ckT", 20 * KB_, [128, 2, 512], BF16)
    v_sb, b_v = A.alloc("v_sb", 22 * KB_, [128, 8, 256], BF16)
    cv_sb, b_cv = A.alloc("cv_sb", 26 * KB_, [128, 4, 256], BF16)
    ck_tok, b_ckt = A.alloc("ck_tok", 28 * KB_, [128, 4, 256], BF16)
    masks, b_mk = A.alloc("masks", 30 * KB_, [128, 8, 7 * 128], BF16)
    cosT, b_cos = A.alloc("cosT", 44 * KB_, [128, NT], F32)
    sinT, b_sin = A.alloc("sinT", 48 * KB_, [128, NT], F32)
    P.dma("pool", masks, E.masks_d.rearrange("p (a b) -> p a b", a=8), writes=[b_mk])
    ti, b_ti = A.alloc("rp_ti", 110 * KB_, [128, NT], I32)
    ri, b_ri = A.alloc("rp_ri", 114 * KB_, [128, NT], I32)
    pos, b_pos = A.alloc("rp_pos", 118 * KB_, [128, NT], F32)
    ni, b_ni = A.alloc("rp_ni", 122 * KB_, [128, NT], I32)
    nf, b_nf = A.alloc("rp_nf", 126 * KB_, [128, NT], F32)
    mm_, b_mm = A.alloc("rp_mm", 130 * KB_, [128, NT], F32)
    pi_, b_pi = A.alloc("rp_pi", 134 * KB_, [128, 1], I32)
    inv, b_inv = A.alloc("rp_inv", 134 * KB_ + 4, [128, 1], F32)
    K.P.op("pool", lambda e: e.iota(ti, pattern=[[1, NT]], base=0, channel_multiplier=0), (), [b_ti])
    K.P.op("pool", lambda e: e.iota(pi_, pattern=[[0, 1]], base=0, channel_multiplier=1), (), [b_pi])
    K.ts("dve", ri[0:64], ti[0:64], 6, None, ALU.arith_shift_right, None, [b_ti], [b_ri])
    K.ts("dve", ri[64:128], ti[64:128], 63, None, ALU.bitwise_and, None, [b_ti], [b_ri])
    K.cp("dve", pos, ri, [b_ri], [b_pos])
    K.ts("dve", pi_, pi_, 31, None, ALU.bitwise_and, None, [b_pi], [b_pi])
    K.cp("dve", inv, pi_, [b_pi], [b_inv])
    K.act(inv, inv, AF.Exp, [b_inv], [b_inv], scale=-math.log(10000.0) / 32.0)
    K.ts("dve", inv, inv, flagc, None, ALU.mult, None, [b_inv, E.b_flag], [b_inv])
    K.ts("dve", pos, pos, inv[:, 0:1], None, ALU.mult, None, [b_pos, b_inv], [b_pos])
    for (dst, bd, off) in ((sinT, b_sin, 32.0), (cosT, b_cos, 32.25)):
        K.ts("dve", dst, pos, 1.0 / (2.0 * math.pi), off, ALU.mult, ALU.add, [b_pos], [bd])
        K.cp("dve", ni, dst, [bd], [b_ni])
        K.cp("dve", nf, ni, [b_ni], [b_nf])
        K.tt("dve", dst, dst, nf, ALU.subtract, [bd, b_nf], [bd])
        K.ts("dve", mm_, dst, 0.5, None, ALU.is_gt, None, [bd], [b_mm])
        K.tt("dve", dst, dst, mm_, ALU.subtract, [bd, b_mm], [bd])
        K.ts("dve", mm_, dst, -0.5, None, ALU.is_lt, None, [bd], [b_mm])
        K.tt("dve", dst, dst, mm_, ALU.add, [bd, b_mm], [bd])
        K.act(dst, dst, AF.Sin, [bd], [bd], scale=2.0 * math.pi)
    for b_ in range(4):
        P.dma("pool", cv_sb[:, b_, :], E.cv_d[l, b_ * 128:(b_ + 1) * 128, :], writes=[b_cv])
        P.dma("pool", ck_tok[:, b_, :], E.ck_d[l, b_ * 128:(b_ + 1) * 128, :], writes=[b_ckt])
    for kvh in range(2):
        for b_ in range(4):
            K.tr(K.bank_bf(4 + kvh)[:, b_ * 128:(b_ + 1) * 128], ck_tok[:, b_, kvh * 128:(kvh + 1) * 128], E.ident_b, [b_ckt, E.b_idb], [K.pb[4 + kvh]])
        K.evac(ckT[:, kvh, :], K.bank_bf(4 + kvh)[:, 0:512], [K.pb[4 + kvh]], [b_ck])

    ck(71)

    def rope(pv, pbs, dst, bdst, idx):
        qr, bqr = A.alloc(f"qraw{idx}", 52 * KB_ + (idx % 2) * 2048, [128, NT], BF16)
        t1, bt1 = A.alloc(f"ropt{idx}", 56 * KB_, [128, NT], F32)
        K.cp("act", qr, pv, pbs, [bqr])
        K.tt("dve", t1, pv, cosT, ALU.mult, pbs + [b_cos], [bt1])
        for tt in range(2):
            K.mm(K.bank(4 + tt), E.rotP_b, qr[:, tt * 512:(tt + 1) * 512], True, True, [E.b_rot, bqr], [K.pb[4 + tt]])
        t2, bt2 = A.alloc(f"ropu{idx}", 60 * KB_, [128, NT], F32)
        K.tt("dve", t2, K.ps_t[:, 4 * 512:6 * 512], sinT, ALU.mult, [K.pb[4], K.pb[5], b_sin], [bt2])
        K.tt("dve", dst, t1, t2, ALU.add, [bt1, bt2], [bdst])

    idx = 0
    for blk in range(4):
        wt, wb = K.wload(E.w_in[l][:, 1024 + blk * 256:1024 + (blk + 1) * 256], KC, 256)
        for m in range(2):
            pv, pbs = K.proj_fm(wt, wb, KC, m * 128, hT, b_h, m)
            ck(711)
            rope(pv, pbs, qT[:, blk * 2 + m, :], b_q, idx)
            ck(712)
            idx += 1
    ck(72)
    wk, wkb = K.wload(E.w_in[l][:, 2048:2304], KC, 256)
    for m in range(2):
        pv, pbs = K.proj_fm(wk, wkb, KC, m * 128, hT, b_h, m)
        rope(pv, pbs, kT[:, m, :], b_k, idx)
        idx += 1
    ck(73)
    wv, wvb = K.wload(E.w_in[l][:, 2304:2560], KC, 256)
    for tt in range(8):
        for which, (wt, wb, od) in enumerate(((wk, wkb, E.kout_d), (wv, wvb, E.vout_d))):
            bk = (tt * 2 + which) % 4
            for kc in range(KC):
                K.mm(K.bank(bk, 256), hT[:, kc, tt * 128:(tt + 1) * 128], wt[:, kc, :], kc == 0, kc == KC - 1, [wb, b_h], [K.pb[bk]])
            stg, bst = A.alloc(f"kvst{tt}{which}", 104 * KB_ + ((tt * 2 + which) % 4) * 1024, [128, 256], F32)
            K.cp("act", stg, K.bank(bk, 256), [K.pb[bk]], [bst])
            P.dma("sp", od[l, tt * 128:(tt + 1) * 128, :], stg, reads=[bst])
            if which == 1:
                K.cp("dve", v_sb[:, tt, :], K.bank(bk, 256), [K.pb[bk]], [b_v])
    ck(74)
    scale = 128.0 ** -0.5
    for h in range(8):
        kvh = h // 4
        for i in range(8):
            kbs = [min(max(i + rel, 0), 7) for rel in (-1, 0, 1)]
            b0 = 4 + 2 * ((h * 8 + i) % 2)
            for n_, kb in enumerate(kbs):
                K.mm(K.ps_t[:, b0 * 512 + n_ * 128:b0 * 512 + (n_ + 1) * 128], kT[:, kvh, kb * 128:(kb + 1) * 128], qT[:, h, i * 128:(i + 1) * 128], True, True,
                     [b_k, b_q], [K.pb[b0], K.pb[b0 + 1]])
            for n_ in range(4):
                K.mm(K.ps_t[:, b0 * 512 + (3 + n_) * 128:b0 * 512 + (4 + n_) * 128], ckT[:, kvh, n_ * 128:(n_ + 1) * 128], qT[:, h, i * 128:(i + 1) * 128], True, True,
                     [b_ck, b_q], [K.pb[b0], K.pb[b0 + 1]])
            Pt, bPt = A.alloc(f"Pt{h}_{i}", 64 * KB_ - 4096 + ((h * 8 + i) % 2) * 2048, [128, 7 * 128], BF16)
            K.act(Pt, K.ps_t[:, b0 * 512:b0 * 512 + 896], AF.Exp, [K.pb[b0], K.pb[b0 + 1]], [bPt], scale=scale)
            K.tt("dve", Pt, Pt, masks[:, i, :], ALU.mult, [bPt, b_mk], [bPt])
            for n_ in range(7):
                if n_ < 3:
                    vv = v_sb[:, kbs[n_], kvh * 128:(kvh + 1) * 128]
                    vb = b_v
                else:
                    vv = cv_sb[:, n_ - 3, kvh * 128:(kvh + 1) * 128]
                    vb = b_cv
                K.mm(K.ps_t[:, b0 * 512:b0 * 512 + 128], vv, Pt[:, n_ * 128:(n_ + 1) * 128], n_ == 0, n_ == 6, [vb, bPt], [K.pb[b0], K.pb[b0 + 1]])
            for n_ in range(7):
                K.mm(K.ps_t[:, b0 * 512 + 128:b0 * 512 + 256], E.ones_b, Pt[:, n_ * 128:(n_ + 1) * 128], n_ == 0, n_ == 6, [E.b_ones, bPt], [K.pb[b0], K.pb[b0 + 1]])
            rd, brd = A.alloc(f"rd{h}_{i}", 108 * KB_ + ((h * 8 + i) % 2) * 512, [128, 128], F32)
            K.act(rd, K.ps_t[:, b0 * 512 + 128:b0 * 512 + 256], AF.Ln, [K.pb[b0], K.pb[b0 + 1], E.b_sink], [brd], bias=E.sinkexp[:, h:h + 1])
            K.act(rd, rd, AF.Exp, [brd], [brd], scale=-1.0)
            K.tt("dve", attnT[:, h, i * 128:(i + 1) * 128], K.ps_t[:, b0 * 512:b0 * 512 + 128], rd, ALU.mult, [K.pb[b0], K.pb[b0 + 1], brd], [b_at])
    env["_y"] = dict(ypool=(ypool, b_yp), yconv=(yconv, b_yc), attnT=(attnT, b_at))


def build_merge_ffn(K, P, l, env):
    E = types.SimpleNamespace(**env)
    A, H = K.A, K.H
    hT, b_h = env["_h"]
    shift1, gate1, shift2, gate2 = env["_mods"]
    xs_b = env["_xs_b"]
    Y = env["_y"]
    S5 = env["_s5"]
    branches = [(Y["ypool"], E.w_br_pool, 4, 0), ((S5["ys5"], S5["b_ys5"]), E.w_br_s5, 4, 1), (Y["attnT"], E.w_br_attn, 8, 2), (Y["yconv"], E.w_br_conv, 4, 3)]
    for _ in env["_mg0"]:
        pass
    K.tt("dve", E.mod[:, 32:96], E.modraw[:, 32:96], E.bada[:, 32:96], ALU.add, [E.b_modraw, E.b_bada], [E.b_mod])
    K.ts("dve", E.s2p, E.mod[:, 64:80], 1.0, None, ALU.add, None, [E.b_mod], [E.b_s2p])
    merged, b_mg = A.alloc("merged", 104 * KB_, [128, KC, NT], BF16)
    for dc in range(KC):
        macc, b_ma = A.alloc(f"macc{dc}", 16 * KB_ + (dc % 2) * 4096, [128, NT], F32)
        for bi, ((yt, yb), wbr, nk, gi) in enumerate(branches):
            wg, wgb = K.wload(E.w_in[l][:, MIXC + gi * D + dc * 128:MIXC + gi * D + (dc + 1) * 128], KC, 128)
            pg, pgb = K.proj_fm(wg, wgb, KC, 0, hT, b_h, bi % 2)
            gt, b_gt = A.alloc(f"gt{dc}_{bi}", (bi % 2) * 2048, [128, NT], BF16)
            K.act(gt, pg, AF.Sigmoid, pgb, [b_gt])
            wb_, wbb = K.wload(wbr[l][:, dc * 128:(dc + 1) * 128], nk, 128)
            pb_, pbb = K.proj_fm(wb_, wbb, nk, 0, yt, yb, 2 + bi % 2)
            if bi == 0:
                K.tt("dve", macc, pb_, gt, ALU.mult, pbb + [b_gt], [b_ma])
            else:
                tm, b_tm = A.alloc(f"mtmp{dc}_{bi}", 8 * KB_ + (bi % 2) * 4096, [128, NT], F32)
                K.tt("dve", tm, pb_, gt, ALU.mult, pbb + [b_gt], [b_tm])
                K.tt("dve", macc, macc, tm, ALU.add, [b_ma, b_tm], [b_ma])
        K.cp("act", merged[:, dc, :], macc, [b_ma], [b_mg])

    def residual_ln(wmat, nk_total, rhs, rhs_b, gate, lng, b_lng, lnb, b_lnb, zlocs, tag, xrloc, lnso=None, next_so=None, lnar=None):
        zs = []
        pend = [None]
        for dc in range(KC):
            ar, off = zlocs[dc]
            za, zb = ar.alloc(f"z{tag}{dc}", off, [128, NT], F32)
            halves = [(0, nk_total)] if nk_total <= 16 else [(0, nk_total // 2), (nk_total // 2, nk_total // 2)]
            for (k0, nk) in halves:
                wt, wb = K.wload(wmat[l][k0 * 128:(k0 + nk) * 128, dc * 128:(dc + 1) * 128], nk, 128)
                pv, pbs = K.proj_fm(wt, wb, nk, 0, rhs, rhs_b, dc % 2, kofs=k0, ktot=nk_total)
            if pend[0] is not None:
                pend[0]()
                pend[0] = None
            K.act(za, pv, AF.Identity, pbs + [E.b_mod], [zb], scale=gate[:, dc:dc + 1], bias=E.zero1[:, 0:1])
            xr, b_xr = xrloc[0].alloc(f"xr{tag}{dc}", xrloc[1] + (dc % 2) * 4096, [128, NT], F32)
            P.dma("sp", xr, E.xs_d[:, dc * NT:(dc + 1) * NT], reads=[xs_b[dc]], writes=[b_xr])
            K.stt(za, xr, ALPHA, za, ALU.mult, ALU.add, [b_xr, zb], [zb])
            zs.append((za, zb))
            if lnso is not None:
                pend[0] = (lambda dc=dc, za=za, zb=zb: K.ln_stats_chunk(dc, KC, za, zb, lnso, lnar))
        if pend[0] is not None:
            pend[0]()
        if lnso is not None:
            R, Rb, MR, MRb = K.ln_stats_finish(D, 64 * KB_)
        else:
            R, Rb, MR, MRb = K.ln_stats(zs, D, 64 * KB_)
        aft = (lambda c, oa, ob: K.ln_stats_chunk(c, KC, oa, ob, next_so)) if next_so is not None else None
        K.ln_apply(zs, zs, R, Rb, MR, MRb, lng, lnb, [b_lng, b_lnb], after=aft)
        return zs

    ck(10)
    zs = residual_ln(E.w_out, KC, merged, b_mg, gate1, E.ln1g, E.b_ln1g, E.ln1b, E.b_ln1b, [(A, dc * 4096) for dc in range(KC)], f"a{l}", (A, 88 * KB_), lnso=64 * KB_, next_so=64 * KB_)
    for dc in range(KC):
        E.xch_b[dc] = zs[dc][1]
    ck(11)
    for c in range(KC):
        P.dma("sp", E.xs_d[:, c * NT:(c + 1) * NT], E.xT[:, c, :], reads=[E.xch_b[c]], writes=[xs_b[c]])
    R, Rb, MR, MRb = K.ln_stats_finish(D, 64 * KB_)
    h2, b_h2 = H.alloc(f"h2{l}", 0, [128, KC, NT], BF16)
    K.ln_apply(E.xchunks(), [(h2[:, c, :], b_h2) for c in range(KC)], R, Rb, MR, MRb, E.s2p, shift2, [E.b_s2p, E.b_mod])
    fdwf, b_fdwf = A.alloc("fdwf", 24 * KB_, [128, 2, NFF], F32)
    K.ts("dve", fdwf[:, 0, :], E.fdw[:, 0, :], E.flag[:, 0:1], None, ALU.mult, None, [E.b_fdw, E.b_flag], [b_fdwf])
    K.ts("dve", fdwf[:, 1, :], E.fdw[:, 2, :], E.flag[:, 0:1], None, ALU.mult, None, [E.b_fdw, E.b_flag], [b_fdwf])
    actT, b_actT = A.alloc("actT", 48 * KB_, [128, 44, NT], BF16)
    actb = [A.alias(f"actb{i}", 48 * KB_ + i * 2048, 2048) for i in range(44)]
    for b_ in actb:
        b_.r = dict(b_actT.r)
    mg_ = mod_gen(K, E, l + 1, 7) if l + 1 < NL else iter(())
    for p_ in range(44):
        next(mg_, None)
        if p_ < 4:
            next(mg_, None)
        us = []
        for which in range(2):
            ch = which * 44 + p_
            wt, wb = K.wload(E.w_up[l][:, ch * 128:(ch + 1) * 128], KC, 128)
            pv, pbs = K.proj_fm(wt, wb, KC, 0, h2, b_h2, which)
            u, b_u = A.alloc(f"u{p_}_{which}", ((p_ % 2) * 2 + which) * 4096, [128, NT], F32)
            K.act(u, pv, AF.Identity, pbs + [E.b_fdw, E.b_fdb], [b_u], scale=E.fdw[:, 1, ch:ch + 1], bias=E.fdb[:, ch:ch + 1])
            uv = u[:, :].rearrange("p (s t) -> p s t", s=4)
            p3 = pv.rearrange("p (s t) -> p s t", s=4)
            K.stt(uv[:, :, 1:256], p3[:, :, 0:255], E.fdw[:, 0, ch:ch + 1], uv[:, :, 1:256], ALU.mult, ALU.add, pbs + [E.b_fdw, b_u], [b_u])
            K.stt(uv[:, :, 0:255], p3[:, :, 1:256], E.fdw[:, 2, ch:ch + 1], uv[:, :, 0:255], ALU.mult, ALU.add, pbs + [E.b_fdw, b_u], [b_u])
            K.stt(uv[:, 1:4, 0], p3[:, 0:3, 255], fdwf[:, 0, ch:ch + 1], uv[:, 1:4, 0], ALU.mult, ALU.add, pbs + [b_fdwf, b_u], [b_u])
            K.stt(uv[:, 0:3, 255], p3[:, 1:4, 0], fdwf[:, 1, ch:ch + 1], uv[:, 0:3, 255], ALU.mult, ALU.add, pbs + [b_fdwf, b_u], [b_u])
            us.append((u, b_u))
        sg, b_sg = A.alloc(f"fsg{p_}", 16 * KB_ + (p_ % 2) * 4096, [128, NT], F32)
        K.act(sg, us[0][0], AF.Silu, [us[0][1]], [b_sg])
        K.tt("dve", actT[:, p_, :], sg, us[1][0], ALU.mult, [b_sg, us[1][1]], [actb[p_]])
    for _ in mg_:
        pass
    for b_ in actb:
        for k, v in b_.w.items():
            if b_actT.w.get(k, 0) < v:
                b_actT.w[k] = v
    zl = [(A, dc * 4096) if dc < 12 else (H, (dc - 12) * 4096) for dc in range(KC)]
    zs = residual_ln(E.w_down, 44, actT, b_actT, gate2, E.ln2g, E.b_ln2g, E.ln2b, E.b_ln2b, zl, f"f{l}", (H, 16 * KB_), lnso=24 * KB_, lnar=H, next_so=(64 * KB_ if l + 1 < NL else None))
    K.ln_pre = (l + 1 < NL)
    for dc in range(12):
        E.xch_b[dc] = zs[dc][1]
    for dc in range(12, KC):
        na, nb = A.alloc(f"xmv{l}{dc}", dc * 4096, [128, NT], F32)
        K.cp("dve" if dc % 2 else "act", na, zs[dc][0], [zs[dc][1]], [nb])
        E.xch_b[dc] = nb


_NC_CACHE = {}


def _consts():
    cst = np.zeros((128, 528), np.float32)
    cst[:, 0:128] = np.eye(128, dtype=np.float32)
    rot = np.zeros((128, 128), np.float32)
    for m in range(128):
        if (m % 64) < 32:
            rot[m + 32, m] = -1.0
        else:
            rot[m - 32, m] = 1.0
    cst[:, 128:256] = rot
    r = np.arange(128)[:, None] // 16
    q = np.arange(128)[None, :] // 16
    cst[:, 256:384] = (q >= r)
    cst[:, 384:512] = (q <= r)
    cst[:, 512:520] = np.arange(8, dtype=np.float32)[None, :]
    cst[:64, 520] = 1.0
    cst[64:, 520] = -1.0
    return cst


def _core_consts(sample):
    n = 1024 if sample else 256
    rc = np.zeros((4, NT), np.float32)
    t = np.arange(NT) % n
    for g, w in enumerate((2, 4, 8, 16)):
        left = w // 2
        right = w - 1 - left
        lo = np.clip(t - left, 0, n)
        hi = np.clip(t + right + 1, 0, n)
        rc[g] = 1.0 / (hi - lo).astype(np.float32)
    rc = np.ascontiguousarray(np.broadcast_to(rc[None], (128, 4, NT)))
    mk = np.zeros((128, 8, 7, 128), np.float32)
    b = np.arange(128)[:, None]
    a = np.arange(128)[None, :]
    for i in range(8):
        if sample:
            if i >= 1:
                mk[:, i, 0, :] = (b >= a)
            mk[:, i, 1, :] = 1.0
            if i <= 6:
                mk[:, i, 2, :] = (b <= a)
            mk[:, i, 3:7, :] = 1.0
        else:
            mk[:, i, 1, :] = 1.0
            if i % 2 == 0:
                mk[:, i, 2, :] = 1.0
            else:
                mk[:, i, 0, :] = 1.0
    mk = mk.reshape(128, 8 * 7 * 128)
    if sample:
        tt = np.arange(NT)
        row, col = tt // 64, tt % 64
        inv = (np.float32(10000.0) ** (-np.arange(32, dtype=np.float32) / np.float32(32))).astype(np.float32)
        cos = np.zeros((128, NT), np.float32)
        sin = np.zeros((128, NT), np.float32)
        for m in range(128):
            pos = (row if m < 64 else col).astype(np.float32)
            ang = (pos * inv[m % 32]).astype(np.float32)
            cos[m] = np.cos(ang)
            sin[m] = np.sin(ang)
    else:
        cos = np.ones((128, NT), np.float32)
        sin = np.zeros((128, NT), np.float32)
    return rc, mk, cos, sin


def kernel(x_prompt, x_sample, cache_k, cache_v, state_s5, c, c_ctx, w_ada, b_ada, w_in,
           pool_w, pool_scale, s5_lambda_re, s5_lambda_im, s5_log_dt, s5_b_re, s5_b_im,
           s5_c_re, s5_c_im, s5_d, s5_w_glu, attn_sink, conv_dw, conv_db, conv_ln_g, conv_ln_b,
           w_br_pool, w_br_s5, w_br_attn, w_br_conv, w_out, ln1_g, ln1_b, ffn_w_up, ffn_dw,
           ffn_db, ffn_w_down, ln2_g, ln2_b):
    f = lambda a: np.ascontiguousarray(np.asarray(a, dtype=np.float32))
    L = NL
    fm = lambda v, n: f(np.asarray(v).reshape(L, n, 128).transpose(0, 2, 1))
    shared = {
        "cst": _consts(),
        "w_ada": f(w_ada), "b_ada_fm": fm(b_ada, 96), "w_in": f(w_in), "pool_w": f(pool_w), "pool_scale_fm": fm(pool_scale, 4),
        "lam_re_dp": f(np.asarray(s5_lambda_re).transpose(0, 1, 3, 2).reshape(L, 128, 32)),
        "lam_im_dp": f(np.asarray(s5_lambda_im).transpose(0, 1, 3, 2).reshape(L, 128, 32)),
        "logdt_dp": f(np.broadcast_to(np.asarray(s5_log_dt)[:, :, None, :], (L, 2, 64, 32)).reshape(L, 128, 32)),
        "b_re_dp": f(np.asarray(s5_b_re).transpose(0, 1, 3, 2, 4).reshape(L, 128, 32, 16)),
        "b_im_dp": f(np.asarray(s5_b_im).transpose(0, 1, 3, 2, 4).reshape(L, 128, 32, 16)),
        "c_re_dp": f(np.asarray(s5_c_re).transpose(0, 1, 4, 2, 3).reshape(L, 128, 32, 16)),
        "c_im_dp": f(np.asarray(s5_c_im).transpose(0, 1, 4, 2, 3).reshape(L, 128, 32, 16)),
        "s5_dcol": f(np.tile(np.asarray(s5_d).reshape(L, 32, 16).transpose(0, 2, 1), (1, 8, 1))),
        "s5_w_glu": f(s5_w_glu),
        "attn_sink_bc": f(np.broadcast_to(np.asarray(attn_sink)[:, None, :], (L, 128, 8))),
        "conv_dw_fm": f(np.asarray(conv_dw).reshape(L, 31, 4, 128).transpose(0, 3, 2, 1)),
        "conv_db_fm": fm(conv_db, 4), "conv_ln_g_fm": fm(conv_ln_g, 4), "conv_ln_b_fm": fm(conv_ln_b, 4),
        "w_br_pool": f(w_br_pool), "w_br_s5": f(w_br_s5), "w_br_attn": f(w_br_attn), "w_br_conv": f(w_br_conv), "w_out": f(w_out),
        "ln1_g_fm": fm(ln1_g, 16), "ln1_b_fm": fm(ln1_b, 16), "ln2_g_fm": fm(ln2_g, 16), "ln2_b_fm": fm(ln2_b, 16),
        "ffn_w_up": f(ffn_w_up), "ffn_dw_fm": f(np.asarray(ffn_dw).reshape(L, 3, NFF, 128).transpose(0, 3, 1, 2)),
        "ffn_db_fm": fm(ffn_db, NFF), "ffn_w_down": f(ffn_w_down),
    }
    pc = _core_consts(False)
    sc = _core_consts(True)
    xp = np.asarray(x_prompt, np.float32)
    xsm = np.asarray(x_sample, np.float32)
    in_maps = []
    for core in range(8):
        sample = core >= 4
        b = core - 4
        rc, mk, cos, sin = sc if sample else pc
        m = dict(shared)
        m["x"] = f(xsm[b]) if sample else f(xp[4 * core:4 * core + 4].reshape(NT, D))
        cv_ = np.asarray(c)[b] if sample else np.asarray(c_ctx)
        m["cvec"] = f(cv_.reshape(16, 128).T)
        m["flag"] = np.full((128, 1), 1.0 if sample else 0.0, np.float32)
        m["pool_rc"] = rc
        m["masks"] = mk
        if sample:
            m["cache_k_c"] = f(np.asarray(cache_k)[b].reshape(L, 512, 256))
            m["cache_v_c"] = f(np.asarray(cache_v)[b].reshape(L, 512, 256))
            m["h0_dp"] = f(np.asarray(state_s5)[b].transpose(0, 1, 4, 2, 3).reshape(L, 128, 2, 32))
        else:
            m["cache_k_c"] = np.zeros((L, 512, 256), np.float32)
            m["cache_v_c"] = np.zeros((L, 512, 256), np.float32)
            m["h0_dp"] = np.zeros((L, 128, 2, 32), np.float32)
        in_maps.append(m)
    if "nc" not in _NC_CACHE:
        _NC_CACHE["nc"] = build()
    ncores = int(os.environ.get("KCORES", "8"))
    res = run_bass_kernel_spmd(_NC_CACHE["nc"], in_maps[:ncores], core_ids=list(range(ncores)))
    r = list(res.results) + [res.results[0]] * (8 - ncores)
    y_prompt = np.concatenate([r[i]["y"].reshape(4, 256, D) for i in range(4)], axis=0).astype(np.float32)
    y_sample = np.stack([r[4 + i]["y"] for i in range(4)], axis=0).astype(np.float32)
    nk = np.concatenate([r[i]["kout"].reshape(L, 4, 256, 2, 128).transpose(1, 0, 2, 3, 4) for i in range(4)], axis=0).astype(np.float32)
    nv = np.concatenate([r[i]["vout"].reshape(L, 4, 256, 2, 128).transpose(1, 0, 2, 3, 4) for i in range(4)], axis=0).astype(np.float32)
    ns = np.concatenate([r[i]["sout"] for i in range(4)], axis=0).astype(np.float32)
    return (y_prompt, y_sample, np.ascontiguousarray(nk), np.ascontiguousarray(nv), ns)
```

```python
import math
from contextlib import ExitStack
import numpy as np
import concourse.bass as bass
import concourse.mybir as mybir
from concourse.bass_utils import run_bass_kernel_spmd

F32 = mybir.dt.float32
BF16 = mybir.dt.bfloat16
I32 = mybir.dt.int32
AF = mybir.ActivationFunctionType
ALU = mybir.AluOpType

D = 2048
KC = 16
NT = 1024
NL = 2
DFF = 5632
NFF = 88
ALPHA = (2.0 * NL) ** 0.25
EPS = 1e-5
MIXC = 3584
N_IN = 11776
ARENA_F32 = 40960


class Buf:
    __slots__ = ("name", "w", "r", "excl")

    def __init__(self, name):
        self.name = name
        self.w = {}
        self.r = {}
        self.excl = False


class Prog:
    COMPUTE = ("pe", "act", "dve", "pool")
    NSLOT = 12

    def __init__(self, nc):
        self.nc = nc
        self.q = {e: [] for e in ("pe", "act", "dve", "pool", "sp")}
        self.cnt = {e: 0 for e in self.COMPUTE}
        self.known = {e: {} for e in self.q}
        self.slot_cnt = {}
        self.slot_rr = {"sp": 0, "pool": 0}
        self.nbuf = 0

    def buf(self, name=None):
        self.nbuf += 1
        return Buf(name or f"b{self.nbuf}")

    def bufs(self, n, name="b"):
        return [self.buf(f"{name}{i}") for i in range(n)]

    def _deps(self, reads, writes):
        deps = {}

        def add(d):
            for k, v in d.items():
                if deps.get(k, 0) < v:
                    deps[k] = v
        for b in reads:
            add(b.w)
            if b.excl:
                add(b.r)
        for b in writes:
            add(b.w)
            add(b.r)
        return deps

    def _filter(self, eng, deps):
        waits = []
        kn = self.known[eng]
        for k, v in deps.items():
            if k == eng and eng == "pe":
                continue
            if kn.get(k, 0) >= v:
                continue
            kn[k] = v
            waits.append((k, v))
        return waits

    def _commit(self, reads, writes, key, val):
        for b in reads:
            if b.r.get(key, 0) < val:
                b.r[key] = val
        for b in writes:
            b.w = {key: val}
            b.r = {}

    def op(self, eng, emit, reads=(), writes=()):
        deps = self._deps(reads, writes)
        waits = self._filter(eng, deps)
        self.cnt[eng] += 1
        val = self.cnt[eng]
        self.q[eng].append((waits, emit, eng, 1))
        self._commit(reads, writes, eng, val)

    def dma(self, queue, out, in_, reads=(), writes=(), **kw):
        deps = self._deps(reads, writes)
        slot = self.slot_rr[queue]
        self.slot_rr[queue] = (slot + 1) % self.NSLOT
        key = f"d_{queue}_{slot}"
        n = self.slot_cnt.get(key, 0)
        if n > 0 and deps.get(key, 0) < 16 * n:
            deps[key] = 16 * n
        self.slot_cnt[key] = n + 1
        val = 16 * (n + 1)
        waits = self._filter(queue, deps)

        def emit(e, out=out, in_=in_, kw=kw):
            return e.dma_start(out=out, in_=in_, **kw)
        self.q[queue].append((waits, emit, key, 16))
        self._commit(reads, writes, key, val)

    def mark(self, queue):
        m = [None, None, None, 0]
        self.q[queue].append(m)
        return m

    def dma_hoisted(self, queue, key, out, in_, after_marker, reads=(), writes=(), **kw):
        deps = self._deps(reads, writes)
        n = self.slot_cnt.get(key, 0)
        if n > 0 and deps.get(key, 0) < 16 * n:
            deps[key] = 16 * n
        self.slot_cnt[key] = n + 1
        val = 16 * (n + 1)
        kn = self.known[queue]
        waits = []
        for k, v in deps.items():
            waits.append((k, v))
            if kn.get(k, 0) < v:
                kn[k] = v

        def emit(e, out=out, in_=in_, kw=kw):
            return e.dma_start(out=out, in_=in_, **kw)
        entry = (waits, emit, key, 16)
        if after_marker is None:
            self.q[queue].append(entry)
        else:
            qq = self.q[queue]
            idx = next(i for i, x in enumerate(qq) if x is after_marker)
            qq.insert(idx + 1, entry)
        self._commit(reads, writes, key, val)

    def finish(self):
        deps = {}
        for key, n in self.slot_cnt.items():
            deps[key] = 16 * n
        for e in self.COMPUTE:
            if self.cnt[e]:
                deps[e] = self.cnt[e]
        waits = self._filter("sp", deps)
        self.q["sp"].append((waits, None, None, 0))

    def emit(self):
        nc = self.nc
        keys = list(self.COMPUTE) + sorted(self.slot_cnt.keys())
        with ExitStack() as st:
            st.enter_context(nc.allow_non_contiguous_dma(reason="small per-partition parameter columns"))
            sems = {k: st.enter_context(nc.semaphore(k)) for k in keys}
            block = st.enter_context(nc.Block())

            def body_for(name):
                items = self.q[name]

                def body(e):
                    for waits, emit, key, inc in items:
                        if waits is None:
                            continue
                        for k, v in waits:
                            e.wait_ge(sems[k], v)
                        if emit is None:
                            continue
                        inst = emit(e)
                        inst.then_inc(sems[key], inc)
                return body

            block.tensor(body_for("pe"))
            block.scalar(body_for("act"))
            block.vector(body_for("dve"))
            block.gpsimd(body_for("pool"))
            block.sync(body_for("sp"))


class Arena:
    def __init__(self, P, tensor, nbytes):
        self.P = P
        self.t = tensor
        self.nbytes = nbytes
        self.live = []

    def alloc(self, name, off, shape, dt=F32):
        esz = 2 if dt == BF16 else 4
        n = 1
        for s in shape[1:]:
            n *= s
        nbytes = n * esz
        assert off % 4 == 0 and nbytes % 4 == 0, (name, off, nbytes)
        assert off + nbytes <= self.nbytes, (name, off, nbytes)
        lo, hi = off, off + nbytes
        b = self.P.buf(name)
        keep = []
        for (l2, h2, b2) in self.live:
            if l2 < hi and lo < h2:
                for src in (b2.w, b2.r):
                    for k, v in src.items():
                        if b.r.get(k, 0) < v:
                            b.r[k] = v
                if l2 < lo:
                    keep.append((l2, lo, b2))
                if hi < h2:
                    keep.append((hi, h2, b2))
            else:
                keep.append((l2, h2, b2))
        keep.append((lo, hi, b))
        self.live = keep
        ap = self.t[0:shape[0], off // 4:(off + nbytes) // 4]
        if dt != F32:
            ap = ap.bitcast(dt)
        if len(shape) == 3:
            ap = ap.rearrange("p (a b) -> p a b", a=shape[1])
        elif len(shape) == 4:
            ap = ap.rearrange("p (a b c) -> p a b c", a=shape[1], b=shape[2])
        return ap, b

    def alias(self, name, off, nbytes):
        b = self.P.buf(name)
        self.live.append((off, off + nbytes, b))
        return b


KB_ = 1024
import os


class StopBuild(Exception):
    pass


def ck(n):
    if int(os.environ.get('KSTOP', '0')) == n:
        raise StopBuild()


class KBld:
    def __init__(self, nc, P, st, dbg=False):
        self.nc, self.P, self.st, self.dbg = nc, P, st, dbg
        self.din = {}
        self.ar_t = st.enter_context(nc.sbuf_tensor("arena", [128, ARENA_F32], F32))
        self.hb_t = st.enter_context(nc.sbuf_tensor("hbuf", [128, 8192], F32))
        self.cs_t = st.enter_context(nc.sbuf_tensor("cstsb", [128, 2048], F32))
        self.ps_t = st.enter_context(nc.psum_tensor("ps", [128, 4096], F32))
        self.A = Arena(P, self.ar_t, ARENA_F32 * 4)
        self.H = Arena(P, self.hb_t, 32768)
        self.C = Arena(P, self.cs_t, 8192)
        self.pb = P.bufs(8, "psb")
        for b_ in self.pb:
            b_.excl = True
        self.wi = 0
        self.wmarks = []
        self.ev = 0

    def inp(self, name, shape):
        ap = self.nc.dram_tensor(name, list(shape), F32, kind="ExternalInput").ap()
        self.din[name] = ap
        return ap

    def outp(self, name, shape):
        return self.nc.dram_tensor(name, list(shape), F32, kind="ExternalOutput").ap()

    def bank(self, i, n=512, off=0):
        return self.ps_t[:, i * 512 + off:i * 512 + off + n]

    def bank_bf(self, i):
        return self.ps_t[:, i * 512:(i + 1) * 512].bitcast(BF16)

    def act(self, out, in_, func, reads, writes, bias=None, scale=None):
        kw = {}
        if bias is not None:
            kw["bias"] = bias
        if scale is not None:
            kw["scale"] = scale
        self.P.op("act", lambda e: e.activation(out=out, in_=in_, func=func, **kw), reads, writes)

    def tt(self, eng, out, a, b, op, reads, writes):
        self.P.op(eng, lambda e: e.tensor_tensor(out=out, in0=a, in1=b, op=op), reads, writes)

    def ts(self, eng, out, a, s1, s2, op0, op1, reads, writes):
        if op1 is None:
            self.P.op(eng, lambda e: e.tensor_scalar(out=out, in0=a, scalar1=s1, scalar2=None, op0=op0), reads, writes)
        else:
            self.P.op(eng, lambda e: e.tensor_scalar(out=out, in0=a, scalar1=s1, scalar2=s2, op0=op0, op1=op1), reads, writes)

    def stt(self, out, in0, scalar, in1, op0, op1, reads, writes):
        self.P.op("dve", lambda e: e.scalar_tensor_tensor(out=out, in0=in0, scalar=scalar, in1=in1, op0=op0, op1=op1), reads, writes)

    def cp(self, eng, out, in_, reads, writes):
        if eng == "act":
            self.P.op("act", lambda e: e.activation(out=out, in_=in_, func=AF.Identity), reads, writes)
        else:
            self.P.op(eng, lambda e: e.tensor_copy(out=out, in_=in_), reads, writes)

    def evac(self, out, in_, reads, writes):
        self.ev += 1
        self.cp("act" if self.ev % 2 else "dve", out, in_, reads, writes)

    def mm(self, out, lhsT, rhs, start, stop, reads, writes):
        self.P.op("pe", lambda e: e.matmul(out, lhsT=lhsT, rhs=rhs, start=start, stop=stop), reads, writes)

    def tr(self, out, in_, ident, reads, writes):
        self.P.op("pe", lambda e: e.transpose(out, in_, ident), reads, writes)

    def memset(self, eng, ap, val, writes):
        self.P.op(eng, lambda e: e.memset(ap, val), (), writes)

    def wload(self, src, nk, ncols):
        assert nk * ncols <= 4096
        i = self.wi % 3
        self.wi += 1
        ap, b = self.A.alloc(f"w{self.wi}", (136 + 8 * i) * KB_, [128, nk, ncols], BF16)
        mk = self.P.mark("pool")
        self.wmarks.append(mk)
        tgt = self.wmarks[-3] if len(self.wmarks) >= 3 else None
        self.P.dma_hoisted("pool", f"d_w_{i}", ap, src.rearrange("(kc p) n -> p kc n", p=128), tgt, writes=[b])
        return ap, b

    def ln_stats_chunk(self, c, nch, xa, xb_, so, arena=None):
        K = self
        AR = arena if arena is not None else K.A
        ones = self.ones_b
        s = c % 2
        cb, cbb = AR.alloc(f"lnc{c}", so + s * 2048, [128, 1024], BF16)
        sq, sqb = AR.alloc(f"lnq{c}", so + 4096 + s * 2048, [128, 1024], BF16)
        K.cp("dve", cb, xa, [xb_], [cbb])
        K.act(sq, xa, AF.Square, [xb_], [sqb])
        for tt in range(2):
            K.mm(K.bank(4 + tt), ones, cb[:, tt * 512:(tt + 1) * 512], c == 0, c == nch - 1, [cbb, K.cb_ones], [K.pb[4 + tt]])
            K.mm(K.bank(6 + tt), ones, sq[:, tt * 512:(tt + 1) * 512], c == 0, c == nch - 1, [sqb, K.cb_ones], [K.pb[6 + tt]])

    def ln_stats_finish(self, Dn, so):
        K = self
        R, Rb = K.A.alloc("lnR", so + 8192, [128, 1024], F32)
        MR, MRb = K.A.alloc("lnMR", so + 12288, [128, 1024], F32)
        T, Tb = K.A.alloc("lnT", so + 16384, [128, 1024], F32)
        inv = 1.0 / Dn
        K.ts("dve", MR, K.ps_t[:, 4 * 512:6 * 512], inv, None, ALU.mult, None, [K.pb[4], K.pb[5]], [MRb])
        K.act(R, K.ps_t[:, 6 * 512:8 * 512], AF.Identity, [K.pb[6], K.pb[7]], [Rb], scale=inv)
        K.tt("dve", T, MR, MR, ALU.mult, [MRb], [Tb])
        K.tt("dve", R, R, T, ALU.subtract, [Rb, Tb], [Rb])
        K.act(R, R, AF.Ln, [Rb, K.cb_eps], [Rb], bias=K.epsc[:, 0:1])
        K.act(R, R, AF.Exp, [Rb], [Rb], scale=-0.5)
        K.stt(MR, MR, -1.0, R, ALU.mult, ALU.mult, [MRb, Rb], [MRb])
        return R, Rb, MR, MRb

    def ln_stats(self, xs, Dn, so):
        for c, (xa, xb_) in enumerate(xs):
            self.ln_stats_chunk(c, len(xs), xa, xb_, so)
        return self.ln_stats_finish(Dn, so)

    def ln_apply(self, xs, outs, R, Rb, MR, MRb, scol, bcol, pb_, func=AF.Identity, after=None):
        K = self
        for c, ((xa, xb_), (oa, ob_)) in enumerate(zip(xs, outs)):
            K.tt("dve", xa, xa, R, ALU.mult, [xb_, Rb], [xb_])
            K.tt("pool" if c % 2 else "dve", xa, xa, MR, ALU.add, [xb_, MRb], [xb_])
            rd = [xb_] + list(pb_)
            K.act(oa, xa, func, rd, [ob_] if ob_ is not xb_ else [xb_], bias=bcol[:, c:c + 1], scale=scol[:, c:c + 1])
            if after is not None:
                after(c, oa, ob_)

    def proj_fm(self, wt, wb, nk, mo, rhs, rhs_b, pair, kofs=0, ktot=None, mw=128):
        K = self
        ktot = ktot or nk
        b0 = pair * 2
        for tt in range(2):
            for kc in range(nk):
                K.mm(K.ps_t[0:mw, (b0 + tt) * 512:(b0 + tt + 1) * 512], wt[:, kc, mo:mo + mw], rhs[:, kofs + kc, tt * 512:(tt + 1) * 512],
                     kofs + kc == 0, kofs + kc == ktot - 1, [wb, rhs_b], [K.pb[b0 + tt]])
        return K.ps_t[0:mw, b0 * 512:(b0 + 2) * 512], [K.pb[b0], K.pb[b0 + 1]]


def build(dbg=False):
    nc = bass.Bass("TRN2", target_bir_lowering=False)
    P = Prog(nc)
    with ExitStack() as st:
        K = KBld(nc, P, st, dbg)
        A, H, C = K.A, K.H, K.C
        x_d = K.inp("x", [NT, D])
        cst_d = K.inp("cst", [128, 528])
        cvec_d = K.inp("cvec", [128, 16])
        flag_d = K.inp("flag", [128, 1])
        w_ada = K.inp("w_ada", [NL, D, 6 * D])
        bada_d = K.inp("b_ada_fm", [NL, 128, 96])
        w_in = K.inp("w_in", [NL, D, N_IN])
        pool_w = K.inp("pool_w", [NL, 4, 128, 128])
        pool_sc_d = K.inp("pool_scale_fm", [NL, 128, 4])
        pool_rc_d = K.inp("pool_rc", [128, 4, NT])
        lam_re_d = K.inp("lam_re_dp", [NL, 128, 32])
        lam_im_d = K.inp("lam_im_dp", [NL, 128, 32])
        logdt_d = K.inp("logdt_dp", [NL, 128, 32])
        bre_d = K.inp("b_re_dp", [NL, 128, 32, 16])
        bim_d = K.inp("b_im_dp", [NL, 128, 32, 16])
        cre_d = K.inp("c_re_dp", [NL, 128, 32, 16])
        cim_d = K.inp("c_im_dp", [NL, 128, 32, 16])
        dcol_d = K.inp("s5_dcol", [NL, 128, 32])
        w_glu = K.inp("s5_w_glu", [NL, 512, 512])
        h0_d = K.inp("h0_dp", [NL, 128, 2, 32])
        sink_d = K.inp("attn_sink_bc", [NL, 128, 8])
        cdw_d = K.inp("conv_dw_fm", [NL, 128, 4, 31])
        cdb_d = K.inp("conv_db_fm", [NL, 128, 4])
        clg_d = K.inp("conv_ln_g_fm", [NL, 128, 4])
        clb_d = K.inp("conv_ln_b_fm", [NL, 128, 4])
        w_br_pool = K.inp("w_br_pool", [NL, 512, D])
        w_br_s5 = K.inp("w_br_s5", [NL, 512, D])
        w_br_attn = K.inp("w_br_attn", [NL, 1024, D])
        w_br_conv = K.inp("w_br_conv", [NL, 512, D])
        w_out = K.inp("w_out", [NL, D, D])
        ln1g_d = K.inp("ln1_g_fm", [NL, 128, 16])
        ln1b_d = K.inp("ln1_b_fm", [NL, 128, 16])
        ln2g_d = K.inp("ln2_g_fm", [NL, 128, 16])
        ln2b_d = K.inp("ln2_b_fm", [NL, 128, 16])
        w_up = K.inp("ffn_w_up", [NL, D, 2 * DFF])
        fdw_d = K.inp("ffn_dw_fm", [NL, 128, 3, NFF])
        fdb_d = K.inp("ffn_db_fm", [NL, 128, NFF])
        w_down = K.inp("ffn_w_down", [NL, DFF, D])
        ck_d = K.inp("cache_k_c", [NL, 512, 256])
        cv_d = K.inp("cache_v_c", [NL, 512, 256])
        masks_d = K.inp("masks", [128, 8 * 7 * 128])
        y_d = K.outp("y", [NT, D])
        kout_d = K.outp("kout", [NL, NT, 256])
        vout_d = K.outp("vout", [NL, NT, 256])
        sout_d = K.outp("sout", [4, NL, 2, 2, 32, 64])
        xs_d = nc.dram_tensor("xstash", [128, KC * NT], F32, kind="Internal").ap()
        xsb = P.buf("xstash")

        ident_f, b_idf = C.alloc("ident_f", 0, [128, 128], F32)
        ident_b, b_idb = C.alloc("ident_b", 512, [128, 128], BF16)
        ones_b, b_ones = C.alloc("ones_b", 768, [128, 128], BF16)
        rotP_b, b_rot = C.alloc("rotP_b", 1024, [128, 128], BF16)
        tm0, b_tm0 = C.alloc("tm0", 1280, [128, 128], F32)
        tm1, b_tm1 = C.alloc("tm1", 1792, [128, 128], F32)
        flag, b_flag = C.alloc("flag", 2304, [128, 1], F32)
        sc_b, b_sc = C.alloc("sc_b", 2320, [128, 16], BF16)
        cvec, b_cvec = C.alloc("cvec", 2352, [128, 16], F32)
        mod, b_mod = C.alloc("mod", 2432, [128, 96], F32)
        bada, b_bada = C.alloc("bada", 2816, [128, 96], F32)
        s1p, b_s1p = C.alloc("s1p", 3200, [128, 16], F32)
        s2p, b_s2p = C.alloc("s2p", 3264, [128, 16], F32)
        ln1g, b_ln1g = C.alloc("ln1g", 3328, [128, 16], F32)
        ln1b, b_ln1b = C.alloc("ln1b", 3392, [128, 16], F32)
        ln2g, b_ln2g = C.alloc("ln2g", 3456, [128, 16], F32)
        ln2b, b_ln2b = C.alloc("ln2b", 3520, [128, 16], F32)
        fdw, b_fdw = C.alloc("fdw", 3584, [128, 3, NFF], F32)
        fdb, b_fdb = C.alloc("fdb", 4640, [128, NFF], F32)
        cdw, b_cdw = C.alloc("cdw", 4992, [128, 4, 31], F32)
        cdb, b_cdb = C.alloc("cdb", 5488, [128, 4], F32)
        clg, b_clg = C.alloc("clg", 5504, [128, 4], F32)
        clb, b_clb = C.alloc("clb", 5520, [128, 4], F32)
        psc, b_psc = C.alloc("psc", 5536, [128, 4], F32)
        sinkexp, b_sink = C.alloc("sinkexp", 5552, [128, 8], F32)
        kio, b_kio = C.alloc("kio", 5600, [128, 8], F32)
        sgn, b_sgn = C.alloc("sgn", 5632, [128, 1], F32)
        gate1a, b_g1a = C.alloc("gate1a", 5648, [128, 16], F32)
        zero1, b_zero1 = C.alloc("zero1", 5712, [128, 1], F32)
        modraw, b_modraw = C.alloc("modraw", 5760, [128, 96], F32)
        K.ones_b, K.cb_ones = ones_b, b_ones
        epsc, b_epsc = C.alloc("epsc", 5728, [128, 1], F32)
        K.memset("dve", epsc, EPS, [b_epsc])
        K.epsc, K.cb_eps = epsc, b_epsc

        P.dma("sp", ident_f, cst_d[:, 0:128], writes=[b_idf])
        P.dma("pool", ident_b, cst_d[:, 0:128], writes=[b_idb])
        P.dma("pool", rotP_b, cst_d[:, 128:256], writes=[b_rot])
        P.dma("sp", tm0, cst_d[:, 256:384], writes=[b_tm0])
        P.dma("sp", tm1, cst_d[:, 384:512], writes=[b_tm1])
        P.dma("sp", kio, cst_d[:, 512:520], writes=[b_kio])
        P.dma("sp", sgn, cst_d[:, 520:521], writes=[b_sgn])
        P.dma("sp", flag, flag_d, writes=[b_flag])
        P.dma("sp", cvec, cvec_d, writes=[b_cvec])
        K.memset("dve", ones_b, 1.0, [b_ones])
        K.memset("dve", zero1, 0.0, [b_zero1])
        K.act(sc_b, cvec, AF.Silu, [b_cvec], [b_sc])

        xT = K.ar_t[:, 0:KC * NT].rearrange("p (a b) -> p a b", a=KC)
        xch_b = [A.alloc(f"xch{c}", c * 4096, [128, NT], F32)[1] for c in range(KC)]
        for tt in range(8):
            stg, b_stg = A.alloc(f"stg{tt}", (64 + 8 * (tt % 2)) * KB_, [128, D], F32)
            P.dma("sp", stg, x_d[tt * 128:(tt + 1) * 128, :], writes=[b_stg])
            for cg in range(4):
                bk = 4 + (tt * 4 + cg) % 4
                for ci in range(4):
                    c = cg * 4 + ci
                    K.tr(K.bank(bk, 128, ci * 128), stg[:, c * 128:(c + 1) * 128], ident_f, [b_stg, b_idf], [K.pb[bk]])
                K.evac(xT[:, cg * 4:(cg + 1) * 4, tt * 128:(tt + 1) * 128],
                       K.bank(bk).rearrange("p (a b) -> p a b", a=4), [K.pb[bk]], [xch_b[cg * 4 + i] for i in range(4)])

        def xchunks():
            return [(xT[:, c, :], xch_b[c]) for c in range(KC)]

        def stash_x():
            for c in range(KC):
                P.dma("sp", xs_d[:, c * NT:(c + 1) * NT], xT[:, c, :], reads=[xch_b[c]], writes=[xsb])

        try:
            ck(1)
            for l in range(NL):
                build_layer(K, P, l, locals())
        except StopBuild:
            P.finish()
            P.emit()
            return nc
        for tt in range(8):
            stg, b_stg = A.alloc(f"ostg{tt}", (64 + 8 * (tt % 2)) * KB_, [128, D], F32)
            for cg in range(4):
                bk = 4 + (tt * 4 + cg) % 4
                for ci in range(4):
                    c = cg * 4 + ci
                    K.tr(K.bank(bk, 128, ci * 128), xT[:, c, tt * 128:(tt + 1) * 128], ident_f, [xch_b[c], b_idf], [K.pb[bk]])
                K.evac(stg[:, cg * 512:(cg + 1) * 512], K.bank(bk), [K.pb[bk]], [b_stg])
            P.dma("sp", y_d[tt * 128:(tt + 1) * 128, :], stg, reads=[b_stg])
        P.finish()
        P.emit()
    return nc


import types


def mod_gen(K, E, l, bk=3):
    for t in range(48):
        wt, wb = K.wload(E.w_ada[l][:, t * 256:(t + 1) * 256], KC, 256)
        for jj in range(2):
            for kc in range(KC):
                K.mm(K.bank(bk, 1, 510 + jj), wt[:, kc, jj * 128:(jj + 1) * 128], E.sc_b[:, kc:kc + 1], kc == 0, kc == KC - 1,
                     [wb, E.b_sc], [K.pb[bk]])
        K.cp("dve", E.modraw[:, 2 * t:2 * t + 2], K.bank(bk, 2, 510), [K.pb[bk]], [E.b_modraw])
        yield


def build_layer(K, P, l, env):
    E = types.SimpleNamespace(**env)
    A, H, C = K.A, K.H, K.C
    PI = math.pi
    for (t, b, src) in [(E.bada, E.b_bada, E.bada_d[l]), (E.ln1g, E.b_ln1g, E.ln1g_d[l]), (E.ln1b, E.b_ln1b, E.ln1b_d[l]),
                        (E.ln2g, E.b_ln2g, E.ln2g_d[l]), (E.ln2b, E.b_ln2b, E.ln2b_d[l]), (E.fdw, E.b_fdw, E.fdw_d[l]),
                        (E.fdb, E.b_fdb, E.fdb_d[l]), (E.cdw, E.b_cdw, E.cdw_d[l]), (E.cdb, E.b_cdb, E.cdb_d[l]),
                        (E.clg, E.b_clg, E.clg_d[l]), (E.clb, E.b_clb, E.clb_d[l]), (E.psc, E.b_psc, E.pool_sc_d[l]),
                        (E.sinkexp, E.b_sink, E.sink_d[l])]:
        P.dma("sp", t, src, writes=[b])
    K.act(E.sinkexp, E.sinkexp, AF.Exp, [E.b_sink], [E.b_sink])

    if l == 0:
        mg0 = mod_gen(K, E, 0)
        for _ in range(16):
            next(mg0)
    else:
        mg0 = iter(())
    env["_mg0"] = mg0
    K.tt("dve", E.mod[:, 0:32], E.modraw[:, 0:32], E.bada[:, 0:32], ALU.add, [E.b_modraw, E.b_bada], [E.b_mod])
    K.ts("dve", E.s1p, E.mod[:, 16:32], 1.0, None, ALU.add, None, [E.b_mod], [E.b_s1p])
    shift1, gate1, shift2, gate2 = E.mod[:, 0:16], E.mod[:, 32:48], E.mod[:, 48:64], E.mod[:, 80:96]
    ck(2)

    xs_b = P.bufs(KC, "xsb")

    def stash_x():
        for c in range(KC):
            P.dma("sp", E.xs_d[:, c * NT:(c + 1) * NT], E.xT[:, c, :], reads=[E.xch_b[c]], writes=[xs_b[c]])

    stash_x()
    if getattr(K, "ln_pre", False):
        R, Rb, MR, MRb = K.ln_stats_finish(D, 64 * KB_)
    else:
        R, Rb, MR, MRb = K.ln_stats(E.xchunks(), D, 64 * KB_)
    hT, b_h = H.alloc(f"h{l}", 0, [128, KC, NT], BF16)
    K.ln_apply(E.xchunks(), [(hT[:, c, :], b_h) for c in range(KC)], R, Rb, MR, MRb, E.s1p, shift1, [E.b_s1p, E.b_mod])

    ck(3)
    ys5, b_ys5 = A.alloc("ys5", 80 * KB_, [128, 4, NT], BF16)
    smp = [40 * KB_]
    smL = [56 * KB_]
    LONG = ("LL", "LI", "LLk", "LIk", "EF", "EFi", "sp_re", "sp_im", "h0t")

    def sm(name, shape):
        n = 4
        for s_ in shape[1:]:
            n *= s_
        if name in LONG:
            ap, b = A.alloc(name, smL[0], shape, F32)
            smL[0] += n
            assert smL[0] <= 60 * KB_
            return ap, b
        if smp[0] < 56 * KB_ and smp[0] + n > 56 * KB_:
            smp[0] = 60 * KB_
        ap, b = A.alloc(name, smp[0], shape, F32)
        smp[0] += n
        return ap, b

    lre, b_lre = sm("lre", [128, 32]); lim, b_lim = sm("lim", [128, 32]); dtt, b_dt = sm("dtt", [128, 32])
    lar, b_lar = sm("lar", [128, 32]); lai, b_lai = sm("lai", [128, 32])
    lrs, b_lrs = sm("lrs", [128, 32]); lis, b_lis = sm("lis", [128, 32])
    Bre, b_Bre = sm("Bre", [128, 32, 16]); Bim, b_Bim = sm("Bim", [128, 32, 16])
    Cre, b_Cre = sm("Cre", [128, 32, 16]); Cim, b_Cim = sm("Cim", [128, 32, 16])
    BBre, b_BBre = sm("BBre", [128, 32, 16]); BBim, b_BBim = sm("BBim", [128, 32, 16])
    tb1, b_tb1 = sm("tb1", [128, 32, 16]); tb2, b_tb2 = sm("tb2", [128, 32, 16])
    dcol, b_dcol = sm("dcol", [128, 32]); h0t, b_h0 = sm("h0t", [128, 2, 32])
    ang, b_ang = sm("ang", [128, 32, 8]); mga, b_mga = sm("mga", [128, 32, 8])
    sn, b_sn = sm("sn", [128, 32, 8]); cs, b_cs = sm("cs", [128, 32, 8])
    mg, b_mg = sm("mg", [128, 32, 8]); mgi, b_mgi = sm("mgi", [128, 32, 8])
    PYre, b_PYre = sm("PYre", [128, 32, 8]); PYim, b_PYim = sm("PYim", [128, 32, 8])
    PXre, b_PXre = sm("PXre", [128, 32, 8]); PXim, b_PXim = sm("PXim", [128, 32, 8])
    kq, b_kq = sm("kq", [128, 4]); sp_re, b_spre = sm("sp_re", [128, 32, 4]); sp_im, b_spim = sm("sp_im", [128, 32, 4])
    cfr, b_cfr = sm("cfr", [128, 32]); cfi, b_cfi = sm("cfi", [128, 32]); den, b_den = sm("den", [128, 32])
    tq1, b_tq1 = sm("tq1", [128, 32]); tq2, b_tq2 = sm("tq2", [128, 32])
    LL, b_LL = sm("LL", [128, 32, 2]); LI, b_LI = sm("LI", [128, 32, 2])
    LLk, b_LLk = sm("LLk", [128, 32, 2]); LIk, b_LIk = sm("LIk", [128, 32, 2])
    EF, b_EF = sm("EF", [128, 32, 2]); EFi, b_EFi = sm("EFi", [128, 32, 2])
    pipi, b_pipi = sm("pipi", [128, 1])
    assert smp[0] <= 76 * KB_, smp[0]
    for (t, b, src) in [(lar, b_lar, E.lam_re_d[l]), (lai, b_lai, E.lam_im_d[l]), (dtt, b_dt, E.logdt_d[l]),
                        (Bre, b_Bre, E.bre_d[l]), (Bim, b_Bim, E.bim_d[l]), (Cre, b_Cre, E.cre_d[l]), (Cim, b_Cim, E.cim_d[l]),
                        (dcol, b_dcol, E.dcol_d[l]), (h0t, b_h0, E.h0_d[l])]:
        P.dma("sp", t, src, writes=[b])
    K.memset("pool", pipi, -PI, [b_pipi])
    K.act(dtt, dtt, AF.Exp, [b_dt], [b_dt])
    K.tt("dve", lre, lar, dtt, ALU.mult, [b_lar, b_dt], [b_lre])
    K.tt("dve", lim, lai, dtt, ALU.mult, [b_lai, b_dt], [b_lim])
    K.ts("dve", lrs, lre, E.sgn[:, 0:1], None, ALU.mult, None, [b_lre, E.b_sgn], [b_lrs])
    K.ts("dve", lis, lim, E.sgn[:, 0:1], None, ALU.mult, None, [b_lim, E.b_sgn], [b_lis])

    def cexp(ang_t, b_a, mga_t, b_m, sn_t, b_s, cs_t, b_c, mg_t, b_g, shape, toff):
        n = 4
        for s_ in shape[1:]:
            n *= s_
        ni, b_ni = A.alloc("cx_ni", toff, shape, I32)
        nf, b_nf = A.alloc("cx_nf", toff + n, shape, F32)
        mm_, b_mm = A.alloc("cx_m", toff + 2 * n, shape, F32)
        for (dst, bd, off) in ((sn_t, b_s, 32.0), (cs_t, b_c, 32.25)):
            K.ts("dve", dst, ang_t, 1.0 / (2.0 * PI), off, ALU.mult, ALU.add, [b_a], [bd])
            K.cp("dve", ni, dst, [bd], [b_ni])
            K.cp("dve", nf, ni, [b_ni], [b_nf])
            K.tt("dve", dst, dst, nf, ALU.subtract, [bd, b_nf], [bd])
            K.ts("dve", mm_, dst, 0.5, None, ALU.is_gt, None, [bd], [b_mm])
            K.tt("dve", dst, dst, mm_, ALU.subtract, [bd, b_mm], [bd])
            K.ts("dve", mm_, dst, -0.5, None, ALU.is_lt, None, [bd], [b_mm])
            K.tt("dve", dst, dst, mm_, ALU.add, [bd, b_mm], [bd])
            K.act(dst, dst, AF.Sin, [bd], [bd], scale=2.0 * PI)
        K.act(mg_t, mga_t, AF.Exp, [b_m], [b_g])

    kio_b = E.kio[:, :].unsqueeze(1).broadcast_to([128, 32, 8])
    K.tt("dve", ang, lis[:, :].unsqueeze(2).broadcast_to([128, 32, 8]), kio_b, ALU.mult, [b_lis, E.b_kio], [b_ang])
    K.tt("dve", mga, lrs[:, :].unsqueeze(2).broadcast_to([128, 32, 8]), kio_b, ALU.mult, [b_lrs, E.b_kio], [b_mga])
    cexp(ang, b_ang, mga, b_mga, sn, b_sn, cs, b_cs, mg, b_mg, [128, 32, 8], 132 * KB_)
    ck(41)
    K.P.op("dve", lambda e: e.reciprocal(out=mgi, in_=mg), [b_mg], [b_mgi])
    K.tt("dve", PYre, mg, cs, ALU.mult, [b_mg, b_cs], [b_PYre])
    K.tt("dve", PYim, mg, sn, ALU.mult, [b_mg, b_sn], [b_PYim])
    K.tt("dve", PXre, mgi, cs, ALU.mult, [b_mgi, b_cs], [b_PXre])
    K.stt(PXim, mgi, -1.0, sn, ALU.mult, ALU.mult, [b_mgi, b_sn], [b_PXim])
    ck(42)
    K.memset("pool", kq[:, 0:1], 8.0, [b_kq])
    K.memset("pool", kq[:, 1:2], 1.0, [b_kq])
    K.ts("dve", kq[:, 2:3], E.sgn[:, 0:1], -3.5, 4.5, ALU.mult, ALU.add, [E.b_sgn, b_kq], [b_kq])
    K.ts("dve", kq[:, 3:4], E.sgn[:, 0:1], 3.5, 3.5, ALU.mult, ALU.add, [E.b_sgn, b_kq], [b_kq])
    ang4, b_ang4 = A.alloc("ang4", 128 * KB_ + 0, [128, 32, 4], F32)
    mga4, b_mga4 = A.alloc("mga4", 128 * KB_ + 512, [128, 32, 4], F32)
    sn4, b_sn4 = A.alloc("sn4", 128 * KB_ + 1024, [128, 32, 4], F32)
    cs4, b_cs4 = A.alloc("cs4", 128 * KB_ + 1536, [128, 32, 4], F32)
    mg4, b_mg4 = A.alloc("mg4", 128 * KB_ + 2048, [128, 32, 4], F32)
    kq_b = kq[:, :].unsqueeze(1).broadcast_to([128, 32, 4])
    K.tt("dve", ang4, lim[:, :].unsqueeze(2).broadcast_to([128, 32, 4]), kq_b, ALU.mult, [b_lim, b_kq], [b_ang4])
    K.tt("dve", mga4, lre[:, :].unsqueeze(2).broadcast_to([128, 32, 4]), kq_b, ALU.mult, [b_lre, b_kq], [b_mga4])
    cexp(ang4, b_ang4, mga4, b_mga4, sn4, b_sn4, cs4, b_cs4, mg4, b_mg4, [128, 32, 4], 132 * KB_)
    K.tt("dve", sp_re, mg4, cs4, ALU.mult, [b_mg4, b_cs4], [b_spre])
    K.tt("dve", sp_im, mg4, sn4, ALU.mult, [b_mg4, b_sn4], [b_spim])
    ck(43)
    K.ts("dve", tq1, sp_re[:, :, 1], -1.0, None, ALU.add, None, [b_spre], [b_tq1])
    K.tt("dve", den, lar, lar, ALU.mult, [b_lar], [b_den])
    K.tt("dve", tq2, lai, lai, ALU.mult, [b_lai], [b_tq2])
    K.tt("dve", den, den, tq2, ALU.add, [b_den, b_tq2], [b_den])
    K.P.op("dve", lambda e: e.reciprocal(out=den, in_=den), [b_den], [b_den])
    K.tt("dve", cfr, tq1, lar, ALU.mult, [b_tq1, b_lar], [b_cfr])
    K.tt("dve", tq2, sp_im[:, :, 1], lai, ALU.mult, [b_spim, b_lai], [b_tq2])
    K.tt("dve", cfr, cfr, tq2, ALU.add, [b_cfr, b_tq2], [b_cfr])
    K.tt("dve", cfr, cfr, den, ALU.mult, [b_cfr, b_den], [b_cfr])
    K.tt("dve", cfi, sp_im[:, :, 1], lar, ALU.mult, [b_spim, b_lar], [b_cfi])
    K.tt("dve", tq2, tq1, lai, ALU.mult, [b_tq1, b_lai], [b_tq2])
    K.tt("dve", cfi, cfi, tq2, ALU.subtract, [b_cfi, b_tq2], [b_cfi])
    K.tt("dve", cfi, cfi, den, ALU.mult, [b_cfi, b_den], [b_cfi])
    cfr_b = cfr[:, :].unsqueeze(2).broadcast_to([128, 32, 16])
    cfi_b = cfi[:, :].unsqueeze(2).broadcast_to([128, 32, 16])
    K.tt("dve", tb1, Bre, cfr_b, ALU.mult, [b_Bre, b_cfr], [b_tb1])
    K.tt("dve", tb2, Bim, cfi_b, ALU.mult, [b_Bim, b_cfi], [b_tb2])
    K.tt("dve", BBre, tb1, tb2, ALU.subtract, [b_tb1, b_tb2], [b_BBre])
    K.tt("dve", tb1, Bim, cfr_b, ALU.mult, [b_Bim, b_cfr], [b_tb1])
    K.tt("dve", tb2, Bre, cfi_b, ALU.mult, [b_Bre, b_cfi], [b_tb2])
    K.tt("dve", BBim, tb1, tb2, ALU.add, [b_tb1, b_tb2], [b_BBim])
    ck(44)
    K.cp("dve", LL[:, :, 0], sp_re[:, :, 0], [b_spre], [b_LL])
    K.cp("dve", LL[:, :, 1], sp_re[:, :, 0], [b_spre], [b_LL])
    K.ts("dve", LI[:, :, 0], sp_im[:, :, 0], -1.0, None, ALU.mult, None, [b_spim], [b_LI])
    K.cp("dve", LI[:, :, 1], sp_im[:, :, 0], [b_spim], [b_LI])
    K.ts("dve", LLk, LL, E.flag[:, 0:1], None, ALU.mult, None, [b_LL, E.b_flag], [b_LLk])
    K.ts("dve", LIk, LI, E.flag[:, 0:1], None, ALU.mult, None, [b_LI, E.b_flag], [b_LIk])
    K.cp("dve", EF[:, :, 0], sp_re[:, :, 3], [b_spre], [b_EF])
    K.cp("dve", EF[:, :, 1], sp_re[:, :, 3], [b_spre], [b_EF])
    K.ts("dve", EFi[:, :, 0], sp_im[:, :, 3], -1.0, None, ALU.mult, None, [b_spim], [b_EFi])
    K.cp("dve", EFi[:, :, 1], sp_im[:, :, 3], [b_spim], [b_EFi])

    ck(45)
    Yb, b_Yb = A.alloc("Yb", 0, [128, 32, 2, 128], BF16)
    XTb, b_XTb = A.alloc("XTb", 16 * KB_, [128, 32, 2, 128], BF16)
    Tb, b_Tb = A.alloc("Tb", 32 * KB_, [128, 32, 128], BF16)
    Xb, b_Xb = A.alloc("Xb", 88 * KB_, [128, 32, 2, 128], BF16)
    t1, b_t1 = A.alloc("t1x", 72 * KB_ + 4096, [128, 8, 8, 16], F32)
    t2, b_t2 = A.alloc("t2x", 124 * KB_, [128, 8, 8, 16], F32)

    def outer(dst, Pr, bPr, Pi, bPi, Vr, bVr, Vi, bVi, g0, sign_im):
        gs = slice(g0, g0 + 8)
        pr = Pr[:, gs, :].unsqueeze(3).broadcast_to([128, 8, 8, 16])
        pi = Pi[:, gs, :].unsqueeze(3).broadcast_to([128, 8, 8, 16])
        vr = Vr[:, gs, :].unsqueeze(2).broadcast_to([128, 8, 8, 16])
        vi = Vi[:, gs, :].unsqueeze(2).broadcast_to([128, 8, 8, 16])
        d0 = dst[:, gs, 0, :].rearrange("p g (s c) -> p g s c", s=8)
        d1 = dst[:, gs, 1, :].rearrange("p g (s c) -> p g s c", s=8)
        K.tt("dve", t1, pr, vr, ALU.mult, [bPr, bVr], [b_t1])
        K.tt("dve", t2, pi, vi, ALU.mult, [bPi, bVi], [b_t2])
        K.tt("dve", d0, t1, t2, ALU.subtract, [b_t1, b_t2], [dst_b[0]])
        K.tt("dve", t1, pr, vi, ALU.mult, [bPr, bVi], [b_t1])
        K.tt("dve", t2, pi, vr, ALU.mult, [bPi, bVr], [b_t2])
        if sign_im > 0:
            K.tt("dve", d1, t1, t2, ALU.add, [b_t1, b_t2], [dst_b[0]])
        else:
            K.stt(d1, t1, -1.0, t2, ALU.mult, ALU.subtract, [b_t1, b_t2], [dst_b[0]])

    dst_b = [b_Xb]
    for g0 in range(0, 32, 8):
        outer(Xb, PXre, b_PXre, PXim, b_PXim, BBre, b_BBre, BBim, b_BBim, g0, +1)
    dst_b = [b_Yb]
    for g0 in range(0, 32, 8):
        outer(Yb, PYre, b_PYre, PYim, b_PYim, Cre, b_Cre, Cim, b_Cim, g0, -1)
    ck(46)
    ta, b_ta = A.alloc("ta", 72 * KB_ + 4096, [128, 128], F32)
    tb_, b_tb = A.alloc("tbb", 72 * KB_ + 4096 + 512, [128, 128], F32)
    tmd, b_tmd = A.alloc("tmd", 72 * KB_ + 4096 + 1024, [128, 128], F32)
    K.tt("dve", tmd, E.tm0, E.tm1, ALU.subtract, [E.b_tm0, E.b_tm1], [b_tmd])
    Xz, b_Xz = A.alloc("Xz0", 104 * KB_, [128, 32, 2, 128], BF16)
    K.memset("dve", Xz, 0.0, [b_Xz])
    K.cp("dve", Xz[0:64], Xb[0:64], [b_Xb], [b_Xz])
    for g in range(32):
        bk = 4 + g % 4
        for r in range(2):
            K.mm(K.bank(bk, 128, 0), Xz[:, g, r, :], Yb[:, g, r, :], r == 0, r == 1, [b_Xz, b_Yb], [K.pb[bk]])
        for r in range(2):
            K.mm(K.bank(bk, 128, 128), Xb[:, g, r, :], Yb[:, g, r, :], r == 0, r == 1, [b_Xb, b_Yb], [K.pb[bk]])
        K.tt("dve", ta, K.bank(bk, 128, 0), tmd, ALU.mult, [K.pb[bk], b_tmd], [b_ta])
        K.tt("dve", tb_, K.bank(bk, 128, 128), E.tm1, ALU.mult, [K.pb[bk], E.b_tm1], [b_tb])
        K.tt("dve", ta, ta, tb_, ALU.add, [b_ta, b_tb], [b_ta])
        K.stt(Tb[:, g, :], E.ident_f, dcol[:, g:g + 1], ta, ALU.mult, ALU.add, [E.b_idf, b_dcol, b_ta], [b_Tb])
    ck(47)
    for g0 in range(0, 32, 4):
        bk = 4 + (g0 // 4) % 4
        for gi in range(4):
            for r in range(2):
                K.tr(K.bank_bf(bk)[:, (gi * 2 + r) * 128:(gi * 2 + r + 1) * 128], Xb[:, g0 + gi, r, :], E.ident_b, [b_Xb, E.b_idb], [K.pb[bk]])
        K.evac(XTb[:, g0:g0 + 4, :, :], K.bank_bf(bk).rearrange("p (g r m) -> p g r m", g=4, r=2), [K.pb[bk]], [b_XTb])
    env["_s5"] = dict(Yb=Yb, b_Yb=b_Yb, XTb=XTb, b_XTb=b_XTb, Tb=Tb, b_Tb=b_Tb, LL=LL, b_LL=b_LL, LI=LI, b_LI=b_LI, LLk=LLk, b_LLk=b_LLk,
                      LIk=LIk, b_LIk=b_LIk, EF=EF, b_EF=b_EF, EFi=EFi, b_EFi=b_EFi, sp_re=sp_re, b_spre=b_spre, sp_im=sp_im, b_spim=b_spim,
                      h0t=h0t, b_h0=b_h0, ys5=ys5, b_ys5=b_ys5)
    env["_h"] = (hT, b_h)
    env["_mods"] = (shift1, gate1, shift2, gate2)
    env["_xs_b"] = xs_b
    for _ in range(6):
        next(mg0, None)
    ck(4)
    build_s5_run(K, P, l, env)
    ck(5)
    build_mixers(K, P, l, env)
    ck(9)
    build_merge_ffn(K, P, l, env)
    ck(12)


def build_s5_run(K, P, l, env):
    E = types.SimpleNamespace(**env)
    S5 = types.SimpleNamespace(**env["_s5"])
    A = K.A
    hT, b_h = env["_h"]
    u_blk, b_ub = A.alloc("u_blk", 48 * KB_, [128, 32, 8, 16], BF16)
    U_T, b_UT = A.alloc("U_T", 40 * KB_, [128, 32, 128], BF16)
    for blk in range(2):
        wt, wb = K.wload(E.w_in[l][:, 512 + blk * 256:512 + (blk + 1) * 256], KC, 256)
        for i in range(8):
            bk = 4 + i % 4
            for kc in range(KC):
                K.mm(K.bank(bk, 256), hT[:, kc, i::8], wt[:, kc, :], kc == 0, kc == KC - 1, [wb, b_h], [K.pb[bk]])
            K.evac(u_blk[:, blk * 16:(blk + 1) * 16, i, :], K.bank(bk, 256).rearrange("p (g c) -> p g c", g=16), [K.pb[bk]], [b_ub])
    for g0 in range(0, 32, 8):
        bk = 4 + (g0 // 8) % 4
        for gi in range(8):
            K.tr(K.bank_bf(bk)[:, gi * 128:(gi + 1) * 128], u_blk[:, g0 + gi, :, :].rearrange("p i c -> p (i c)"), E.ident_b,
                 [b_ub, E.b_idb], [K.pb[bk]])
        K.evac(U_T[:, g0:g0 + 8, :], K.bank_bf(bk).rearrange("p (g j) -> p g j", g=8), [K.pb[bk]], [b_UT])
    S, b_S = A.alloc("Sst", 88 * KB_, [128, 32, 2, 129], F32)
    b_S0, b_S1 = b_S, A.alias("S1", 88 * KB_, 33024)
    b_S1.r = dict(b_S.r)
    for g0 in range(0, 32, 4):
        for r in range(2):
            bk = 4 + ((g0 // 4) * 2 + r) % 4
            for gi in range(4):
                K.mm(K.bank(bk, 128, gi * 128), S5.XTb[:, g0 + gi, r, :], U_T[:, g0 + gi, :], True, True, [S5.b_XTb, b_UT], [K.pb[bk]])
            pv = K.bank(bk).rearrange("p (g j) -> p g j", g=4)
            K.cp("act", S[0:64, g0:g0 + 4, r, 1:129], pv[0:64], [K.pb[bk]], [b_S0])
            K.cp("dve", S[64:128, g0:g0 + 4, r, 1:129], pv[64:128][:, :, ::-1], [K.pb[bk]], [b_S1])
    i1, b_i1 = A.alloc("s5i1", 72 * KB_, [128, 32, 2], F32)
    i2, b_i2 = A.alloc("s5i2", 72 * KB_ + 256, [128, 32, 2], F32)
    h0v = S5.h0t[:, :, :].rearrange("p r g -> p g r")
    h0s = S5.h0t[:, ::-1, :].rearrange("p r g -> p g r")
    mr = S5.sp_re[:, :, 2:3].broadcast_to([128, 32, 2])
    K.tt("dve", i1, h0v, mr, ALU.mult, [S5.b_h0, S5.b_spre], [b_i1])
    K.tt("dve", i2, h0s, S5.sp_im[:, :, 2:3].broadcast_to([128, 32, 2]), ALU.mult, [S5.b_h0, S5.b_spim], [b_i2])
    K.tt("dve", S[:, :, 0, 0], i1[:, :, 0], i2[:, :, 0], ALU.subtract, [b_i1, b_i2], [b_S0, b_S1])
    K.tt("dve", S[:, :, 1, 0], i1[:, :, 1], i2[:, :, 1], ALU.add, [b_i1, b_i2], [b_S0, b_S1])
    finA, b_fA = A.alloc("finA", 72 * KB_ + 512, [128, 32, 2, 4], F32)
    b_fA1 = A.alias("finA1", 72 * KB_ + 512, 1024)
    b_fA1.r = dict(b_fA.r)
    tmp = {}
    for par in (0, 1):
        o = 72 * KB_ + 2048 + par * 768
        tmp[par] = [A.alloc(f"sA{par}", o, [128, 32, 2], F32), A.alloc(f"sT1{par}", o + 256, [128, 32, 2], F32),
                    A.alloc(f"sT2{par}", o + 512, [128, 32, 2], F32)]
    mg0 = env["_mg0"]
    bSS = [b_S0, b_S1]
    for j in range(128):
        if j % 8 == 0:
            next(mg0, None)
        (Aa, bA), (T1, bT1), (T2, bT2) = tmp[j % 2]
        cur, zc, dst = S[:, :, :, j], S[:, :, :, j + 1], S[:, :, :, j + 1]
        bnd = (j % 32 == 31)
        if bnd:
            q = j // 32
            K.tt("dve", finA[0:64, :, :, q], cur[0:64], zc[0:64], ALU.add, bSS, [b_fA])
            K.tt("dve", finA[64:128, :, :, 3 - q], cur[64:128], zc[64:128], ALU.add, bSS, [b_fA1])
            ll, bll, li, bli = S5.LLk, S5.b_LLk, S5.LIk, S5.b_LIk
            for rs, q_, bF in ((slice(0, 64), q, b_fA), (slice(64, 128), 3 - q, b_fA1)):
                Av = finA[rs, :, :, q_]
                K.tt("dve", T1[rs], Av, ll[rs], ALU.mult, [bF, bll], [bT1])
                K.tt("dve", T2[rs], Av[:, :, ::-1], li[rs], ALU.mult, [bF, bli], [bT2])
            K.tt("dve", dst, T1, T2, ALU.add, [bT1, bT2], bSS)
        else:
            K.tt("dve", Aa, cur, zc, ALU.add, bSS, [bA])
            K.tt("dve", T1, Aa, S5.LL, ALU.mult, [bA, S5.b_LL], [bT1])
            K.tt("dve", T2, Aa[:, :, ::-1], S5.LI, ALU.mult, [bA, S5.b_LI], [bT2])
            K.tt("dve", dst, T1, T2, ALU.add, [bT1, bT2], bSS)
    f1, b_f1 = A.alloc("s5f1", 76 * KB_, [128, 32, 2, 4], F32)
    f2, b_f2 = A.alloc("s5f2", 77 * KB_, [128, 32, 2, 4], F32)
    K.tt("dve", f1, finA, S5.EF[:, :, :].unsqueeze(3).broadcast_to([128, 32, 2, 4]), ALU.mult, [b_fA, b_fA1, S5.b_EF], [b_f1])
    K.tt("dve", f2, finA[:, :, ::-1, :], S5.EFi[:, :, :].unsqueeze(3).broadcast_to([128, 32, 2, 4]), ALU.mult, [b_fA, b_fA1, S5.b_EFi], [b_f2])
    K.tt("dve", f1, f1, f2, ALU.add, [b_f1, b_f2], [b_f1])
    fst, b_fst = A.alloc("s5fst", 48 * KB_, [32, 4, 2, 128], F32)
    for s_ in range(4):
        for r in range(2):
            K.tr(K.ps_t[0:32, 4 * 512 + (s_ * 2 + r) * 128:4 * 512 + (s_ * 2 + r + 1) * 128], f1[:, :, r, s_], E.ident_f, [b_f1, E.b_idf], [K.pb[4], K.pb[5]])
    K.cp("dve", fst, K.ps_t[0:32, 4 * 512:4 * 512 + 1024].rearrange("p (s r m) -> p s r m", s=4, r=2), [K.pb[4], K.pb[5]], [b_fst])
    for s_ in range(4):
        for r in range(2):
            for d_ in range(2):
                P.dma("sp", E.sout_d[s_, l, d_, r], fst[:, s_, r, d_ * 64:(d_ + 1) * 64], reads=[b_fst])
    Sb, b_Sb = A.alloc("Sb", 16 * KB_, [128, 32, 2, 128], BF16)
    K.cp("act", Sb[0:64], S[0:64, :, :, 0:128], [b_S0], [b_Sb])
    K.cp("dve", Sb[64:128], S[64:128, :, :, 127::-1], [b_S1], [b_Sb])
    yblk, b_yb = A.alloc("yblk", 72 * KB_, [128, 8, 32, 16], BF16)
    yT, b_yT = A.alloc("yT", 124 * KB_, [128, 4, NT], BF16)
    for g0 in range(0, 32, 4):
        bk = 4 + (g0 // 4) % 4
        for gi in range(4):
            g = g0 + gi
            o = K.bank(bk, 128, gi * 128)
            K.mm(o, U_T[:, g, :], S5.Tb[:, g, :], True, False, [b_UT, S5.b_Tb], [K.pb[bk]])
            K.mm(o, Sb[:, g, 0, :], S5.Yb[:, g, 0, :], False, False, [b_Sb, S5.b_Yb], [K.pb[bk]])
            K.mm(o, Sb[:, g, 1, :], S5.Yb[:, g, 1, :], False, True, [b_Sb, S5.b_Yb], [K.pb[bk]])
        s = (g0 // 4) % 2
        ga, b_ga = A.alloc(f"gla{g0}", (132 + 0) * KB_ + s * 2048, [128, 512], F32)
        pv = K.bank(bk)
        K.act(ga, pv, AF.Square, [K.pb[bk]], [b_ga])
        K.ts("dve", ga, ga, 0.044715, 1.0, ALU.mult, ALU.add, [b_ga], [b_ga])
        K.tt("dve", ga, ga, pv, ALU.mult, [b_ga, K.pb[bk]], [b_ga])
        K.act(ga, ga, AF.Sigmoid, [b_ga], [b_ga], scale=1.5957691216057308)
        K.tt("dve", yblk[:, :, g0:g0 + 4, :].rearrange("p i g c -> p g i c"), ga[:, :].rearrange("p (g i c) -> p g i c", g=4, i=8),
             pv.rearrange("p (g i c) -> p g i c", g=4, i=8), ALU.mult, [b_ga, K.pb[bk]], [b_yb])
    for i in range(8):
        bk = 4 + i % 4
        for q in range(4):
            K.tr(K.bank_bf(bk)[:, q * 128:(q + 1) * 128], yblk[:, i, q * 8:(q + 1) * 8, :].rearrange("p g c -> p (g c)"), E.ident_b,
                 [b_yb, E.b_idb], [K.pb[bk]])
        K.evac(yT[:, :, i::8], K.bank_bf(bk)[:, 0:512].rearrange("p (q j) -> p q j", q=4), [K.pb[bk]], [b_yT])
    wg, wgb = K.wload(E.w_glu[l], 4, 512)
    for m in range(4):
        pv, pbs = K.proj_fm(wg, wgb, 4, m * 128, yT, b_yT, m % 2)
        sg, b_sg = A.alloc(f"glus{m}", 132 * KB_ + (m % 2) * 2048, [128, NT], BF16)
        K.act(sg, pv, AF.Sigmoid, pbs, [b_sg])
        K.tt("dve", S5.ys5[:, m, :], yT[:, m, :], sg, ALU.mult, [b_yT, b_sg], [S5.b_ys5])


def build_mixers(K, P, l, env):
    E = types.SimpleNamespace(**env)
    A = K.A
    hT, b_h = env["_h"]
    flagc = E.flag[:, 0:1]
    yconv, b_yc = A.alloc("yconv", 72 * KB_, [128, 4, NT], BF16)
    xpad, b_xp = A.alloc("xpad", 0, [128, 4, 4, 286], BF16)
    acc = K.ar_t[:, 20 * 256:20 * 256 + 4 * NT].rearrange("p (a b) -> p a b", a=4)
    accb = [A.alloc(f"cacc{c}", 20 * KB_ + c * 4096, [128, NT], F32)[1] for c in range(4)]
    dg, b_dg = A.alloc("cdiag", 88 * KB_, [128, 4, 31, 128], BF16)
    for c in range(4):
        K.tt("dve", dg[:, c, :, :], E.ident_f[:, :].unsqueeze(1).broadcast_to([128, 31, 128]),
             E.cdw[:, c, :].unsqueeze(2).broadcast_to([128, 31, 128]), ALU.mult, [E.b_idf, E.b_cdw], [b_dg])
    K.memset("dve", xpad[:, :, 0, 0:15], 0.0, [b_xp])
    K.memset("dve", xpad[:, :, 3, 271:286], 0.0, [b_xp])
    for c in range(4):
        wa, wab = K.wload(E.w_in[l][:, 2560 + c * 128:2560 + (c + 1) * 128], KC, 128)
        wg, wgb = K.wload(E.w_in[l][:, 3072 + c * 128:3072 + (c + 1) * 128], KC, 128)
        pa, pab = K.proj_fm(wa, wab, KC, 0, hT, b_h, 0)
        pg, pgb = K.proj_fm(wg, wgb, KC, 0, hT, b_h, 1)
        sg, b_sg = A.alloc(f"csig{c}", 36 * KB_, [128, NT], F32)
        K.act(sg, pg, AF.Sigmoid, pgb, [b_sg])
        K.tt("dve", xpad[:, c, :, 15:271], pa.rearrange("p (s t) -> p s t", s=4), sg[:, :].rearrange("p (s t) -> p s t", s=4), ALU.mult,
             pab + [b_sg], [b_xp])
    K.ts("dve", xpad[:, :, 1:4, 0:15], xpad[:, :, 0:3, 256:271], flagc, None, ALU.mult, None, [b_xp, E.b_flag], [b_xp])
    K.ts("dve", xpad[:, :, 0:3, 271:286], xpad[:, :, 1:4, 15:30], flagc, None, ALU.mult, None, [b_xp, E.b_flag], [b_xp])
    for c in range(4):
        for half in range(2):
            bk = 4 + (c * 2 + half) % 4
            for k in range(31):
                K.mm(K.bank(bk), dg[:, c, k, :], xpad[:, c, half * 2:half * 2 + 2, k:k + 256], k == 0, k == 30, [b_dg, b_xp], [K.pb[bk]])
            K.act(acc[:, c, half * 512:(half + 1) * 512], K.bank(bk), AF.Identity, [K.pb[bk], E.b_cdb], [accb[c]], bias=E.cdb[:, c:c + 1])
    xs = [(acc[:, c, :], accb[c]) for c in range(4)]
    R, Rb, MR, MRb = K.ln_stats(xs, 512, 40 * KB_)
    K.ln_apply(xs, [(yconv[:, c, :], b_yc) for c in range(4)], R, Rb, MR, MRb, E.clg, E.clb, [E.b_clg, E.b_clb], func=AF.Silu)

    ck(6)
    ypool, b_yp = A.alloc("ypool", 64 * KB_, [128, 4, NT], BF16)
    apad, b_ap = A.alloc("apad", 0, [128, 4, 4, 272], F32)
    pooled, b_pl = A.alloc("pooled", 31 * KB_, [128, 4, NT], BF16)
    K.memset("dve", apad[:, :, 0, 0:8], 0.0, [b_ap])
    K.memset("dve", apad[:, :, 3, 264:272], 0.0, [b_ap])
    for half in range(2):
        wt, wb = K.wload(E.w_in[l][:, half * 256:(half + 1) * 256], KC, 256)
        for m in range(2):
            g = half * 2 + m
            pv, pbs = K.proj_fm(wt, wb, KC, m * 128, hT, b_h, m)
            K.evac(apad[:, g, :, 8:264], pv.rearrange("p (s t) -> p s t", s=4), pbs, [b_ap])
    K.ts("dve", apad[:, :, 1:4, 0:8], apad[:, :, 0:3, 256:264], flagc, None, ALU.mult, None, [b_ap, E.b_flag], [b_ap])
    K.ts("dve", apad[:, :, 0:3, 264:272], apad[:, :, 1:4, 8:16], flagc, None, ALU.mult, None, [b_ap, E.b_flag], [b_ap])
    ck(61)
    for g, w in enumerate((2, 4, 8, 16)):
        left = w // 2
        t1, b1 = A.alloc(f"pt1{g}", 17 * KB_, [128, 4, 272], F32)
        t2, b2 = A.alloc(f"pt2{g}", 22 * KB_, [128, 4, 272], F32)
        rc, brc = A.alloc(f"prc{g}", 27 * KB_, [128, NT], F32)
        P.dma("sp", rc, E.pool_rc_d[:, g, :], writes=[brc])
        src, bsrc = apad[:, g, :, :], b_ap
        dst, bdst = t1, b1
        n = 272
        step = 1
        while step < w:
            n2 = n - step
            K.tt("dve", dst[:, :, 0:n2], src[:, :, 0:n2], src[:, :, step:step + n2], ALU.add, [bsrc], [bdst])
            src, bsrc = dst, bdst
            dst, bdst = (t2, b2) if dst is t1 else (t1, b1)
            n = n2
            step *= 2
        o = 8 - left
        K.tt("dve", dst[:, :, 0:256], src[:, :, o:o + 256], rc[:, :].rearrange("p (s t) -> p s t", s=4), ALU.mult, [bsrc, brc], [bdst])
        K.tt("dve", pooled[:, g, :].rearrange("p (s t) -> p s t", s=4), dst[:, :, 0:256], apad[:, g, :, 8:264], ALU.subtract, [bdst, b_ap], [b_pl])
    ck(62)
    pwt, pwb = A.alloc("poolw", 40 * KB_, [128, 4, 128], BF16)
    for g in range(4):
        P.dma("pool", pwt[:, g, :], E.pool_w[l, g], writes=[pwb])
    for g in range(4):
        for tt in range(2):
            K.mm(K.bank(g % 2 * 2 + tt), pwt[:, g, :], pooled[:, g, tt * 512:(tt + 1) * 512], True, True, [pwb, b_pl], [K.pb[g % 2 * 2 + tt]])
        b0 = g % 2 * 2
        K.act(ypool[:, g, :], K.ps_t[:, b0 * 512:(b0 + 2) * 512], AF.Identity, [K.pb[b0], K.pb[b0 + 1], E.b_psc], [b_yp], scale=E.psc[:, g:g + 1])

    ck(7)
    attnT, b_at = A.alloc("attnT", 88 * KB_, [128, 8, NT], BF16)
    qT, b_q = A.alloc("qT", 0, [128, 8, NT], BF16)
    kT, b_k = A.alloc("kT", 16 * KB_, [128, 2, NT], BF16)
    ckT, b_ck = A.alloc("ckT", 20 * KB_, [128, 2, 512], BF16)
    v_sb, b_v = A.alloc("v_sb", 22 * KB_, [128, 8, 256], BF16)
    cv_sb, b_cv = A.alloc("cv_sb", 26 * KB_, [128, 4, 256], BF16)
    ck_tok, b_ckt = A.alloc("ck_tok", 28 * KB_, [128, 4, 256], BF16)
    masks, b_mk = A.alloc("masks", 30 * KB_, [128, 8, 7 * 128], BF16)
    cosT, b_cos = A.alloc("cosT", 44 * KB_, [128, NT], F32)
    sinT, b_sin = A.alloc("sinT", 48 * KB_, [128, NT], F32)
    P.dma("pool", masks, E.masks_d.rearrange("p (a b) -> p a b", a=8), writes=[b_mk])
    ti, b_ti = A.alloc("rp_ti", 110 * KB_, [128, NT], I32)
    ri, b_ri = A.alloc("rp_ri", 114 * KB_, [128, NT], I32)
    pos, b_pos = A.alloc("rp_pos", 118 * KB_, [128, NT], F32)
    ni, b_ni = A.alloc("rp_ni", 122 * KB_, [128, NT], I32)
    nf, b_nf = A.alloc("rp_nf", 126 * KB_, [128, NT], F32)
    mm_, b_mm = A.alloc("rp_mm", 130 * KB_, [128, NT], F32)
    pi_, b_pi = A.alloc("rp_pi", 134 * KB_, [128, 1], I32)
    inv, b_inv = A.alloc("rp_inv", 134 * KB_ + 4, [128, 1], F32)
    K.P.op("pool", lambda e: e.iota(ti, pattern=[[1, NT]], base=0, channel_multiplier=0), (), [b_ti])
    K.P.op("pool", lambda e: e.iota(pi_, pattern=[[0, 1]], base=0, channel_multiplier=1), (), [b_pi])
    K.ts("dve", ri[0:64], ti[0:64], 6, None, ALU.arith_shift_right, None, [b_ti], [b_ri])
    K.ts("dve", ri[64:128], ti[64:128], 63, None, ALU.bitwise_and, None, [b_ti], [b_ri])
    K.cp("dve", pos, ri, [b_ri], [b_pos])
    K.ts("dve", pi_, pi_, 31, None, ALU.bitwise_and, None, [b_pi], [b_pi])
    K.cp("dve", inv, pi_, [b_pi], [b_inv])
    K.act(inv, inv, AF.Exp, [b_inv], [b_inv], scale=-math.log(10000.0) / 32.0)
    K.ts("dve", inv, inv, flagc, None, ALU.mult, None, [b_inv, E.b_flag], [b_inv])
    K.ts("dve", pos, pos, inv[:, 0:1], None, ALU.mult, None, [b_pos, b_inv], [b_pos])
    for (dst, bd, off) in ((sinT, b_sin, 32.0), (cosT, b_cos, 32.25)):
        K.ts("dve", dst, pos, 1.0 / (2.0 * math.pi), off, ALU.mult, ALU.add, [b_pos], [bd])
        K.cp("dve", ni, dst, [bd], [b_ni])
        K.cp("dve", nf, ni, [b_ni], [b_nf])
        K.tt("dve", dst, dst, nf, ALU.subtract, [bd, b_nf], [bd])
        K.ts("dve", mm_, dst, 0.5, None, ALU.is_gt, None, [bd], [b_mm])
        K.tt("dve", dst, dst, mm_, ALU.subtract, [bd, b_mm], [bd])
        K.ts("dve", mm_, dst, -0.5, None, ALU.is_lt, None, [bd], [b_mm])
        K.tt("dve", dst, dst, mm_, ALU.add, [bd, b_mm], [bd])
        K.act(dst, dst, AF.Sin, [bd], [bd], scale=2.0 * math.pi)
    for b_ in range(4):
        P.dma("pool", cv_sb[:, b_, :], E.cv_d[l, b_ * 128:(b_ + 1) * 128, :], writes=[b_cv])
        P.dma("pool", ck_tok[:, b_, :], E.ck_d[l, b_ * 128:(b_ + 1) * 128, :], writes=[b_ckt])
    for kvh in range(2):
        for b_ in range(4):
            K.tr(K.bank_bf(4 + kvh)[:, b_ * 128:(b_ + 1) * 128], ck_tok[:, b_, kvh * 128:(kvh + 1) * 128], E.ident_b, [b_ckt, E.b_idb], [K.pb[4 + kvh]])
        K.evac(ckT[:, kvh, :], K.bank_bf(4 + kvh)[:, 0:512], [K.pb[4 + kvh]], [b_ck])

    ck(71)

    def rope(pv, pbs, dst, bdst, idx):
        qr, bqr = A.alloc(f"qraw{idx}", 52 * KB_ + (idx % 2) * 2048, [128, NT], BF16)
        t1, bt1 = A.alloc(f"ropt{idx}", 56 * KB_, [128, NT], F32)
        K.cp("act", qr, pv, pbs, [bqr])
        K.tt("dve", t1, pv, cosT, ALU.mult, pbs + [b_cos], [bt1])
        for tt in range(2):
            K.mm(K.bank(4 + tt), E.rotP_b, qr[:, tt * 512:(tt + 1) * 512], True, True, [E.b_rot, bqr], [K.pb[4 + tt]])
        t2, bt2 = A.alloc(f"ropu{idx}", 60 * KB_, [128, NT], F32)
        K.tt("dve", t2, K.ps_t[:, 4 * 512:6 * 512], sinT, ALU.mult, [K.pb[4], K.pb[5], b_sin], [bt2])
        K.tt("dve", dst, t1, t2, ALU.add, [bt1, bt2], [bdst])

    idx = 0
    for blk in range(4):
        wt, wb = K.wload(E.w_in[l][:, 1024 + blk * 256:1024 + (blk + 1) * 256], KC, 256)
        for m in range(2):
            pv, pbs = K.proj_fm(wt, wb, KC, m * 128, hT, b_h, m)
            ck(711)
            rope(pv, pbs, qT[:, blk * 2 + m, :], b_q, idx)
            ck(712)
            idx += 1
    ck(72)
    wk, wkb = K.wload(E.w_in[l][:, 2048:2304], KC, 256)
    for m in range(2):
        pv, pbs = K.proj_fm(wk, wkb, KC, m * 128, hT, b_h, m)
        rope(pv, pbs, kT[:, m, :], b_k, idx)
        idx += 1
    ck(73)
    wv, wvb = K.wload(E.w_in[l][:, 2304:2560], KC, 256)
    for tt in range(8):
        for which, (wt, wb, od) in enumerate(((wk, wkb, E.kout_d), (wv, wvb, E.vout_d))):
            bk = (tt * 2 + which) % 4
            for kc in range(KC):
                K.mm(K.bank(bk, 256), hT[:, kc, tt * 128:(tt + 1) * 128], wt[:, kc, :], kc == 0, kc == KC - 1, [wb, b_h], [K.pb[bk]])
            stg, bst = A.alloc(f"kvst{tt}{which}", 104 * KB_ + ((tt * 2 + which) % 4) * 1024, [128, 256], F32)
            K.cp("act", stg, K.bank(bk, 256), [K.pb[bk]], [bst])
            P.dma("sp", od[l, tt * 128:(tt + 1) * 128, :], stg, reads=[bst])
            if which == 1:
                K.cp("dve", v_sb[:, tt, :], K.bank(bk, 256), [K.pb[bk]], [b_v])
    ck(74)
    scale = 128.0 ** -0.5
    its = [(h, i) for h in range(8) for i in range(8)]
    st_ = {}

    def att_front(n):
        h, i = its[n]
        kvh = h // 4
        kbs = [min(max(i + rel, 0), 7) for rel in (-1, 0, 1)]
        b0 = 4 + 2 * (n % 2)
        for n_, kb in enumerate(kbs):
            K.mm(K.ps_t[:, b0 * 512 + n_ * 128:b0 * 512 + (n_ + 1) * 128], kT[:, kvh, kb * 128:(kb + 1) * 128], qT[:, h, i * 128:(i + 1) * 128], True, True,
                 [b_k, b_q], [K.pb[b0], K.pb[b0 + 1]])
        for n_ in range(4):
            K.mm(K.ps_t[:, b0 * 512 + (3 + n_) * 128:b0 * 512 + (4 + n_) * 128], ckT[:, kvh, n_ * 128:(n_ + 1) * 128], qT[:, h, i * 128:(i + 1) * 128], True, True,
                 [b_ck, b_q], [K.pb[b0], K.pb[b0 + 1]])
        Pt, bPt = A.alloc(f"Pt{n}", 64 * KB_ - 4096 + (n % 2) * 2048, [128, 7 * 128], BF16)
        K.act(Pt, K.ps_t[:, b0 * 512:b0 * 512 + 896], AF.Exp, [K.pb[b0], K.pb[b0 + 1]], [bPt], scale=scale)
        K.tt("dve", Pt, Pt, masks[:, i, :], ALU.mult, [bPt, b_mk], [bPt])
        st_[n] = (Pt, bPt, kbs, b0)

    def att_back(n):
        h, i = its[n]
        kvh = h // 4
        Pt, bPt, kbs, b0 = st_.pop(n)
        for n_ in range(7):
            if n_ < 3:
                vv, vb = v_sb[:, kbs[n_], kvh * 128:(kvh + 1) * 128], b_v
            else:
                vv, vb = cv_sb[:, n_ - 3, kvh * 128:(kvh + 1) * 128], b_cv
            K.mm(K.ps_t[:, b0 * 512:b0 * 512 + 128], vv, Pt[:, n_ * 128:(n_ + 1) * 128], n_ == 0, n_ == 6, [vb, bPt], [K.pb[b0], K.pb[b0 + 1]])
        for n_ in range(7):
            K.mm(K.ps_t[:, b0 * 512 + 128:b0 * 512 + 256], E.ones_b, Pt[:, n_ * 128:(n_ + 1) * 128], n_ == 0, n_ == 6, [E.b_ones, bPt], [K.pb[b0], K.pb[b0 + 1]])
        rd, brd = A.alloc(f"rd{n}", 108 * KB_ + (n % 2) * 512, [128, 128], F32)
        K.act(rd, K.ps_t[:, b0 * 512 + 128:b0 * 512 + 256], AF.Ln, [K.pb[b0], K.pb[b0 + 1], E.b_sink], [brd], bias=E.sinkexp[:, h:h + 1])
        K.act(rd, rd, AF.Exp, [brd], [brd], scale=-1.0)
        K.tt("dve", attnT[:, h, i * 128:(i + 1) * 128], K.ps_t[:, b0 * 512:b0 * 512 + 128], rd, ALU.mult, [K.pb[b0], K.pb[b0 + 1], brd], [b_at])

    att_front(0)
    for n in range(len(its)):
        if n + 1 < len(its):
            att_front(n + 1)
        att_back(n)
    env["_y"] = dict(ypool=(ypool, b_yp), yconv=(yconv, b_yc), attnT=(attnT, b_at))


def build_merge_ffn(K, P, l, env):
    E = types.SimpleNamespace(**env)
    A, H = K.A, K.H
    hT, b_h = env["_h"]
    shift1, gate1, shift2, gate2 = env["_mods"]
    xs_b = env["_xs_b"]
    Y = env["_y"]
    S5 = env["_s5"]
    branches = [(Y["ypool"], E.w_br_pool, 4, 0), ((S5["ys5"], S5["b_ys5"]), E.w_br_s5, 4, 1), (Y["attnT"], E.w_br_attn, 8, 2), (Y["yconv"], E.w_br_conv, 4, 3)]
    for _ in env["_mg0"]:
        pass
    K.tt("dve", E.mod[:, 32:96], E.modraw[:, 32:96], E.bada[:, 32:96], ALU.add, [E.b_modraw, E.b_bada], [E.b_mod])
    K.ts("dve", E.s2p, E.mod[:, 64:80], 1.0, None, ALU.add, None, [E.b_mod], [E.b_s2p])
    merged, b_mg = A.alloc("merged", 104 * KB_, [128, KC, NT], BF16)
    for dc in range(KC):
        macc, b_ma = A.alloc(f"macc{dc}", 16 * KB_ + (dc % 2) * 4096, [128, NT], F32)
        for bi, ((yt, yb), wbr, nk, gi) in enumerate(branches):
            wg, wgb = K.wload(E.w_in[l][:, MIXC + gi * D + dc * 128:MIXC + gi * D + (dc + 1) * 128], KC, 128)
            pg, pgb = K.proj_fm(wg, wgb, KC, 0, hT, b_h, bi % 2)
            gt, b_gt = A.alloc(f"gt{dc}_{bi}", (bi % 2) * 2048, [128, NT], BF16)
            K.act(gt, pg, AF.Sigmoid, pgb, [b_gt])
            wb_, wbb = K.wload(wbr[l][:, dc * 128:(dc + 1) * 128], nk, 128)
            pb_, pbb = K.proj_fm(wb_, wbb, nk, 0, yt, yb, 2 + bi % 2)
            if bi == 0:
                K.tt("dve", macc, pb_, gt, ALU.mult, pbb + [b_gt], [b_ma])
            else:
                tm, b_tm = A.alloc(f"mtmp{dc}_{bi}", 8 * KB_ + (bi % 2) * 4096, [128, NT], F32)
                K.tt("dve", tm, pb_, gt, ALU.mult, pbb + [b_gt], [b_tm])
                K.tt("dve", macc, macc, tm, ALU.add, [b_ma, b_tm], [b_ma])
        K.cp("act", merged[:, dc, :], macc, [b_ma], [b_mg])

    def residual_ln(wmat, nk_total, rhs, rhs_b, gate, lng, b_lng, lnb, b_lnb, zlocs, tag, xrloc, lnso=None, next_so=None, lnar=None):
        zs = []
        pend = [None]
        for dc in range(KC):
            ar, off = zlocs[dc]
            za, zb = ar.alloc(f"z{tag}{dc}", off, [128, NT], F32)
            halves = [(0, nk_total)] if nk_total <= 16 else [(0, nk_total // 2), (nk_total // 2, nk_total // 2)]
            for (k0, nk) in halves:
                wt, wb = K.wload(wmat[l][k0 * 128:(k0 + nk) * 128, dc * 128:(dc + 1) * 128], nk, 128)
                pv, pbs = K.proj_fm(wt, wb, nk, 0, rhs, rhs_b, dc % 2, kofs=k0, ktot=nk_total)
            if pend[0] is not None:
                pend[0]()
                pend[0] = None
            K.act(za, pv, AF.Identity, pbs + [E.b_mod], [zb], scale=gate[:, dc:dc + 1], bias=E.zero1[:, 0:1])
            xr, b_xr = xrloc[0].alloc(f"xr{tag}{dc}", xrloc[1] + (dc % 2) * 4096, [128, NT], F32)
            P.dma("sp", xr, E.xs_d[:, dc * NT:(dc + 1) * NT], reads=[xs_b[dc]], writes=[b_xr])
            K.stt(za, xr, ALPHA, za, ALU.mult, ALU.add, [b_xr, zb], [zb])
            zs.append((za, zb))
            if lnso is not None:
                pend[0] = (lambda dc=dc, za=za, zb=zb: K.ln_stats_chunk(dc, KC, za, zb, lnso, lnar))
        if pend[0] is not None:
            pend[0]()
        if lnso is not None:
            R, Rb, MR, MRb = K.ln_stats_finish(D, 64 * KB_)
        else:
            R, Rb, MR, MRb = K.ln_stats(zs, D, 64 * KB_)
        aft = (lambda c, oa, ob: K.ln_stats_chunk(c, KC, oa, ob, next_so)) if next_so is not None else None
        K.ln_apply(zs, zs, R, Rb, MR, MRb, lng, lnb, [b_lng, b_lnb], after=aft)
        return zs

    ck(10)
    zs = residual_ln(E.w_out, KC, merged, b_mg, gate1, E.ln1g, E.b_ln1g, E.ln1b, E.b_ln1b, [(A, dc * 4096) for dc in range(KC)], f"a{l}", (A, 88 * KB_), lnso=64 * KB_, next_so=64 * KB_)
    for dc in range(KC):
        E.xch_b[dc] = zs[dc][1]
    ck(11)
    for c in range(KC):
        P.dma("sp", E.xs_d[:, c * NT:(c + 1) * NT], E.xT[:, c, :], reads=[E.xch_b[c]], writes=[xs_b[c]])
    R, Rb, MR, MRb = K.ln_stats_finish(D, 64 * KB_)
    h2, b_h2 = H.alloc(f"h2{l}", 0, [128, KC, NT], BF16)
    K.ln_apply(E.xchunks(), [(h2[:, c, :], b_h2) for c in range(KC)], R, Rb, MR, MRb, E.s2p, shift2, [E.b_s2p, E.b_mod])
    fdwf, b_fdwf = A.alloc("fdwf", 24 * KB_, [128, 2, NFF], F32)
    K.ts("dve", fdwf[:, 0, :], E.fdw[:, 0, :], E.flag[:, 0:1], None, ALU.mult, None, [E.b_fdw, E.b_flag], [b_fdwf])
    K.ts("dve", fdwf[:, 1, :], E.fdw[:, 2, :], E.flag[:, 0:1], None, ALU.mult, None, [E.b_fdw, E.b_flag], [b_fdwf])
    actT, b_actT = A.alloc("actT", 48 * KB_, [128, 44, NT], BF16)
    actb = [A.alias(f"actb{i}", 48 * KB_ + i * 2048, 2048) for i in range(44)]
    for b_ in actb:
        b_.r = dict(b_actT.r)
    mg_ = mod_gen(K, E, l + 1, 7) if l + 1 < NL else iter(())
    for p_ in range(44):
        next(mg_, None)
        if p_ < 4:
            next(mg_, None)
        us = []
        for which in range(2):
            ch = which * 44 + p_
            wt, wb = K.wload(E.w_up[l][:, ch * 128:(ch + 1) * 128], KC, 128)
            pv, pbs = K.proj_fm(wt, wb, KC, 0, h2, b_h2, which)
            u, b_u = A.alloc(f"u{p_}_{which}", ((p_ % 2) * 2 + which) * 4096, [128, NT], F32)
            K.act(u, pv, AF.Identity, pbs + [E.b_fdw, E.b_fdb], [b_u], scale=E.fdw[:, 1, ch:ch + 1], bias=E.fdb[:, ch:ch + 1])
            uv = u[:, :].rearrange("p (s t) -> p s t", s=4)
            p3 = pv.rearrange("p (s t) -> p s t", s=4)
            K.stt(uv[:, :, 1:256], p3[:, :, 0:255], E.fdw[:, 0, ch:ch + 1], uv[:, :, 1:256], ALU.mult, ALU.add, pbs + [E.b_fdw, b_u], [b_u])
            K.stt(uv[:, :, 0:255], p3[:, :, 1:256], E.fdw[:, 2, ch:ch + 1], uv[:, :, 0:255], ALU.mult, ALU.add, pbs + [E.b_fdw, b_u], [b_u])
            K.stt(uv[:, 1:4, 0], p3[:, 0:3, 255], fdwf[:, 0, ch:ch + 1], uv[:, 1:4, 0], ALU.mult, ALU.add, pbs + [b_fdwf, b_u], [b_u])
            K.stt(uv[:, 0:3, 255], p3[:, 1:4, 0], fdwf[:, 1, ch:ch + 1], uv[:, 0:3, 255], ALU.mult, ALU.add, pbs + [b_fdwf, b_u], [b_u])
            us.append((u, b_u))
        sg, b_sg = A.alloc(f"fsg{p_}", 16 * KB_ + (p_ % 2) * 4096, [128, NT], F32)
        K.act(sg, us[0][0], AF.Silu, [us[0][1]], [b_sg])
        K.tt("dve", actT[:, p_, :], sg, us[1][0], ALU.mult, [b_sg, us[1][1]], [actb[p_]])
    for _ in mg_:
        pass
    for b_ in actb:
        for k, v in b_.w.items():
            if b_actT.w.get(k, 0) < v:
                b_actT.w[k] = v
    zl = [(A, dc * 4096) if dc < 12 else (H, (dc - 12) * 4096) for dc in range(KC)]
    zs = residual_ln(E.w_down, 44, actT, b_actT, gate2, E.ln2g, E.b_ln2g, E.ln2b, E.b_ln2b, zl, f"f{l}", (H, 16 * KB_), lnso=24 * KB_, lnar=H, next_so=(64 * KB_ if l + 1 < NL else None))
    K.ln_pre = (l + 1 < NL)
    for dc in range(12):
        E.xch_b[dc] = zs[dc][1]
    for dc in range(12, KC):
        na, nb = A.alloc(f"xmv{l}{dc}", dc * 4096, [128, NT], F32)
        K.cp("dve" if dc % 2 else "act", na, zs[dc][0], [zs[dc][1]], [nb])
        E.xch_b[dc] = nb


_NC_CACHE = {}


def _consts():
    cst = np.zeros((128, 528), np.float32)
    cst[:, 0:128] = np.eye(128, dtype=np.float32)
    rot = np.zeros((128, 128), np.float32)
    for m in range(128):
        if (m % 64) < 32:
            rot[m + 32, m] = -1.0
        else:
            rot[m - 32, m] = 1.0
    cst[:, 128:256] = rot
    r = np.arange(128)[:, None] // 16
    q = np.arange(128)[None, :] // 16
    cst[:, 256:384] = (q >= r)
    cst[:, 384:512] = (q <= r)
    cst[:, 512:520] = np.arange(8, dtype=np.float32)[None, :]
    cst[:64, 520] = 1.0
    cst[64:, 520] = -1.0
    return cst


def _core_consts(sample):
    n = 1024 if sample else 256
    rc = np.zeros((4, NT), np.float32)
    t = np.arange(NT) % n
    for g, w in enumerate((2, 4, 8, 16)):
        left = w // 2
        right = w - 1 - left
        lo = np.clip(t - left, 0, n)
        hi = np.clip(t + right + 1, 0, n)
        rc[g] = 1.0 / (hi - lo).astype(np.float32)
    rc = np.ascontiguousarray(np.broadcast_to(rc[None], (128, 4, NT)))
    mk = np.zeros((128, 8, 7, 128), np.float32)
    b = np.arange(128)[:, None]
    a = np.arange(128)[None, :]
    for i in range(8):
        if sample:
            if i >= 1:
                mk[:, i, 0, :] = (b >= a)
            mk[:, i, 1, :] = 1.0
            if i <= 6:
                mk[:, i, 2, :] = (b <= a)
            mk[:, i, 3:7, :] = 1.0
        else:
            mk[:, i, 1, :] = 1.0
            if i % 2 == 0:
                mk[:, i, 2, :] = 1.0
            else:
                mk[:, i, 0, :] = 1.0
    mk = mk.reshape(128, 8 * 7 * 128)
    if sample:
        tt = np.arange(NT)
        row, col = tt // 64, tt % 64
        inv = (np.float32(10000.0) ** (-np.arange(32, dtype=np.float32) / np.float32(32))).astype(np.float32)
        cos = np.zeros((128, NT), np.float32)
        sin = np.zeros((128, NT), np.float32)
        for m in range(128):
            pos = (row if m < 64 else col).astype(np.float32)
            ang = (pos * inv[m % 32]).astype(np.float32)
            cos[m] = np.cos(ang)
            sin[m] = np.sin(ang)
    else:
        cos = np.ones((128, NT), np.float32)
        sin = np.zeros((128, NT), np.float32)
    return rc, mk, cos, sin


def kernel(x_prompt, x_sample, cache_k, cache_v, state_s5, c, c_ctx, w_ada, b_ada, w_in,
           pool_w, pool_scale, s5_lambda_re, s5_lambda_im, s5_log_dt, s5_b_re, s5_b_im,
           s5_c_re, s5_c_im, s5_d, s5_w_glu, attn_sink, conv_dw, conv_db, conv_ln_g, conv_ln_b,
           w_br_pool, w_br_s5, w_br_attn, w_br_conv, w_out, ln1_g, ln1_b, ffn_w_up, ffn_dw,
           ffn_db, ffn_w_down, ln2_g, ln2_b):
    f = lambda a: np.ascontiguousarray(np.asarray(a, dtype=np.float32))
    L = NL
    fm = lambda v, n: f(np.asarray(v).reshape(L, n, 128).transpose(0, 2, 1))
    shared = {
        "cst": _consts(),
        "w_ada": f(w_ada), "b_ada_fm": fm(b_ada, 96), "w_in": f(w_in), "pool_w": f(pool_w), "pool_scale_fm": fm(pool_scale, 4),
        "lam_re_dp": f(np.asarray(s5_lambda_re).transpose(0, 1, 3, 2).reshape(L, 128, 32)),
        "lam_im_dp": f(np.asarray(s5_lambda_im).transpose(0, 1, 3, 2).reshape(L, 128, 32)),
        "logdt_dp": f(np.broadcast_to(np.asarray(s5_log_dt)[:, :, None, :], (L, 2, 64, 32)).reshape(L, 128, 32)),
        "b_re_dp": f(np.asarray(s5_b_re).transpose(0, 1, 3, 2, 4).reshape(L, 128, 32, 16)),
        "b_im_dp": f(np.asarray(s5_b_im).transpose(0, 1, 3, 2, 4).reshape(L, 128, 32, 16)),
        "c_re_dp": f(np.asarray(s5_c_re).transpose(0, 1, 4, 2, 3).reshape(L, 128, 32, 16)),
        "c_im_dp": f(np.asarray(s5_c_im).transpose(0, 1, 4, 2, 3).reshape(L, 128, 32, 16)),
        "s5_dcol": f(np.tile(np.asarray(s5_d).reshape(L, 32, 16).transpose(0, 2, 1), (1, 8, 1))),
        "s5_w_glu": f(s5_w_glu),
        "attn_sink_bc": f(np.broadcast_to(np.asarray(attn_sink)[:, None, :], (L, 128, 8))),
        "conv_dw_fm": f(np.asarray(conv_dw).reshape(L, 31, 4, 128).transpose(0, 3, 2, 1)),
        "conv_db_fm": fm(conv_db, 4), "conv_ln_g_fm": fm(conv_ln_g, 4), "conv_ln_b_fm": fm(conv_ln_b, 4),
        "w_br_pool": f(w_br_pool), "w_br_s5": f(w_br_s5), "w_br_attn": f(w_br_attn), "w_br_conv": f(w_br_conv), "w_out": f(w_out),
        "ln1_g_fm": fm(ln1_g, 16), "ln1_b_fm": fm(ln1_b, 16), "ln2_g_fm": fm(ln2_g, 16), "ln2_b_fm": fm(ln2_b, 16),
        "ffn_w_up": f(ffn_w_up), "ffn_dw_fm": f(np.asarray(ffn_dw).reshape(L, 3, NFF, 128).transpose(0, 3, 1, 2)),
        "ffn_db_fm": fm(ffn_db, NFF), "ffn_w_down": f(ffn_w_down),
    }
    pc = _core_consts(False)
    sc = _core_consts(True)
    xp = np.asarray(x_prompt, np.float32)
    xsm = np.asarray(x_sample, np.float32)
    in_maps = []
    for core in range(8):
        sample = core >= 4
        b = core - 4
        rc, mk, cos, sin = sc if sample else pc
        m = dict(shared)
        m["x"] = f(xsm[b]) if sample else f(xp[4 * core:4 * core + 4].reshape(NT, D))
        cv_ = np.asarray(c)[b] if sample else np.asarray(c_ctx)
        m["cvec"] = f(cv_.reshape(16, 128).T)
        m["flag"] = np.full((128, 1), 1.0 if sample else 0.0, np.float32)
        m["pool_rc"] = rc
        m["masks"] = mk
        if sample:
            m["cache_k_c"] = f(np.asarray(cache_k)[b].reshape(L, 512, 256))
            m["cache_v_c"] = f(np.asarray(cache_v)[b].reshape(L, 512, 256))
            m["h0_dp"] = f(np.asarray(state_s5)[b].transpose(0, 1, 4, 2, 3).reshape(L, 128, 2, 32))
        else:
            m["cache_k_c"] = np.zeros((L, 512, 256), np.float32)
            m["cache_v_c"] = np.zeros((L, 512, 256), np.float32)
            m["h0_dp"] = np.zeros((L, 128, 2, 32), np.float32)
        in_maps.append(m)
    if "nc" not in _NC_CACHE:
        _NC_CACHE["nc"] = build()
    ncores = int(os.environ.get("KCORES", "8"))
    res = run_bass_kernel_spmd(_NC_CACHE["nc"], in_maps[:ncores], core_ids=list(range(ncores)))
    r = list(res.results) + [res.results[0]] * (8 - ncores)
    y_prompt = np.concatenate([r[i]["y"].reshape(4, 256, D) for i in range(4)], axis=0).astype(np.float32)
    y_sample = np.stack([r[4 + i]["y"] for i in range(4)], axis=0).astype(np.float32)
    nk = np.concatenate([r[i]["kout"].reshape(L, 4, 256, 2, 128).transpose(1, 0, 2, 3, 4) for i in range(4)], axis=0).astype(np.float32)
    nv = np.concatenate([r[i]["vout"].reshape(L, 4, 256, 2, 128).transpose(1, 0, 2, 3, 4) for i in range(4)], axis=0).astype(np.float32)
    ns = np.concatenate([r[i]["sout"] for i in range(4)], axis=0).astype(np.float32)
    return (y_prompt, y_sample, np.ascontiguousarray(nk), np.ascontiguousarray(nv), ns)
```
